# Optimizing a Trainium2 kernel written in Bass

```python
import math
import jax, jax.numpy as jnp
from jax import lax
import numpy as np

D_MODEL = 4096
BATCH = 32
SEQ = 256
DEPTH = 4
DEC_BATCH = 2
DEC_SEQ = 4096
PAST_LEN = 512

GRID_W = 64
HEAD_DIM = 128
ATTN_WIDTH = D_MODEL // 2
N_HEADS = ATTN_WIDTH // HEAD_DIM
N_KV_HEADS = N_HEADS // 4
KV_WIDTH = N_KV_HEADS * HEAD_DIM
SSM_WIDTH = D_MODEL // 4
SSM_GROUP = 16
N_SSM_GROUPS = SSM_WIDTH // SSM_GROUP
SSM_STATE = 64
FFT_WIDTH = D_MODEL // 4
N_FFT_GROUPS = 4
FFT_GROUP = FFT_WIDTH // N_FFT_GROUPS
MIX_WIDTH = ATTN_WIDTH + SSM_WIDTH + FFT_WIDTH
IN_SIZES = (ATTN_WIDTH, KV_WIDTH, KV_WIDTH, ATTN_WIDTH, SSM_WIDTH, SSM_WIDTH, FFT_WIDTH, FFT_WIDTH)
IN_WIDTH = 2 * ATTN_WIDTH + 2 * KV_WIDTH + 2 * SSM_WIDTH + 2 * FFT_WIDTH
Q_BLOCK = 128
ROPE_THETA = 10000.0
ROPE_PAIRS = HEAD_DIM // 4
NORM_EPS = 1e-6

kernel_name = 'hymba_style_flow_backbone_step'


def rms_norm(x, g):
    xf = x.astype(jnp.float32)
    y = xf * lax.rsqrt(jnp.mean(xf * xf, axis=-1, keepdims=True) + NORM_EPS)
    return (y * g.astype(jnp.float32)).astype(x.dtype)


def grid_angles(n_tokens):
    rows = n_tokens // GRID_W
    row = jnp.repeat(jnp.arange(rows, dtype=jnp.float32), GRID_W)
    col = jnp.tile(jnp.arange(GRID_W, dtype=jnp.float32), rows)
    inv = ROPE_THETA ** (-jnp.arange(ROPE_PAIRS, dtype=jnp.float32) / ROPE_PAIRS)
    return row[:, None] * inv[None, :], col[:, None] * inv[None, :]


def rope_1d(x, ang):
    r = ang.shape[-1]
    x1, x2 = x[..., :r], x[..., r:]
    cos = jnp.cos(ang)[None, :, None, :]
    sin = jnp.sin(ang)[None, :, None, :]
    return jnp.concatenate([x1 * cos - x2 * sin, x1 * sin + x2 * cos], axis=-1)


def axial_rope(x, row_ang, col_ang):
    half = HEAD_DIM // 2
    xf = x.astype(jnp.float32)
    out = jnp.concatenate([rope_1d(xf[..., :half], row_ang), rope_1d(xf[..., half:], col_ang)], axis=-1)
    return out.astype(x.dtype)


def blocked_attention(q, k, v):
    bsz, lq = q.shape[0], q.shape[1]
    nb = lq // Q_BLOCK
    grp = N_HEADS // N_KV_HEADS
    qb = q.reshape(bsz, nb, Q_BLOCK, N_KV_HEADS, grp, HEAD_DIM).transpose(1, 0, 2, 3, 4, 5)
    scale = HEAD_DIM ** -0.5

    def one_block(qblk):
        s = jnp.einsum('bqkgd,bskd->bkgqs', qblk, k, preferred_element_type=jnp.float32) * scale
        p = jax.nn.softmax(s, axis=-1)
        return jnp.einsum('bkgqs,bskd->bqkgd', p.astype(v.dtype), v)

    o = lax.map(one_block, qb)
    return o.transpose(1, 0, 2, 3, 4, 5).reshape(bsz, lq, N_HEADS * HEAD_DIM)


def zoh(lam_re, lam_im, log_step, b_re, b_im):
    lr = lam_re.astype(jnp.float32)
    li = lam_im.astype(jnp.float32)
    dt = jnp.exp(log_step.astype(jnp.float32))[:, None]
    mag = jnp.exp(lr * dt)
    ang = li * dt
    ab_re, ab_im = mag * jnp.cos(ang), mag * jnp.sin(ang)
    nr, ni = ab_re - 1.0, ab_im
    den = lr * lr + li * li
    f_re = (nr * lr + ni * li) / den
    f_im = (ni * lr - nr * li) / den
    br, bi = b_re.astype(jnp.float32), b_im.astype(jnp.float32)
    bb_re = f_re[..., None] * br - f_im[..., None] * bi
    bb_im = f_re[..., None] * bi + f_im[..., None] * br
    return ab_re, ab_im, bb_re, bb_im


def linrec_combine(e1, e2):
    a1r, a1i, b1r, b1i = e1
    a2r, a2i, b2r, b2i = e2
    return (a2r * a1r - a2i * a1i,
            a2r * a1i + a2i * a1r,
            a2r * b1r - a2i * b1i + b2r,
            a2r * b1i + a2i * b1r + b2i)


def ssm_direction(u, lam_re, lam_im, log_step, b_re, b_im, c_re, c_im, h0_re, h0_im, reverse):
    ab_re, ab_im, bb_re, bb_im = zoh(lam_re, lam_im, log_step, b_re, b_im)
    if reverse:
        u = jnp.flip(u, axis=1)
    bu_re = jnp.einsum('blgc,gpc->blgp', u, bb_re)
    bu_im = jnp.einsum('blgc,gpc->blgp', u, bb_im)
    a_re = jnp.broadcast_to(ab_re, bu_re.shape)
    a_im = jnp.broadcast_to(ab_im, bu_im.shape)
    cum_r, cum_i, s_r, s_i = lax.associative_scan(linrec_combine, (a_re, a_im, bu_re, bu_im), axis=1)
    h0r = h0_re.astype(jnp.float32)[:, None]
    h0i = h0_im.astype(jnp.float32)[:, None]
    h_r = cum_r * h0r - cum_i * h0i + s_r
    h_i = cum_r * h0i + cum_i * h0r + s_i
    y = (jnp.einsum('blgp,gcp->blgc', h_r, c_re.astype(jnp.float32))
         - jnp.einsum('blgp,gcp->blgc', h_i, c_im.astype(jnp.float32)))
    if reverse:
        y = jnp.flip(y, axis=1)
    return y, h_r[:, -1], h_i[:, -1]


def ssm_branch(u, p, h0):
    bsz, length = u.shape[0], u.shape[1]
    uf = u.astype(jnp.float32).reshape(bsz, length, N_SSM_GROUPS, SSM_GROUP)
    y_f, fr, fi = ssm_direction(uf, p['lam_re'][0], p['lam_im'][0], p['log_step'][0], p['b_re'][0], p['b_im'][0],
                                p['c_re'][0], p['c_im'][0], h0[0][0], h0[0][1], False)
    y_b, br, bi = ssm_direction(uf, p['lam_re'][1], p['lam_im'][1], p['log_step'][1], p['b_re'][1], p['b_im'][1],
                                p['c_re'][1], p['c_im'][1], h0[1][0], h0[1][1], True)
    y = ((y_f + y_b).reshape(bsz, length, SSM_WIDTH)
         + p['d_skip'].astype(jnp.float32) * uf.reshape(bsz, length, SSM_WIDTH))
    z = y.astype(u.dtype) @ p['w_glu']
    a, g = jnp.split(z, 2, axis=-1)
    return a * jax.nn.sigmoid(g), ((fr, fi), (br, bi))


def fourier_branch(f, w_fft):
    bsz, length = f.shape[0], f.shape[1]
    fg = f.astype(jnp.float32).reshape(bsz, length, N_FFT_GROUPS, FFT_GROUP)
    mixed = jnp.real(jnp.fft.fft2(fg, axes=(1, 3), norm='ortho'))
    return mixed.reshape(bsz, length, FFT_WIDTH).astype(f.dtype) @ w_fft


def modulated_input(x, cvec, p):
    mod = jax.nn.silu(cvec) @ p['w_mod'] + p['b_mod']
    shift, scale, gate = jnp.split(mod[:, None, :], 3, axis=-1)
    h = rms_norm(x, p['norm_g']) * (1 + scale) + shift
    return h @ p['w_in'], gate


def split_proj(proj):
    idx = np.cumsum(IN_SIZES)[:-1].tolist()
    return jnp.split(proj, idx, axis=-1)


def mixer_layer(x, cvec, p, ctx_k=None, ctx_v=None, h0=None, angles=None):
    bsz, length = x.shape[0], x.shape[1]
    proj, gate = modulated_input(x, cvec, p)
    q, k, v, g_attn, u, g_ssm, f, g_fft = split_proj(proj)
    q = rms_norm(q.reshape(bsz, length, N_HEADS, HEAD_DIM), p['q_norm'])
    k = rms_norm(k.reshape(bsz, length, N_KV_HEADS, HEAD_DIM), p['k_norm'])
    v = v.reshape(bsz, length, N_KV_HEADS, HEAD_DIM)
    if angles is None:
        keys, vals = k, v
        zeros = jnp.zeros((bsz, N_SSM_GROUPS, SSM_STATE), jnp.float32)
        h0 = ((zeros, zeros), (zeros, zeros))
    else:
        q = axial_rope(q, angles[0], angles[1])
        k = axial_rope(k, angles[0], angles[1])
        keys = jnp.concatenate([ctx_k.astype(k.dtype), k], axis=1)
        vals = jnp.concatenate([ctx_v.astype(v.dtype), v], axis=1)
    attn = blocked_attention(q, keys, vals) * jax.nn.silu(g_attn)
    ssm, finals = ssm_branch(u, p, h0)
    ssm = ssm * jax.nn.silu(g_ssm)
    four = fourier_branch(f, p['w_fft']) * jax.nn.silu(g_fft)
    out = jnp.concatenate([attn, ssm.astype(attn.dtype), four.astype(attn.dtype)], axis=-1) @ p['w_out']
    return x + gate * out, k, v, finals


def layer_params(l, norm_g, w_mod, b_mod, w_in, q_norm, k_norm, lam_re, lam_im, log_step,
                 b_re, b_im, c_re, c_im, d_skip, w_glu, w_fft, w_out):
    return dict(norm_g=norm_g[l], w_mod=w_mod[l], b_mod=b_mod[l], w_in=w_in[l],
                q_norm=q_norm[l], k_norm=k_norm[l], lam_re=lam_re[l], lam_im=lam_im[l],
                log_step=log_step[l], b_re=b_re[l], b_im=b_im[l], c_re=c_re[l], c_im=c_im[l],
                d_skip=d_skip[l], w_glu=w_glu[l], w_fft=w_fft[l], w_out=w_out[l])


def setup_inputs(seed: int = 0) -> dict:
    key = jax.random.key(seed)
    ks = jax.random.split(key, 32)
    f32 = jnp.float32

    def nrm(k, shape, s):
        return s * jax.random.normal(k, shape, f32)

    n_idx = jnp.arange(SSM_STATE, dtype=f32)
    ssm_shape = (DEPTH, 2, N_SSM_GROUPS, SSM_STATE)
    state_shape = (DEC_BATCH, DEPTH, N_SSM_GROUPS, SSM_STATE)
    kv_shape = (DEC_BATCH, DEPTH, PAST_LEN, N_KV_HEADS, HEAD_DIM)
    return {
        'x_prompt': nrm(ks[0], (BATCH, SEQ, D_MODEL), 1.0),
        'x_sample': nrm(ks[1], (DEC_BATCH, DEC_SEQ, D_MODEL), 1.0),
        'cache_k': nrm(ks[2], kv_shape, 1.0),
        'cache_v': nrm(ks[3], kv_shape, 1.0),
        'state_fwd_re': nrm(ks[4], state_shape, 0.5),
        'state_fwd_im': nrm(ks[5], state_shape, 0.5),
        'state_bwd_re': nrm(ks[6], state_shape, 0.5),
        'state_bwd_im': nrm(ks[7], state_shape, 0.5),
        'c': nrm(ks[8], (DEC_BATCH, D_MODEL), 1.0),
        'c_ctx': nrm(ks[9], (D_MODEL,), 1.0),
        'norm_g': 1.0 + nrm(ks[10], (DEPTH, D_MODEL), 0.02),
        'w_mod': nrm(ks[11], (DEPTH, D_MODEL, 3 * D_MODEL), 0.5 * D_MODEL ** -0.5),
        'b_mod': nrm(ks[12], (DEPTH, 3 * D_MODEL), 0.01),
        'w_in': nrm(ks[13], (DEPTH, D_MODEL, IN_WIDTH), D_MODEL ** -0.5),
        'q_norm': 1.0 + nrm(ks[14], (DEPTH, HEAD_DIM), 0.02),
        'k_norm': 1.0 + nrm(ks[15], (DEPTH, HEAD_DIM), 0.02),
        'lam_re': -0.5 + nrm(ks[16], ssm_shape, 0.01),
        'lam_im': math.pi * n_idx + nrm(ks[17], ssm_shape, 0.01),
        'log_step': jax.random.uniform(ks[18], (DEPTH, 2, N_SSM_GROUPS), f32,
                                       minval=math.log(1e-3), maxval=math.log(1e-1)),
        'b_re': nrm(ks[19], (DEPTH, 2, N_SSM_GROUPS, SSM_STATE, SSM_GROUP), (2 * SSM_GROUP) ** -0.5),
        'b_im': nrm(ks[20], (DEPTH, 2, N_SSM_GROUPS, SSM_STATE, SSM_GROUP), (2 * SSM_GROUP) ** -0.5),
        'c_re': nrm(ks[21], (DEPTH, 2, N_SSM_GROUPS, SSM_GROUP, SSM_STATE), (2 * SSM_STATE) ** -0.5),
        'c_im': nrm(ks[22], (DEPTH, 2, N_SSM_GROUPS, SSM_GROUP, SSM_STATE), (2 * SSM_STATE) ** -0.5),
        'd_skip': nrm(ks[23], (DEPTH, SSM_WIDTH), 0.5),
        'w_glu': nrm(ks[24], (DEPTH, SSM_WIDTH, 2 * SSM_WIDTH), SSM_WIDTH ** -0.5),
        'w_fft': nrm(ks[25], (DEPTH, FFT_WIDTH, FFT_WIDTH), FFT_WIDTH ** -0.5),
        'w_out': nrm(ks[26], (DEPTH, MIX_WIDTH, D_MODEL), MIX_WIDTH ** -0.5),
        'final_norm_g': 1.0 + nrm(ks[27], (D_MODEL,), 0.02),
    }


def reference(x_prompt, x_sample, cache_k, cache_v, state_fwd_re, state_fwd_im, state_bwd_re, state_bwd_im,
              c, c_ctx, norm_g, w_mod, b_mod, w_in, q_norm, k_norm, lam_re, lam_im, log_step,
              b_re, b_im, c_re, c_im, d_skip, w_glu, w_fft, w_out, final_norm_g):
    params = [layer_params(l, norm_g, w_mod, b_mod, w_in, q_norm, k_norm, lam_re, lam_im, log_step,
                           b_re, b_im, c_re, c_im, d_skip, w_glu, w_fft, w_out) for l in range(DEPTH)]

    ctx_vec = c_ctx[None, :]
    xp = x_prompt
    ks_, vs_, fr_, fi_, br_, bi_ = [], [], [], [], [], []
    for l in range(DEPTH):
        xp, k_l, v_l, fin = mixer_layer(xp, ctx_vec, params[l])
        ks_.append(k_l)
        vs_.append(v_l)
        fr_.append(fin[0][0])
        fi_.append(fin[0][1])
        br_.append(fin[1][0])
        bi_.append(fin[1][1])
    y_prompt = rms_norm(xp, final_norm_g)

    angles = grid_angles(x_sample.shape[1])
    xs = x_sample
    for l in range(DEPTH):
        h0 = ((state_fwd_re[:, l], state_fwd_im[:, l]), (state_bwd_re[:, l], state_bwd_im[:, l]))
        xs, _, _, _ = mixer_layer(xs, c, params[l], cache_k[:, l], cache_v[:, l], h0, angles)
    y_sample = rms_norm(xs, final_norm_g)

    return (y_prompt, y_sample,
            jnp.stack(ks_, axis=1), jnp.stack(vs_, axis=1),
            jnp.stack(fr_, axis=1), jnp.stack(fi_, axis=1),
            jnp.stack(br_, axis=1), jnp.stack(bi_, axis=1))
```

```python
import math
import numpy as np
import ml_dtypes
import concourse.bass as bass
import concourse.mybir as mybir
from concourse.bass_utils import run_bass_kernel_spmd

F32 = mybir.dt.float32
BF16 = mybir.dt.bfloat16
I32 = mybir.dt.int32
ALU = mybir.AluOpType
AF = mybir.ActivationFunctionType

D = 4096
HD = 128
NH = 16
NKV = 4
INW = 9216
EPS = 1e-6
PI = math.pi
TW0 = 512

CFG_FULL = dict(DEPTH=4, NPS=4, LP=256, LS=4096, PAST=512)


class Sched:
    def __init__(self, nc, sems, dma_sems):
        self.nc = nc
        self.eng = dict(pe=nc.tensor, dve=nc.vector, act=nc.scalar, pool=nc.gpsimd, sp=nc.sync)
        self.sem = sems
        self.cnt = {e: 0 for e in sems}
        self.dsem = dma_sems
        self.dcnt = {q: [0] * len(v) for q, v in dma_sems.items()}
        self.dnext = {q: 0 for q in dma_sems}
        self.seen = {e: {} for e in self.eng}
        self.lastw = {}
        self.readers = {}
        self.semobj = {}

    def _need(self, e, r, w):
        need = {}
        def add(tok):
            if tok is None:
                return
            sid, val = tok
            if need.get(sid, 0) < val:
                need[sid] = val
        for k in r:
            add(self.lastw.get(k))
        for k in w:
            add(self.lastw.get(k))
            for t in self.readers.get(k, {}).items():
                add(t)
        eng = self.eng[e]
        for sid, val in need.items():
            if e == 'pe' and sid == 'E_pe':
                continue
            if self.seen[e].get(sid, 0) < val:
                eng.wait_ge(self.semobj[sid], val)
                self.seen[e][sid] = val

    def _record(self, tok, r, w):
        for k in w:
            self.lastw[k] = tok
            self.readers[k] = {}
        for k in r:
            d = self.readers.setdefault(k, {})
            if d.get(tok[0], 0) < tok[1]:
                d[tok[0]] = tok[1]

    def op(self, e, fn, r=(), w=()):
        self._need(e, r, w)
        inst = fn(self.eng[e])
        sid = 'E_' + e
        self.semobj[sid] = self.sem[e]
        self.cnt[e] += 1
        inst.then_inc(self.sem[e], 1)
        self._record((sid, self.cnt[e]), r, w)

    def dma(self, q, out, in_, r=(), w=(), slow=False):
        self._need(q, r, w)
        i = self.dnext[q]
        self.dnext[q] = (i + 1) % len(self.dsem[q])
        sid = 'D_%s_%d' % (q, i)
        self.semobj[sid] = self.dsem[q][i]
        if self.dcnt[q][i] > 0 and self.seen[q].get(sid, 0) < self.dcnt[q][i]:
            self.eng[q].wait_ge(self.dsem[q][i], self.dcnt[q][i])
            self.seen[q][sid] = self.dcnt[q][i]
        self.dcnt[q][i] += 16
        if slow:
            inst = self.eng[q].dma_start(out=out, in_=in_, allow_slow_non_contiguous=True)
        else:
            inst = self.eng[q].dma_start(out=out, in_=in_)
        inst.then_inc(self.dsem[q][i], 16)
        self._record((sid, self.dcnt[q][i]), r, w)

    def drain(self):
        self.finish()
        self.nc.all_engine_barrier()

    def finish(self):
        sp = self.eng['sp']
        for q, lst in self.dsem.items():
            for i, s in enumerate(lst):
                if self.dcnt[q][i] > 0:
                    sp.wait_ge(s, self.dcnt[q][i])
        for e, s in self.sem.items():
            if self.cnt[e] > 0:
                sp.wait_ge(s, self.cnt[e])


def build(cfg):
    DEPTH, NPS, LP, LS, PAST = cfg['DEPTH'], cfg['NPS'], cfg['LP'], cfg['LS'], cfg['PAST']
    NTP = NPS * LP
    NT = NTP + LS
    NKEY = NT + PAST
    assert NTP % TW0 == 0 and LS % TW0 == 0
    NTILE = NT // TW0
    NPT = NTP // TW0
    seqs = [(i * LP, LP, False) for i in range(NPS)] + [(NTP, LS, True)]

    nc = bass.Bass("TRN2", target_bir_lowering=False)

    def din(name, shape, dt=F32):
        return nc.dram_tensor(name, list(shape), dt, kind="ExternalInput").ap()

    def dout(name, shape, dt=F32):
        return nc.dram_tensor(name, list(shape), dt, kind="ExternalOutput").ap()

    def dscr(name, shape, dt=BF16):
        return nc.dram_tensor(name, list(shape), dt, kind="Internal").ap()

    x_in = din("x_in", [NT, D])
    cvec = din("cvec", [128, 32, 2])
    cache_kT = din("cache_kT", [DEPTH, NKV, 128, PAST])
    cache_v = din("cache_v", [DEPTH, PAST, NKV * HD])
    st_q = din("st_q", [DEPTH, 128, 2, 2, 32])
    normg_p = din("normg_p", [DEPTH, 128, 32])
    w_mod = din("w_mod", [DEPTH, D, 3 * D])
    bmod_ps = din("bmod_ps", [DEPTH, 128, 64])
    bmodg_rep = din("bmodg_rep", [DEPTH, 128, D])
    w_in = din("w_in", [DEPTH, D, INW])
    qk_g = din("qk_g", [DEPTH, 128, 2])
    lam_q = din("lam_q", [DEPTH, 3, 128, 64])
    lam_rep = din("lam_rep", [DEPTH, 3, 128, 2, 4096])
    bT_pad = din("bT_pad", [DEPTH, 2, 8, 128, 2, 512])
    cT_pad = din("cT_pad", [DEPTH, 2, 8, 128, 2, 4, 128])
    dskip_p = din("dskip_p", [DEPTH, 128, 8])
    w_glu = din("w_glu", [DEPTH, 1024, 2048])
    w_fft = din("w_fft", [DEPTH, 1024, 1024])
    w_out = din("w_out", [DEPTH, D, D])
    fng_rep = din("fng_rep", [128, D])
    c_ident = din("c_ident", [128, 128])
    c_RT = din("c_RT", [128, 128])
    c_rope = din("c_rope", [2, 128, LS])
    c_jv = din("c_jv", [128, 512])
    c_cs256 = din("c_cs256", [128, 2, 512])
    c_dftp = din("c_dftp", [2, LP, LP], BF16)
    c_dfts = din("c_dfts", [2, LS, LS], BF16)

    y_out = dout("y_out", [NT, D])
    newk_out = dout("newk_out", [NPS, DEPTH, LP, NKV * HD])
    newv_out = dout("newv_out", [NPS, DEPTH, LP, NKV * HD])
    fin_out = dout("fin_out", [DEPTH, 128, NPS * 2 * 2 * 32])

    xres = dscr("xres", [NT, D], F32)
    qT = dscr("qT", [NH, 128, NT])
    kT = dscr("kT", [NKV, 128, NT])
    vS = dscr("vS", [NT, NKV * HD])
    gaT = dscr("gaT", [2048, NT])
    gsT = dscr("gsT", [1024, NT])
    gfT = dscr("gfT", [1024, NT])
    uT = dscr("uT", [1024, NT])
    fT = dscr("fT", [1024, NT])
    yT = dscr("yT", [1024, NT])
    fcs = dscr("fcs", [NT, 4, 512])
    dT = dscr("dT", [1024, NT])
    mixT = dscr("mixT", [D, NT])

    import contextlib
    es = contextlib.ExitStack()
    with es:
        sems = {e: es.enter_context(nc.semaphore("E_" + e)) for e in ('pe', 'dve', 'act', 'pool')}
        dsems = {q: [es.enter_context(nc.semaphore("D_%s_%d" % (q, i))) for i in range(12)] for q in ('sp', 'pool')}
        S = Sched(nc, sems, dsems)

        def sb(name, shape, dt):
            return es.enter_context(nc.sbuf_tensor(name, list(shape), dt))

        ps = [es.enter_context(nc.psum_tensor("ps%d" % i, [128, 512], F32)) for i in range(8)]
        PSK = ['ps%d' % i for i in range(8)]

        ident = sb("ident", [128, 128], F32)
        RTb = sb("RTb", [128, 128], BF16)
        onesb = sb("onesb", [128, 128], BF16)
        onesf = sb("onesf", [128, 128], F32)
        cv = sb("cv", [128, 32, 2], F32)
        s_bf = sb("s_bf", [128, 32, 2], BF16)
        s_f = sb("s_f", [128, 32, 2], F32)
        Amod = sb("Amod", [128, 32, 2], F32)
        Bmod = sb("Bmod", [128, 32, 2], F32)
        modsb = sb("modsb", [128, 64, 2], F32)
        bmp = sb("bmp", [128, 64], F32)
        ngp = sb("ngp", [128, 32], F32)
        qkg = sb("qkg", [128, 2], F32)
        qkg2 = sb("qkg2", [128, 2], F32)
        small = sb("small", [128, 16], F32)
        negpi = sb("negpi", [128, 1], F32)

        S.dma('sp', ident[:], c_ident[:, :], w=['ident'])
        S.dma('pool', RTb[:], c_RT[:, :], w=['RTb'])
        S.dma('sp', cv[:], cvec[:, :, :], w=['cv'])
        S.op('dve', lambda e: e.memset(onesf[:], 1.0), w=['onesf'])
        S.op('dve', lambda e: e.memset(onesb[:], 1.0), w=['onesb'])
        S.op('act', lambda e: e.activation(out=s_f[:], in_=cv[:], func=AF.Silu), r=['cv'], w=['s_f'])
        S.op('dve', lambda e: e.tensor_copy(out=s_bf[:], in_=s_f[:]), r=['s_f'], w=['s_bf'])

        slab = [None, None]
        slab_i = [0]

        def load_slab(src_ap):
            i = slab_i[0] % 2
            slab_i[0] += 1
            S.dma('pool', slab[i][:], src_ap.rearrange("(c p) m -> p c m", p=128), w=['slab%d' % i])
            return slab[i], 'slab%d' % i

        def rstd_from(out_ap, in_ap, scale, keys_r, key_w, tmpk='small'):
            S.op('dve', lambda e: e.tensor_scalar(out=out_ap, in0=in_ap, scalar1=scale, scalar2=EPS,
                                                  op0=ALU.mult, op1=ALU.add), r=keys_r, w=[key_w])
            S.op('act', lambda e: e.activation(out=out_ap, in_=out_ap, func=AF.Sqrt), r=[key_w], w=[key_w])
            S.op('dve', lambda e: e.reciprocal(out=out_ap, in_=out_ap), r=[key_w], w=[key_w])

        def sin_of(out_ap, arg_ap, tmpf, tmpi, shift, keys_r, key_w, kf, ki):
            S.op('dve', lambda e: e.tensor_scalar(out=tmpf, in0=arg_ap, scalar1=shift, scalar2=1.0 / (2 * PI),
                                                  op0=ALU.add, op1=ALU.mult), r=keys_r, w=[kf])
            S.op('dve', lambda e: e.tensor_copy(out=tmpi, in_=tmpf), r=[kf], w=[ki])
            S.op('dve', lambda e: e.tensor_copy(out=tmpf, in_=tmpi), r=[ki], w=[kf])
            S.op('dve', lambda e: e.scalar_tensor_tensor(out=tmpf, in0=tmpf, scalar=-2 * PI, in1=arg_ap,
                                                         op0=ALU.mult, op1=ALU.add), r=[kf] + list(keys_r), w=[kf])
            S.op('dve', lambda e: e.tensor_scalar(out=tmpf, in0=tmpf, scalar1=shift, scalar2=3.1415925,
                                                  op0=ALU.add, op1=ALU.min), r=[kf], w=[kf])
            S.op('dve', lambda e: e.tensor_scalar(out=tmpf, in0=tmpf, scalar1=-3.1415925, scalar2=None,
                                                  op0=ALU.max), r=[kf], w=[kf])
            S.op('act', lambda e: e.activation(out=out_ap, in_=tmpf, func=AF.Sin), r=[kf], w=[key_w])

        for l in range(DEPTH):
            xsrc = x_in if l == 0 else xres
            with contextlib.ExitStack() as ph:
                def sbp(name, shape, dt):
                    return ph.enter_context(nc.sbuf_tensor("%s_u%d" % (name, nc.next_id()), list(shape), dt))
                S.dma('sp', bmp[:], bmod_ps[l], w=['bmp'])
                S.dma('sp', ngp[:], normg_p[l], w=['ngp'])
                S.dma('sp', qkg[:], qk_g[l], w=['qkg'])
                slab[0] = sbp("slabA", [128, 32, 512], BF16)
                slab[1] = sbp("slabB", [128, 32, 512], BF16)
                Srep = sbp("Srep", [128, 2, 32, 128], BF16)
                gbias = sbp("gbias", [128, D], F32)
                S.dma('sp', gbias[:], bmodg_rep[l], w=['gbias'])
                for v in range(2):
                    for c in range(32):
                        S.op('pool', lambda e, v=v, c=c: e.tensor_scalar(out=Srep[:, v, c, :], in0=onesf[:], scalar1=s_f[:, c, v:v + 1],
                                                                        scalar2=None, op0=ALU.mult),
                             r=['onesf', 's_f'], w=['Srep'])
                for si in range(16):
                    sl, sk = load_slab(w_mod[l][:, si * 512:(si + 1) * 512])
                    for j in range(4):
                        blk = si * 4 + j
                        for c in range(32):
                            S.op('pe', lambda e, c=c, j=j, blk=blk, sl=sl: e.matmul(ps[0][:, blk * 2:blk * 2 + 2], lhsT=sl[:, c, j * 128:(j + 1) * 128],
                                                                                  rhs=s_bf[:, c, :], start=(c == 0), stop=(c == 31)),
                                 r=[sk, 's_bf'], w=['ps0'])
                S.op('dve', lambda e: e.tensor_tensor(out=modsb[:], in0=ps[0][:, 0:128].rearrange("p (b v) -> p b v", v=2),
                                                      in1=bmp[:].unsqueeze(2).to_broadcast([128, 64, 2]), op=ALU.add),
                     r=['ps0', 'bmp'], w=['modsb'])
                S.op('dve', lambda e: e.tensor_copy(out=Bmod[:], in_=modsb[:, 0:32, :]), r=['modsb'], w=['Bmod'])
                S.op('dve', lambda e: e.tensor_scalar(out=Amod[:], in0=modsb[:, 32:64, :], scalar1=1.0, scalar2=None, op0=ALU.add),
                     r=['modsb'], w=['Amod'])
                S.op('dve', lambda e: e.tensor_tensor(out=Amod[:], in0=Amod[:], in1=ngp[:].unsqueeze(2).to_broadcast([128, 32, 2]), op=ALU.mult),
                     r=['Amod', 'ngp'], w=['Amod'])
                S.op('dve', lambda e: e.tensor_scalar(out=qkg2[:, 0:1], in0=qkg[:, 0:1], scalar1=HD ** -0.5, scalar2=None, op0=ALU.mult),
                     r=['qkg'], w=['qkg2'])
                S.op('dve', lambda e: e.tensor_copy(out=qkg2[:, 1:2], in_=qkg[:, 1:2]), r=['qkg', 'qkg2'], w=['qkg2'])
                gate_sb = sbp("gate_sb", [128, 2, 512], F32)
                gate_dr = dscr("gate_dr_l%d" % l, [2, 128, D], F32)
                for si in range(16, 24):
                    sl, sk = load_slab(w_mod[l][:, si * 512:(si + 1) * 512])
                    g0 = (si - 16) * 512
                    for v in range(2):
                        for c in range(32):
                            S.op('pe', lambda e, c=c, v=v, sl=sl: e.matmul(ps[1 + v][:, :], lhsT=Srep[:, v, c, :], rhs=sl[:, c, :],
                                                                         start=(c == 0), stop=(c == 31)),
                                 r=[sk, 'Srep'], w=[PSK[1 + v]])
                        S.op('dve', lambda e, v=v, g0=g0: e.tensor_tensor(out=gate_sb[:, v, :], in0=ps[1 + v][:, :], in1=gbias[:, g0:g0 + 512], op=ALU.add),
                             r=[PSK[1 + v], 'gbias'], w=['gate_sb%d' % v])
                        S.dma('sp', gate_dr[v][:, g0:g0 + 512], gate_sb[:, v, :], r=['gate_sb%d' % v], w=['gate_dr'])
            S.drain()
            if cfg.get('STOP', 99) == 0:
                break

            with contextlib.ExitStack() as ph:
                def sbp(name, shape, dt):
                    return ph.enter_context(nc.sbuf_tensor("%s_u%d" % (name, nc.next_id()), list(shape), dt))
                slab[0] = sbp("slabA", [128, 32, 512], BF16)
                slab[1] = sbp("slabB", [128, 32, 512], BF16)
                xs = [sbp("xs%d" % i, [128, D], F32) for i in range(2)]
                junk = sbp("junk", [128, D], BF16)
                hT = sbp("hT", [128, 32, 512], BF16)
                ssq = sbp("ssq", [128, 4], F32)
                sq = sbp("sq", [128, 512], BF16)
                rs = sbp("rs", [128, 512], F32)
                qn = sbp("qn", [128, 512], F32)
                qb = sbp("qb", [128, 512], BF16)
                qo = [sbp("qo%d" % i, [128, 512], BF16) for i in range(2)]
                t1 = sbp("t1", [128, 512], F32)
                t2 = sbp("t2", [128, 512], F32)
                ropec = sbp("ropec", [128, 512], F32)
                ropes = sbp("ropes", [128, 512], F32)
                vb = sbp("vb", [128, 512], BF16)
                vf = sbp("vf", [128, 512], F32)
                ko = sbp("ko", [128, 512], F32)
                ev = [sbp("ev%d" % i, [128, 512], BF16) for i in range(2)]
                evi = 0
                for ti in range(NTILE):
                    t0 = ti * TW0
                    v = 0 if ti < NPT else 1
                    samp = ti >= NPT
                    if samp:
                        p0 = t0 - NTP
                        S.dma('sp', ropec[:], c_rope[0][:, p0:p0 + 512], w=['ropec'])
                        S.dma('sp', ropes[:], c_rope[1][:, p0:p0 + 512], w=['ropes'])
                    for sub in range(4):
                        xt = xs[sub % 2]
                        xk = 'xs%d' % (sub % 2)
                        S.dma('sp', xt[:], xsrc[t0 + sub * 128:t0 + (sub + 1) * 128, :], w=[xk])
                        S.op('act', lambda e, xt=xt, sub=sub: e.activation(out=junk[:], in_=xt[:], func=AF.Square, accum_out=ssq[:, sub:sub + 1]),
                             r=[xk], w=['junk', 'ssq%d' % sub])
                        rstd_from(ssq[:, sub:sub + 1], ssq[:, sub:sub + 1], 1.0 / D, ['ssq%d' % sub], 'ssq%d' % sub)
                        S.op('pool', lambda e, xt=xt, sub=sub: e.tensor_scalar(out=xt[:], in0=xt[:], scalar1=ssq[:, sub:sub + 1], scalar2=None, op0=ALU.mult),
                             r=[xk, 'ssq%d' % sub], w=[xk])
                        for c0 in range(0, 32, 4):
                            bank = 4 + (c0 // 4) % 2
                            for cc in range(4):
                                c = c0 + cc
                                S.op('pe', lambda e, c=c, cc=cc, xt=xt, bank=bank: e.transpose(out=ps[bank][:, cc * 128:(cc + 1) * 128], in_=xt[:, c * 128:(c + 1) * 128], identity=ident[:]),
                                     r=[xk, 'ident'], w=[PSK[bank]])
                            for cc in range(4):
                                c = c0 + cc
                                S.op('dve', lambda e, c=c, cc=cc, bank=bank, sub=sub, v=v: e.tensor_scalar(
                                    out=hT[:, c, sub * 128:(sub + 1) * 128], in0=ps[bank][:, cc * 128:(cc + 1) * 128],
                                    scalar1=Amod[:, c, v:v + 1], scalar2=Bmod[:, c, v:v + 1], op0=ALU.mult, op1=ALU.add),
                                    r=[PSK[bank], 'Amod', 'Bmod'], w=['hT'])
                    if cfg.get('P1STOP', 0) == 1:
                        break
                    for si in range(18):
                        if cfg.get('P1STOP', 0) == 2 + si:
                            break
                        sl, sk = load_slab(w_in[l][:, si * 512:(si + 1) * 512])
                        if si == 5:
                            for sub in range(4):
                                bank = sub % 2
                                for c in range(32):
                                    S.op('pe', lambda e, c=c, sub=sub, sl=sl, bank=bank: e.matmul(ps[bank][:, :], lhsT=hT[:, c, sub * 128:(sub + 1) * 128], rhs=sl[:, c, :],
                                                                                                start=(c == 0), stop=(c == 31)),
                                         r=[sk, 'hT'], w=[PSK[bank]])
                                S.op('dve', lambda e, bank=bank: e.tensor_copy(out=vf[:], in_=ps[bank][:, :]), r=[PSK[bank]], w=['vf'])
                                S.op('act', lambda e: e.activation(out=vb[:], in_=vf[:], func=AF.Copy), r=['vf'], w=['vb'])
                                S.dma('sp', vS[t0 + sub * 128:t0 + (sub + 1) * 128, :], vb[:], r=['vb'], w=['vS%d' % ti])
                                if not samp:
                                    tok = t0 + sub * 128
                                    S.dma('sp', newv_out[tok // LP, l, tok % LP:tok % LP + 128, :], vf[:], r=['vf'], w=['newv'])
                            continue
                        for j in range(4):
                            fb = si * 4 + j
                            bank = fb % 2
                            for c in range(32):
                                S.op('pe', lambda e, c=c, j=j, sl=sl, bank=bank: e.matmul(ps[bank][:, :], lhsT=sl[:, c, j * 128:(j + 1) * 128], rhs=hT[:, c, :],
                                                                                        start=(c == 0), stop=(c == 31)),
                                     r=[sk, 'hT'], w=[PSK[bank]])
                            if si <= 4:
                                isk = si == 4
                                S.op('act', lambda e, bank=bank: e.activation(out=sq[:], in_=ps[bank][:, :], func=AF.Square), r=[PSK[bank]], w=['sq'])
                                S.op('pe', lambda e: e.matmul(ps[2][:, :], lhsT=onesb[:], rhs=sq[:], start=True, stop=True), r=['sq', 'onesb'], w=['ps2'])
                                rstd_from(rs[:], ps[2][:, :], 1.0 / HD, ['ps2'], 'rs')
                                gcol = 1 if isk else 0
                                S.op('dve', lambda e, bank=bank, gcol=gcol: e.scalar_tensor_tensor(out=qn[:], in0=ps[bank][:, :], scalar=qkg2[:, gcol:gcol + 1], in1=rs[:],
                                                                                                 op0=ALU.mult, op1=ALU.mult),
                                     r=[PSK[bank], 'qkg2', 'rs'], w=['qn'])
                                if isk and not samp:
                                    for sub in range(4):
                                        S.op('pe', lambda e, sub=sub: e.transpose(out=ps[3][:, sub * 128:(sub + 1) * 128], in_=qn[:, sub * 128:(sub + 1) * 128], identity=ident[:]),
                                             r=['qn', 'ident'], w=['ps3'])
                                    S.op('dve', lambda e: e.tensor_copy(out=ko[:], in_=ps[3][:, :]), r=['ps3'], w=['ko'])
                                    for sub in range(4):
                                        tok = t0 + sub * 128
                                        S.dma('sp', newk_out[tok // LP, l, tok % LP:tok % LP + 128, j * 128:(j + 1) * 128], ko[:, sub * 128:(sub + 1) * 128],
                                              r=['ko'], w=['newk'])
                                o = qo[evi % 2]
                                ok = 'qo%d' % (evi % 2)
                                evi += 1
                                if samp:
                                    S.op('act', lambda e: e.activation(out=qb[:], in_=qn[:], func=AF.Copy), r=['qn'], w=['qb'])
                                    S.op('pe', lambda e: e.matmul(ps[3][:, :], lhsT=RTb[:], rhs=qb[:], start=True, stop=True), r=['qb', 'RTb'], w=['ps3'])
                                    S.op('pool', lambda e: e.tensor_tensor(out=t1[:], in0=qn[:], in1=ropec[:], op=ALU.mult), r=['qn', 'ropec'], w=['t1'])
                                    S.op('dve', lambda e: e.tensor_tensor(out=t2[:], in0=ps[3][:, :], in1=ropes[:], op=ALU.mult), r=['ps3', 'ropes'], w=['t2'])
                                    S.op('pool', lambda e, o=o: e.tensor_tensor(out=o[:], in0=t1[:], in1=t2[:], op=ALU.add), r=['t1', 't2'], w=[ok])
                                else:
                                    S.op('act', lambda e, o=o: e.activation(out=o[:], in_=qn[:], func=AF.Copy), r=['qn'], w=[ok])
                                if isk:
                                    S.dma('sp', kT[j][:, t0:t0 + 512], o[:], r=[ok], w=['kT%d' % ti])
                                else:
                                    S.dma('sp', qT[fb][:, t0:t0 + 512], o[:], r=[ok], w=['qT%d' % ti])
                            else:
                                o = ev[evi % 2]
                                ok = 'ev%d' % (evi % 2)
                                evi += 1
                                f0 = fb * 128
                                if 3072 <= f0 < 5120:
                                    dst, dk_, fn = gaT[f0 - 3072:f0 - 3072 + 128, t0:t0 + 512], 'gaT%d' % ti, AF.Silu
                                elif 5120 <= f0 < 6144:
                                    dst, dk_, fn = uT[f0 - 5120:f0 - 5120 + 128, t0:t0 + 512], 'uT', AF.Copy
                                elif 6144 <= f0 < 7168:
                                    dst, dk_, fn = gsT[f0 - 6144:f0 - 6144 + 128, t0:t0 + 512], 'gsT%d' % ti, AF.Silu
                                elif 7168 <= f0 < 8192:
                                    dst, dk_, fn = fT[f0 - 7168:f0 - 7168 + 128, t0:t0 + 512], 'fT%d' % ti, AF.Copy
                                else:
                                    dst, dk_, fn = gfT[f0 - 8192:f0 - 8192 + 128, t0:t0 + 512], 'gfT%d' % ti, AF.Silu
                                S.op('act', lambda e, o=o, bank=bank, fn=fn: e.activation(out=o[:], in_=ps[bank][:, :], func=fn), r=[PSK[bank]], w=[ok])
                                S.dma('sp', dst, o[:], r=[ok], w=[dk_])
            S.drain()
            if cfg.get('STOP', 99) == 1:
                break

            with contextlib.ExitStack() as ph:
                def sbp(name, shape, dt):
                    return ph.enter_context(nc.sbuf_tensor("%s_u%d" % (name, nc.next_id()), list(shape), dt))
                NKMAX = LS + PAST
                KTs = sbp("KTs", [128, NKV, NKMAX], BF16)
                Vs = sbp("Vs", [128, NKMAX // 128, NKV * HD], BF16)
                Q4 = [sbp("Q4_%d" % i, [128, 4, 128], BF16) for i in range(2)]
                G4 = [sbp("G4_%d" % i, [128, 4, 128], BF16) for i in range(2)]
                Pb = [sbp("Pb%d" % i, [128, 512], BF16) for i in range(3)]
                rl = sbp("rl", [128, 512], F32)
                pacc = [sbp("pacc%d" % i, [128, 512], F32) for i in range(2)]
                ot = sbp("ot", [128, 512], F32)
                mo = [sbp("mo%d" % i, [128, 4, 128], BF16) for i in range(2)]
                it = 0
                pi_ = 0
                for (s0, L, samp) in seqs:
                    nk = L + (PAST if samp else 0)
                    koff = PAST if samp else 0
                    if samp:
                        for kv in range(NKV):
                            S.dma('pool', KTs[:, kv, 0:PAST], cache_kT[l, kv], w=['KTs'])
                        S.dma('pool', Vs[:, 0:PAST // 128, :], cache_v[l].rearrange("(b p) f -> p b f", p=128), w=['Vs'])
                    for kv in range(NKV):
                        S.dma('sp', KTs[:, kv, koff:koff + L], kT[kv][:, s0:s0 + L], r=['kT%d' % i for i in range(NTILE)], w=['KTs'])
                    S.dma('sp', Vs[:, koff // 128:(koff + L) // 128, :], vS[s0:s0 + L, :].rearrange("(b p) f -> p b f", p=128),
                          r=['vS%d' % i for i in range(NTILE)], w=['Vs'])
                    nkb = nk // 128
                    for kv in range(NKV):
                        for qb_ in range(L // 128):
                            q0 = s0 + qb_ * 128
                            Q = Q4[it % 2]
                            G = G4[it % 2]
                            M = mo[it % 2]
                            qk_ = 'Q4_%d' % (it % 2)
                            gk_ = 'G4_%d' % (it % 2)
                            mk_ = 'mo%d' % (it % 2)
                            po, pl = (4, 5) if it % 2 == 0 else (6, 7)
                            pa, pak = pacc[it % 2], 'pacc%d' % (it % 2)
                            it += 1
                            S.dma('sp', Q[:], qT[4 * kv:4 * kv + 4, :, q0:q0 + 128].rearrange("h d t -> d h t"),
                                  r=['qT%d' % i for i in range(NTILE)], w=[qk_])
                            S.dma('sp', G[:], gaT[kv * 512:(kv + 1) * 512, q0:q0 + 128].rearrange("(h d) t -> d h t", d=128),
                                  r=['gaT%d' % i for i in range(NTILE)], w=[gk_])
                            Qf = Q[:].rearrange("d h t -> d (h t)")
                            def emit_s(kb):
                                sbank = kb % 2
                                S.op('pe', lambda e, kb=kb, kv=kv, sbank=sbank, Qf=Qf: e.matmul(ps[sbank][:, :], lhsT=KTs[:, kv, kb * 128:(kb + 1) * 128], rhs=Qf, start=True, stop=True),
                                     r=['KTs', qk_], w=[PSK[sbank]])
                            emit_s(0)
                            for kb in range(nkb):
                                sbank = kb % 2
                                P = Pb[pi_ % 3]
                                pk_ = 'Pb%d' % (pi_ % 3)
                                pi_ += 1
                                if kb + 1 < nkb:
                                    emit_s(kb + 1)
                                S.op('act', lambda e, P=P, sbank=sbank: e.activation(out=P[:], in_=ps[sbank][:, :], func=AF.Exp), r=[PSK[sbank]], w=[pk_])
                                S.op('pe', lambda e, kb=kb, kv=kv, P=P, po=po: e.matmul(ps[po][:, :], lhsT=Vs[:, kb, kv * 128:(kv + 1) * 128], rhs=P[:], start=(kb == 0), stop=(kb == nkb - 1)),
                                     r=['Vs', pk_], w=[PSK[po]])
                                if kb == 0:
                                    S.op('pool', lambda e, P=P, pa=pa: e.tensor_copy(out=pa[:], in_=P[:]), r=[pk_], w=[pak])
                                else:
                                    S.op('pool', lambda e, P=P, pa=pa: e.tensor_tensor(out=pa[:], in0=pa[:], in1=P[:], op=ALU.add), r=[pk_, pak], w=[pak])
                            S.op('pe', lambda e, pa=pa, pl=pl: e.matmul(ps[pl][:, :], lhsT=onesf[:], rhs=pa[:], start=True, stop=True),
                                 r=['onesf', pak], w=[PSK[pl]])
                            S.op('dve', lambda e, pl=pl: e.reciprocal(out=rl[:], in_=ps[pl][:, :]), r=[PSK[pl]], w=['rl'])
                            S.op('dve', lambda e, po=po: e.tensor_tensor(out=ot[:], in0=ps[po][:, :], in1=rl[:], op=ALU.mult), r=[PSK[po], 'rl'], w=['ot'])
                            S.op('pool', lambda e, M=M, G=G: e.tensor_tensor(out=M[:].rearrange("d h t -> d (h t)"), in0=ot[:], in1=G[:].rearrange("d h t -> d (h t)"), op=ALU.mult),
                                 r=['ot', gk_], w=[mk_])
                            S.dma('sp', mixT[kv * 512:(kv + 1) * 512, q0:q0 + 128].rearrange("(h d) t -> d h t", d=128), M[:], r=[mk_], w=['mixT_a'])
            S.drain()
            if cfg.get('STOP', 99) == 2:
                break

            with contextlib.ExitStack() as ph:
                def sbp(name, shape, dt):
                    return ph.enter_context(nc.sbuf_tensor("%s_u%d" % (name, nc.next_id()), list(shape), dt))
                cs256 = sbp("cs256", [128, 2, 512], BF16)
                S.dma('pool', cs256[:], c_cs256[:, :, :], w=['cs256'])
                fTt = [sbp("fTt%d" % i, [128, 8, 512], BF16) for i in range(2)]
                fco = [sbp("fco%d" % i, [128, 4, 512], BF16) for i in range(2)]
                for ti in range(NTILE):
                    t0 = ti * TW0
                    ft = fTt[ti % 2]
                    fk = 'fTt%d' % (ti % 2)
                    S.dma('sp', ft[:], fT[:, t0:t0 + 512].rearrange("(c p) t -> p c t", p=128), r=['fT%d' % ti], w=[fk])
                    for sub in range(4):
                        fo = fco[sub % 2]
                        fok = 'fco%d' % (sub % 2)
                        for g in range(4):
                            bank = g % 2
                            for cc in range(2):
                                S.op('pe', lambda e, g=g, cc=cc, sub=sub, ft=ft, bank=bank: e.matmul(ps[bank][:, :], lhsT=ft[:, 2 * g + cc, sub * 128:(sub + 1) * 128], rhs=cs256[:, cc, :],
                                                                                                 start=(cc == 0), stop=(cc == 1)),
                                     r=[fk, 'cs256'], w=[PSK[bank]])
                            S.op('act', lambda e, g=g, fo=fo, bank=bank: e.activation(out=fo[:, g, :], in_=ps[bank][:, :], func=AF.Copy), r=[PSK[bank]], w=[fok])
                        S.dma('sp', fcs[t0 + sub * 128:t0 + (sub + 1) * 128, :, :], fo[:], r=[fok], w=['fcs'])
            S.drain()
            if cfg.get('STOP', 99) == 3:
                break
            with contextlib.ExitStack() as ph:
                def sbp(name, shape, dt):
                    return ph.enter_context(nc.sbuf_tensor("%s_u%d" % (name, nc.next_id()), list(shape), dt))
                LMAX = LS
                cosl = sbp("cosl", [128, LMAX // 128, min(512, LMAX)], BF16)
                sinl = sbp("sinl", [128, LMAX // 128, min(512, LMAX)], BF16)
                Fc = [sbp("Fc%d" % i, [128, LMAX // 128, 128], BF16) for i in range(2)]
                Fs = [sbp("Fs%d" % i, [128, LMAX // 128, 128], BF16) for i in range(2)]
                dfo = [sbp("dfo%d" % i, [128, 512], BF16) for i in range(2)]
                it = 0
                for (s0, L, samp) in seqs:
                    dsrc = c_dfts if samp else c_dftp
                    ntb = L // 128
                    KW = min(512, L)
                    for kt in range(L // KW):
                        S.dma('sp', cosl[:, 0:ntb, 0:KW], dsrc[0][:, kt * KW:(kt + 1) * KW].rearrange("(b p) k -> p b k", p=128), w=['cosl'])
                        S.dma('sp', sinl[:, 0:ntb, 0:KW], dsrc[1][:, kt * KW:(kt + 1) * KW].rearrange("(b p) k -> p b k", p=128), w=['sinl'])
                        for mb in range(8):
                            g, half = mb // 2, mb % 2
                            fc_, fs_ = Fc[it % 2], Fs[it % 2]
                            fck, fsk = 'Fc%d' % (it % 2), 'Fs%d' % (it % 2)
                            do_ = dfo[it % 2]
                            dok = 'dfo%d' % (it % 2)
                            bank = 2 + it % 2
                            it += 1
                            S.dma('sp', fc_[:, 0:ntb, :], fcs[s0:s0 + L, g, half * 128:half * 128 + 128].rearrange("(b p) m -> p b m", p=128), r=['fcs'], w=[fck])
                            S.dma('sp', fs_[:, 0:ntb, :], fcs[s0:s0 + L, g, 256 + half * 128:256 + half * 128 + 128].rearrange("(b p) m -> p b m", p=128), r=['fcs'], w=[fsk])
                            for tb in range(ntb):
                                S.op('pe', lambda e, tb=tb, fc_=fc_, bank=bank, KW=KW: e.matmul(ps[bank][:, 0:KW], lhsT=fc_[:, tb, :], rhs=cosl[:, tb, 0:KW], start=(tb == 0), stop=False),
                                     r=[fck, 'cosl'], w=[PSK[bank]])
                                S.op('pe', lambda e, tb=tb, fs_=fs_, bank=bank, KW=KW, ntb=ntb: e.matmul(ps[bank][:, 0:KW], lhsT=fs_[:, tb, :], rhs=sinl[:, tb, 0:KW], start=False, stop=(tb == ntb - 1)),
                                     r=[fsk, 'sinl'], w=[PSK[bank]])
                            S.op('act', lambda e, do_=do_, bank=bank, KW=KW: e.activation(out=do_[:, 0:KW], in_=ps[bank][:, 0:KW], func=AF.Copy), r=[PSK[bank]], w=[dok])
                            S.dma('sp', dT[mb * 128:(mb + 1) * 128, s0 + kt * KW:s0 + (kt + 1) * KW], do_[:, 0:KW], r=[dok], w=['dT'])
            S.drain()
            if cfg.get('STOP', 99) == 4:
                break
            with contextlib.ExitStack() as ph:
                def sbp(name, shape, dt):
                    return ph.enter_context(nc.sbuf_tensor("%s_u%d" % (name, nc.next_id()), list(shape), dt))
                wf = sbp("wf", [128, 8, 1024], BF16)
                S.dma('pool', wf[:], w_fft[l].rearrange("(c p) m -> p c m", p=128), w=['wf'])
                dTt = [sbp("dTt%d" % i, [128, 8, 512], BF16) for i in range(2)]
                gft = [sbp("gft%d" % i, [128, 8, 512], BF16) for i in range(2)]
                mfo = [sbp("mfo%d" % i, [128, 512], BF16) for i in range(2)]
                it = 0
                for ti in range(NTILE):
                    t0 = ti * TW0
                    dt_, gt_ = dTt[ti % 2], gft[ti % 2]
                    dtk, gtk = 'dTt%d' % (ti % 2), 'gft%d' % (ti % 2)
                    S.dma('sp', dt_[:], dT[:, t0:t0 + 512].rearrange("(c p) t -> p c t", p=128), r=['dT'], w=[dtk])
                    S.dma('sp', gt_[:], gfT[:, t0:t0 + 512].rearrange("(c p) t -> p c t", p=128), r=['gfT%d' % ti], w=[gtk])
                    for ob in range(8):
                        bank = it % 2
                        m_ = mfo[it % 2]
                        mk_ = 'mfo%d' % (it % 2)
                        it += 1
                        for mb in range(8):
                            S.op('pe', lambda e, mb=mb, ob=ob, dt_=dt_, bank=bank: e.matmul(ps[bank][:, :], lhsT=wf[:, mb, ob * 128:(ob + 1) * 128], rhs=dt_[:, mb, :], start=(mb == 0), stop=(mb == 7)),
                                 r=['wf', dtk], w=[PSK[bank]])
                        S.op('dve', lambda e, m_=m_, bank=bank, gt_=gt_, ob=ob: e.tensor_tensor(out=m_[:], in0=ps[bank][:, :], in1=gt_[:, ob, :], op=ALU.mult), r=[PSK[bank], gtk], w=[mk_])
                        S.dma('sp', mixT[3072 + ob * 128:3072 + (ob + 1) * 128, t0:t0 + 512], m_[:], r=[mk_], w=['mixT_f'])
            S.drain()
            if cfg.get('STOP', 99) == 5:
                break

            with contextlib.ExitStack() as ph:
                def sbp(name, shape, dt):
                    return ph.enter_context(nc.sbuf_tensor("%s_u%d" % (name, nc.next_id()), list(shape), dt))
                lq = sbp("lq", [128, 3, 64], F32)
                dtq = sbp("dtq", [128, 64], F32)
                r_q = sbp("r_q", [128, 64], F32)
                ang_q = sbp("ang_q", [128, 64], F32)
                stq = sbp("stq", [128, 2, 2, 32], F32)
                dsk = sbp("dsk", [128, 8], F32)
                fin = sbp("fin", [128, NPS, 2, 2, 32], F32)
                jv = sbp("jv", [128, 512], F32)
                S.dma('sp', lq[:], lam_q[l].rearrange("a p c -> p a c"), w=['lq'])
                S.dma('sp', stq[:], st_q[l], w=['stq'])
                S.dma('sp', dsk[:], dskip_p[l], w=['dsk'])
                S.dma('sp', jv[:], c_jv[:, :], w=['jv'])
                S.op('act', lambda e: e.activation(out=dtq[:], in_=lq[:, 2, :], func=AF.Exp), r=['lq'], w=['dtq'])
                S.op('dve', lambda e: e.tensor_tensor(out=r_q[:], in0=lq[:, 0, :], in1=dtq[:], op=ALU.mult), r=['lq', 'dtq'], w=['r_q'])
                S.op('act', lambda e: e.activation(out=r_q[:], in_=r_q[:], func=AF.Exp), r=['r_q'], w=['r_q'])
                S.op('dve', lambda e: e.tensor_tensor(out=ang_q[:], in0=lq[:, 1, :], in1=dtq[:], op=ALU.mult), r=['lq', 'dtq'], w=['ang_q'])

                lr_ = sbp("lr_", [128, 512], F32)
                li_ = sbp("li_", [128, 512], F32)
                ls_ = sbp("ls_", [128, 512], F32)
                z1 = sbp("z1", [128, 512], F32)
                z2 = sbp("z2", [128, 512], F32)
                z3 = sbp("z3", [128, 512], F32)
                z4 = sbp("z4", [128, 512], F32)
                zi = sbp("zi", [128, 512], I32)
                fre = sbp("fre", [128, 512], F32)
                fim = sbp("fim", [128, 512], F32)
                bre = sbp("bre", [128, 512], F32)
                bim = sbp("bim", [128, 512], F32)
                lBr = sbp("lBr", [128, 2, 512], BF16)
                lBi = sbp("lBi", [128, 2, 512], BF16)
                cTr = sbp("cTr", [128, 2, 4, 128], BF16)
                cTi = sbp("cTi", [128, 2, 4, 128], BF16)
                TC = sbp("TC", [128, 8, 512], F32)
                TS = sbp("TS", [128, 8, 512], F32)
                targ = sbp("targ", [128, 512], F32)
                ttf = sbp("ttf", [128, 512], F32)
                tti = sbp("tti", [128, 512], I32)
                uS = sbp("uS", [128, NT], BF16)
                ysb = sbp("ysb", [128, NT], F32)
                ybf = sbp("ybf", [128, NT], BF16)
                car = sbp("car", [128, 4, 2], F32)
                cw = sbp("cw", [128, 8], F32)
                W = {n: sbp("W" + n, [128, 512], F32) for n in ('a', 'b', 'c', 'd', 'wr', 'wi', 'gr', 'gi', 'e', 'f', 'g', 'h')}
                WB = dict(a=(lr_, 'lr_'), b=(li_, 'li_'), c=(ls_, 'ls_'), d=(z1, 'z1'), wr=(z2, 'z2'), wi=(z3, 'z3'), gr=(z4, 'z4'),
                          gi=(fre, 'fre'), e=(fim, 'fim'), f=(bre, 'bre'), g=(bim, 'bim'), h=(ttf, 'ttf'))
                WA = {n: (t, 'W' + n) for n, t in W.items()}
                hrb = [sbp("hrb%d" % i, [128, 512], BF16) for i in range(2)]
                hib = [sbp("hib%d" % i, [128, 512], BF16) for i in range(2)]

                for uc in range(8):
                    S.dma('pool', cTr[:], cT_pad[l, 0, uc], w=['cTr'])
                    S.dma('pool', cTi[:], cT_pad[l, 1, uc], w=['cTi'])
                    S.dma('sp', uS[:], uT[uc * 128:(uc + 1) * 128, :], r=['uT'], w=['uS'])
                    for dz in range(2):
                        S.dma('sp', lr_[:], lam_rep[l, 0][:, dz, uc * 512:(uc + 1) * 512], w=['lr_'])
                        S.dma('sp', li_[:], lam_rep[l, 1][:, dz, uc * 512:(uc + 1) * 512], w=['li_'])
                        S.dma('sp', ls_[:], lam_rep[l, 2][:, dz, uc * 512:(uc + 1) * 512], w=['ls_'])
                        S.dma('sp', bre[:], bT_pad[l, 0, uc][:, dz, :], w=['bre'])
                        S.dma('sp', bim[:], bT_pad[l, 1, uc][:, dz, :], w=['bim'])
                        fl = lambda t: t[:]
                        S.op('act', lambda e: e.activation(out=fl(z1), in_=fl(ls_), func=AF.Exp), r=['ls_'], w=['z1'])
                        S.op('dve', lambda e: e.tensor_tensor(out=fl(z2), in0=fl(lr_), in1=fl(z1), op=ALU.mult), r=['lr_', 'z1'], w=['z2'])
                        S.op('act', lambda e: e.activation(out=fl(z2), in_=fl(z2), func=AF.Exp), r=['z2'], w=['z2'])
                        S.op('dve', lambda e: e.tensor_tensor(out=fl(z1), in0=fl(li_), in1=fl(z1), op=ALU.mult), r=['li_', 'z1'], w=['z1'])
                        sin_of(fl(z3), fl(z1), fl(z4), fl(zi), 0.0, ['z1'], 'z3', 'z4', 'zi')
                        S.op('dve', lambda e: e.tensor_tensor(out=fl(z3), in0=fl(z3), in1=fl(z2), op=ALU.mult), r=['z3', 'z2'], w=['z3'])
                        sin_of(fl(fre), fl(z1), fl(z4), fl(zi), PI / 2, ['z1'], 'fre', 'z4', 'zi')
                        S.op('dve', lambda e: e.tensor_tensor(out=fl(z2), in0=fl(fre), in1=fl(z2), op=ALU.mult), r=['fre', 'z2'], w=['z2'])
                        S.op('dve', lambda e: e.tensor_scalar(out=fl(z2), in0=fl(z2), scalar1=-1.0, scalar2=None, op0=ALU.add), r=['z2'], w=['z2'])
                        S.op('dve', lambda e: e.tensor_tensor(out=fl(z1), in0=fl(lr_), in1=fl(lr_), op=ALU.mult), r=['lr_', 'z1'], w=['z1'])
                        S.op('dve', lambda e: e.tensor_tensor(out=fl(z4), in0=fl(li_), in1=fl(li_), op=ALU.mult), r=['li_'], w=['z4'])
                        S.op('dve', lambda e: e.tensor_tensor(out=fl(z1), in0=fl(z1), in1=fl(z4), op=ALU.add), r=['z1', 'z4'], w=['z1'])
                        S.op('dve', lambda e: e.reciprocal(out=fl(z1), in_=fl(z1)), r=['z1'], w=['z1'])
                        S.op('dve', lambda e: e.tensor_tensor(out=fl(fre), in0=fl(z2), in1=fl(lr_), op=ALU.mult), r=['z2', 'lr_'], w=['fre'])
                        S.op('dve', lambda e: e.tensor_tensor(out=fl(z4), in0=fl(z3), in1=fl(li_), op=ALU.mult), r=['z3', 'li_'], w=['z4'])
                        S.op('dve', lambda e: e.tensor_tensor(out=fl(fre), in0=fl(fre), in1=fl(z4), op=ALU.add), r=['fre', 'z4'], w=['fre'])
                        S.op('dve', lambda e: e.tensor_tensor(out=fl(fre), in0=fl(fre), in1=fl(z1), op=ALU.mult), r=['fre', 'z1'], w=['fre'])
                        S.op('dve', lambda e: e.tensor_tensor(out=fl(fim), in0=fl(z3), in1=fl(lr_), op=ALU.mult), r=['z3', 'lr_'], w=['fim'])
                        S.op('dve', lambda e: e.tensor_tensor(out=fl(z4), in0=fl(z2), in1=fl(li_), op=ALU.mult), r=['z2', 'li_'], w=['z4'])
                        S.op('dve', lambda e: e.tensor_tensor(out=fl(fim), in0=fl(fim), in1=fl(z4), op=ALU.subtract), r=['fim', 'z4'], w=['fim'])
                        S.op('dve', lambda e: e.tensor_tensor(out=fl(fim), in0=fl(fim), in1=fl(z1), op=ALU.mult), r=['fim', 'z1'], w=['fim'])
                        S.op('dve', lambda e: e.tensor_tensor(out=fl(z1), in0=fl(fre), in1=fl(bre), op=ALU.mult), r=['fre', 'bre'], w=['z1'])
                        S.op('dve', lambda e: e.tensor_tensor(out=fl(z2), in0=fl(fim), in1=fl(bim), op=ALU.mult), r=['fim', 'bim'], w=['z2'])
                        S.op('dve', lambda e, dz=dz: e.tensor_tensor(out=lBr[:, dz, :], in0=fl(z1), in1=fl(z2), op=ALU.subtract), r=['z1', 'z2'], w=['lBr'])
                        S.op('dve', lambda e: e.tensor_tensor(out=fl(z1), in0=fl(fre), in1=fl(bim), op=ALU.mult), r=['fre', 'bim'], w=['z1'])
                        S.op('dve', lambda e: e.tensor_tensor(out=fl(z2), in0=fl(fim), in1=fl(bre), op=ALU.mult), r=['fim', 'bre'], w=['z2'])
                        S.op('dve', lambda e, dz=dz: e.tensor_tensor(out=lBi[:, dz, :], in0=fl(z1), in1=fl(z2), op=ALU.add), r=['z1', 'z2'], w=['lBi'])
                    for d in range(2):
                        for k in range(4):
                            dk = d * 4 + k
                            col = d * 32 + uc * 4 + k
                            S.op('dve', lambda e, col=col: e.tensor_scalar(out=targ[:], in0=jv[:], scalar1=ang_q[:, col:col + 1], scalar2=None, op0=ALU.mult),
                                 r=['jv', 'ang_q'], w=['targ'])
                            sin_of(TS[:, dk, :], targ[:], ttf[:], tti[:], 0.0, ['targ'], 'TS%d' % dk, 'ttf', 'tti')
                            sin_of(TC[:, dk, :], targ[:], ttf[:], tti[:], PI / 2, ['targ'], 'TC%d' % dk, 'ttf', 'tti')
                    hi_ = 0
                    yb_ = 0
                    for si_, (s0, L, samp) in enumerate(seqs):
                        TW = min(512, L)
                        ntl = L // TW
                        for d in range(2):
                            order = range(ntl) if d == 0 else range(ntl - 1, -1, -1)
                            for k in range(4):
                                st = uc * 4 + k
                                if samp:
                                    S.op('pool', lambda e, k=k, d=d, st=st: e.tensor_copy(out=car[:, k, :], in_=stq[:, d, :, st]), r=['stq'], w=['car%d' % k])
                                else:
                                    S.op('pool', lambda e, k=k: e.memset(car[:, k, :], 0.0), w=['car%d' % k])
                            for tl in order:
                                c0 = s0 + tl * TW
                                ybank = 6 + (yb_ % 2)
                                yb_ += 1
                                for k in range(4):
                                    dk = d * 4 + k
                                    col = d * 32 + uc * 4 + k
                                    br, bi = (0, 1) if k % 2 == 0 else (2, 3)
                                    S.op('pe', lambda e, d=d, k=k, c0=c0, TW=TW, br=br: e.matmul(ps[br][:, 0:TW], lhsT=lBr[:, d, k * 128:(k + 1) * 128], rhs=uS[:, c0:c0 + TW], start=True, stop=True),
                                         r=['lBr', 'uS'], w=[PSK[br]])
                                    S.op('pe', lambda e, d=d, k=k, c0=c0, TW=TW, bi=bi: e.matmul(ps[bi][:, 0:TW], lhsT=lBi[:, d, k * 128:(k + 1) * 128], rhs=uS[:, c0:c0 + TW], start=True, stop=True),
                                         r=['lBi', 'uS'], w=[PSK[bi]])
                                    if d == 0:
                                        pr, pi2 = ps[br][:, 0:TW], ps[bi][:, 0:TW]
                                    else:
                                        pr, pi2 = ps[br][:, 0:TW][:, ::-1], ps[bi][:, 0:TW][:, ::-1]
                                    tc_, ts_ = TC[:, dk, 0:TW], TS[:, dk, 0:TW]
                                    tck, tsk = 'TC%d' % dk, 'TS%d' % dk
                                    WS = WA if k % 2 == 0 else WB
                                    Wv = {n: t[:, 0:TW] for n, (t, _) in WS.items()}
                                    Wk = {n: kk for n, (_, kk) in WS.items()}
                                    Wt = {n: t for n, (t, _) in WS.items()}
                                    S.op('dve', lambda e, pr=pr, tc_=tc_, Wv=Wv: e.tensor_tensor(out=Wv['a'], in0=pr, in1=tc_, op=ALU.mult), r=[PSK[br], tck], w=[Wk['a']])
                                    S.op('dve', lambda e, pi2=pi2, ts_=ts_, Wv=Wv: e.tensor_tensor(out=Wv['b'], in0=pi2, in1=ts_, op=ALU.mult), r=[PSK[bi], tsk], w=[Wk['b']])
                                    S.op('pool', lambda e, Wv=Wv: e.tensor_tensor(out=Wv['wr'], in0=Wv['a'], in1=Wv['b'], op=ALU.add), r=[Wk['a'], Wk['b']], w=[Wk['wr']])
                                    S.op('dve', lambda e, pi2=pi2, tc_=tc_, Wv=Wv: e.tensor_tensor(out=Wv['c'], in0=pi2, in1=tc_, op=ALU.mult), r=[PSK[bi], tck], w=[Wk['c']])
                                    S.op('dve', lambda e, pr=pr, ts_=ts_, Wv=Wv: e.tensor_tensor(out=Wv['d'], in0=pr, in1=ts_, op=ALU.mult), r=[PSK[br], tsk], w=[Wk['d']])
                                    S.op('pool', lambda e, Wv=Wv: e.tensor_tensor(out=Wv['wi'], in0=Wv['c'], in1=Wv['d'], op=ALU.subtract), r=[Wk['c'], Wk['d']], w=[Wk['wi']])
                                    rb = r_q[:, col:col + 1].to_broadcast([128, TW])
                                    S.op('dve', lambda e, Wv=Wv, rb=rb, k=k: e.tensor_tensor_scan(out=Wv['gr'], data0=rb, data1=Wv['wr'], initial=car[:, k, 0:1], op0=ALU.mult, op1=ALU.add),
                                         r=[Wk['wr'], 'r_q', 'car%d' % k], w=[Wk['gr']])
                                    S.op('dve', lambda e, Wv=Wv, rb=rb, k=k: e.tensor_tensor_scan(out=Wv['gi'], data0=rb, data1=Wv['wi'], initial=car[:, k, 1:2], op0=ALU.mult, op1=ALU.add),
                                         r=[Wk['wi'], 'r_q', 'car%d' % k], w=[Wk['gi']])
                                    gl_r, gl_i = Wt['gr'][:, TW - 1:TW], Wt['gi'][:, TW - 1:TW]
                                    cl, sl_ = TC[:, dk, TW - 1:TW], TS[:, dk, TW - 1:TW]
                                    S.op('pool', lambda e, gl_r=gl_r, cl=cl: e.tensor_tensor(out=cw[:, 0:1], in0=gl_r, in1=cl, op=ALU.mult), r=[Wk['gr'], tck], w=['cw0'])
                                    S.op('pool', lambda e, gl_i=gl_i, sl_=sl_: e.tensor_tensor(out=cw[:, 1:2], in0=gl_i, in1=sl_, op=ALU.mult), r=[Wk['gi'], tsk], w=['cw1'])
                                    S.op('pool', lambda e, gl_r=gl_r, sl_=sl_: e.tensor_tensor(out=cw[:, 2:3], in0=gl_r, in1=sl_, op=ALU.mult), r=[Wk['gr'], tsk], w=['cw2'])
                                    S.op('pool', lambda e, gl_i=gl_i, cl=cl: e.tensor_tensor(out=cw[:, 3:4], in0=gl_i, in1=cl, op=ALU.mult), r=[Wk['gi'], tck], w=['cw3'])
                                    S.op('pool', lambda e, k=k: e.tensor_tensor(out=car[:, k, 0:1], in0=cw[:, 0:1], in1=cw[:, 1:2], op=ALU.subtract), r=['cw0', 'cw1'], w=['car%d' % k])
                                    S.op('pool', lambda e, k=k: e.tensor_tensor(out=car[:, k, 1:2], in0=cw[:, 2:3], in1=cw[:, 3:4], op=ALU.add), r=['cw2', 'cw3', 'car%d' % k], w=['car%d' % k])
                                    hr_, hn_ = hrb[hi_ % 2], hib[hi_ % 2]
                                    hrk, hnk = 'hrb%d' % (hi_ % 2), 'hib%d' % (hi_ % 2)
                                    if d == 0:
                                        hro, hno = hr_[:, 0:TW], hn_[:, 0:TW]
                                    else:
                                        hro, hno = hr_[:, 0:TW][:, ::-1], hn_[:, 0:TW][:, ::-1]
                                    S.op('pool', lambda e, Wv=Wv, tc_=tc_: e.tensor_tensor(out=Wv['e'], in0=Wv['gr'], in1=tc_, op=ALU.mult), r=[Wk['gr'], tck], w=[Wk['e']])
                                    S.op('pool', lambda e, Wv=Wv, ts_=ts_: e.tensor_tensor(out=Wv['f'], in0=Wv['gi'], in1=ts_, op=ALU.mult), r=[Wk['gi'], tsk], w=[Wk['f']])
                                    S.op('pool', lambda e, Wv=Wv, hro=hro: e.tensor_tensor(out=hro, in0=Wv['e'], in1=Wv['f'], op=ALU.subtract), r=[Wk['e'], Wk['f']], w=[hrk])
                                    S.op('dve', lambda e, Wv=Wv, ts_=ts_: e.tensor_tensor(out=Wv['g'], in0=Wv['gr'], in1=ts_, op=ALU.mult), r=[Wk['gr'], tsk], w=[Wk['g']])
                                    S.op('dve', lambda e, Wv=Wv, tc_=tc_: e.tensor_tensor(out=Wv['h'], in0=Wv['gi'], in1=tc_, op=ALU.mult), r=[Wk['gi'], tck], w=[Wk['h']])
                                    S.op('dve', lambda e, Wv=Wv, hno=hno: e.scalar_tensor_tensor(out=hno, in0=Wv['g'], scalar=-1.0, in1=Wv['h'], op0=ALU.mult, op1=ALU.subtract),
                                         r=[Wk['g'], Wk['h']], w=[hnk])
                                    S.op('pe', lambda e, d=d, k=k, hr_=hr_, TW=TW, ybank=ybank: e.matmul(ps[ybank][:, 0:TW], lhsT=cTr[:, d, k, :], rhs=hr_[:, 0:TW], start=(k == 0), stop=False),
                                         r=['cTr', hrk], w=[PSK[ybank]])
                                    S.op('pe', lambda e, d=d, k=k, hn_=hn_, TW=TW, ybank=ybank: e.matmul(ps[ybank][:, 0:TW], lhsT=cTi[:, d, k, :], rhs=hn_[:, 0:TW], start=False, stop=(k == 3)),
                                         r=['cTi', hnk], w=[PSK[ybank]])
                                    hi_ += 1
                                if d == 0:
                                    S.op('dve', lambda e, c0=c0, TW=TW, ybank=ybank, uc=uc: e.scalar_tensor_tensor(out=ysb[:, c0:c0 + TW], in0=uS[:, c0:c0 + TW], scalar=dsk[:, uc:uc + 1],
                                                                                                               in1=ps[ybank][:, 0:TW], op0=ALU.mult, op1=ALU.add),
                                         r=['uS', 'dsk', PSK[ybank]], w=['ysb'])
                                else:
                                    S.op('dve', lambda e, c0=c0, TW=TW, ybank=ybank: e.tensor_tensor(out=ybf[:, c0:c0 + TW], in0=ysb[:, c0:c0 + TW], in1=ps[ybank][:, 0:TW], op=ALU.add),
                                         r=['ysb', PSK[ybank]], w=['ybf'])
                            if not samp:
                                for k in range(4):
                                    st = uc * 4 + k
                                    S.op('pool', lambda e, k=k, d=d, st=st, si_=si_: e.tensor_copy(out=fin[:, si_, d, :, st], in_=car[:, k, :]), r=['car%d' % k], w=['fin'])
                    S.dma('sp', yT[uc * 128:(uc + 1) * 128, :], ybf[:], r=['ybf'], w=['yT'])
                S.dma('sp', fin_out[l], fin[:].rearrange("p a b c d -> p (a b c d)"), r=['fin'], w=['fin_out'])
            S.drain()
            if cfg.get('STOP', 99) == 6:
                break

            with contextlib.ExitStack() as ph:
                def sbp(name, shape, dt):
                    return ph.enter_context(nc.sbuf_tensor("%s_u%d" % (name, nc.next_id()), list(shape), dt))
                wg = sbp("wg", [128, 8, 2048], BF16)
                S.dma('pool', wg[:], w_glu[l].rearrange("(c p) m -> p c m", p=128), w=['wg'])
                yTt = [sbp("yTt%d" % i, [128, 8, 512], BF16) for i in range(2)]
                gst = [sbp("gst%d" % i, [128, 8, 512], BF16) for i in range(2)]
                sg = sbp("sg", [128, 512], F32)
                tg = sbp("tg", [128, 512], F32)
                mgo = [sbp("mgo%d" % i, [128, 512], BF16) for i in range(2)]
                it = 0
                for ti in range(NTILE):
                    t0 = ti * TW0
                    yt_, gt_ = yTt[ti % 2], gst[ti % 2]
                    ytk, gtk = 'yTt%d' % (ti % 2), 'gst%d' % (ti % 2)
                    S.dma('sp', yt_[:], yT[:, t0:t0 + 512].rearrange("(c p) t -> p c t", p=128), r=['yT'], w=[ytk])
                    S.dma('sp', gt_[:], gsT[:, t0:t0 + 512].rearrange("(c p) t -> p c t", p=128), r=['gsT%d' % ti], w=[gtk])
                    for ob in range(8):
                        m_ = mgo[it % 2]
                        mk_ = 'mgo%d' % (it % 2)
                        ba, bg = (0, 1) if it % 2 == 0 else (2, 3)
                        it += 1
                        for uc in range(8):
                            S.op('pe', lambda e, uc=uc, ob=ob, yt_=yt_, ba=ba: e.matmul(ps[ba][:, :], lhsT=wg[:, uc, ob * 128:(ob + 1) * 128], rhs=yt_[:, uc, :], start=(uc == 0), stop=(uc == 7)),
                                 r=['wg', ytk], w=[PSK[ba]])
                        for uc in range(8):
                            S.op('pe', lambda e, uc=uc, ob=ob, yt_=yt_, bg=bg: e.matmul(ps[bg][:, :], lhsT=wg[:, uc, 1024 + ob * 128:1024 + (ob + 1) * 128], rhs=yt_[:, uc, :], start=(uc == 0), stop=(uc == 7)),
                                 r=['wg', ytk], w=[PSK[bg]])
                        S.op('act', lambda e, bg=bg: e.activation(out=sg[:], in_=ps[bg][:, :], func=AF.Sigmoid), r=[PSK[bg]], w=['sg'])
                        S.op('dve', lambda e, ba=ba: e.tensor_tensor(out=tg[:], in0=ps[ba][:, :], in1=sg[:], op=ALU.mult), r=[PSK[ba], 'sg'], w=['tg'])
                        S.op('pool', lambda e, m_=m_, gt_=gt_, ob=ob: e.tensor_tensor(out=m_[:], in0=tg[:], in1=gt_[:, ob, :], op=ALU.mult), r=['tg', gtk], w=[mk_])
                        S.dma('sp', mixT[2048 + ob * 128:2048 + (ob + 1) * 128, t0:t0 + 512], m_[:], r=[mk_], w=['mixT_s'])
            S.drain()
            if cfg.get('STOP', 99) == 7:
                break

            with contextlib.ExitStack() as ph:
                def sbp(name, shape, dt):
                    return ph.enter_context(nc.sbuf_tensor("%s_u%d" % (name, nc.next_id()), list(shape), dt))
                slab[0] = sbp("slabA", [128, 32, 512], BF16)
                slab[1] = sbp("slabB", [128, 32, 512], BF16)
                mt = sbp("mt", [128, 32, 512], BF16)
                gbc = sbp("gbc", [128, 2, D], F32)
                S.dma('sp', gbc[:], gate_dr.rearrange("v p f -> p v f"), r=['gate_dr'], w=['gbc'])
                xo = [sbp("xo%d" % i, [128, 512], F32) for i in range(4)]
                xn_ = [sbp("xn%d" % i, [128, 512], F32) for i in range(4)]
                it = 0
                for ti in range(NTILE):
                    t0 = ti * TW0
                    v = 0 if ti < NPT else 1
                    S.dma('sp', mt[:], mixT[:, t0:t0 + 512].rearrange("(c p) t -> p c t", p=128), r=['mixT_a', 'mixT_f', 'mixT_s'], w=['mt'])
                    for so in range(8):
                        sl, sk = load_slab(w_out[l][:, so * 512:(so + 1) * 512])
                        for sub in range(4):
                            bank = it % 4
                            x_, xk = xo[it % 4], 'xo%d' % (it % 4)
                            n_, nk_ = xn_[it % 4], 'xn%d' % (it % 4)
                            it += 1
                            rows = slice(t0 + sub * 128, t0 + (sub + 1) * 128)
                            S.dma('sp', x_[:], xsrc[rows, so * 512:(so + 1) * 512], r=['xres_w'], w=[xk])
                            for c in range(32):
                                S.op('pe', lambda e, c=c, sub=sub, sl=sl, bank=bank: e.matmul(ps[bank][:, :], lhsT=mt[:, c, sub * 128:(sub + 1) * 128], rhs=sl[:, c, :], start=(c == 0), stop=(c == 31)),
                                     r=[sk, 'mt'], w=[PSK[bank]])
                            S.op('dve', lambda e, n_=n_, bank=bank, v=v, so=so: e.tensor_tensor(out=n_[:], in0=ps[bank][:, :], in1=gbc[:, v, so * 512:(so + 1) * 512], op=ALU.mult),
                                 r=[PSK[bank], 'gbc'], w=[nk_])
                            S.op('pool', lambda e, n_=n_, x_=x_: e.tensor_tensor(out=n_[:], in0=n_[:], in1=x_[:], op=ALU.add), r=[nk_, xk], w=[nk_])
                            S.dma('sp', xres[rows, so * 512:(so + 1) * 512], n_[:], r=[nk_], w=['xres_w'])
            S.drain()
            if cfg.get('STOP', 99) == 8:
                break

        with contextlib.ExitStack() as ph:
          if cfg.get('STOP', 99) == 99:
            fg = ph.enter_context(nc.sbuf_tensor("fg", [128, D], F32))
            xf = [ph.enter_context(nc.sbuf_tensor("xf%d" % i, [128, D], F32)) for i in range(2)]
            yo = [ph.enter_context(nc.sbuf_tensor("yo%d" % i, [128, D], F32)) for i in range(2)]
            junk2 = ph.enter_context(nc.sbuf_tensor("junk2", [128, D], BF16))
            ss2 = ph.enter_context(nc.sbuf_tensor("ss2", [128, 2], F32))
            S.dma('sp', fg[:], fng_rep[:, :], w=['fg'])
            for b in range(NT // 128):
                x_, xk = xf[b % 2], 'xf%d' % (b % 2)
                y_, yk = yo[b % 2], 'yo%d' % (b % 2)
                sk_ = 'ss2_%d' % (b % 2)
                S.dma('sp', x_[:], xres[b * 128:(b + 1) * 128, :], r=['xres_w'], w=[xk])
                S.op('act', lambda e, x_=x_, b=b: e.activation(out=junk2[:], in_=x_[:], func=AF.Square, accum_out=ss2[:, b % 2:b % 2 + 1]), r=[xk], w=['junk2', sk_])
                rstd_from(ss2[:, b % 2:b % 2 + 1], ss2[:, b % 2:b % 2 + 1], 1.0 / D, [sk_], sk_)
                S.op('dve', lambda e, x_=x_, y_=y_, b=b: e.scalar_tensor_tensor(out=y_[:], in0=x_[:], scalar=ss2[:, b % 2:b % 2 + 1], in1=fg[:], op0=ALU.mult, op1=ALU.mult),
                     r=[xk, sk_, 'fg'], w=[yk])
                S.dma('sp', y_out[b * 128:(b + 1) * 128, :], y_[:], r=[yk], w=['y_out'])
        S.finish()
    return nc


def _consts(cfg):
    LP, LS = cfg['LP'], cfg['LS']
    c = {}
    c['c_ident'] = np.eye(128, dtype=np.float32)
    R = np.zeros((128, 128), np.float32)
    for base in (0, 64):
        for i in range(32):
            R[base + i, base + 32 + i] = -1.0
            R[base + 32 + i, base + i] = 1.0
    c['c_RT'] = np.ascontiguousarray(R.T)
    t = np.arange(LS)
    inv = 10000.0 ** (-np.arange(32, dtype=np.float32) / 32).astype(np.float32)
    row = (t // 64).astype(np.float32)[:, None] * inv[None, :]
    col = (t % 64).astype(np.float32)[:, None] * inv[None, :]
    ang = np.concatenate([row, row, col, col], axis=1).T.astype(np.float32)
    c['c_rope'] = np.stack([np.cos(ang), np.sin(ang)]).astype(np.float32)
    c['c_jv'] = np.tile(np.arange(1, 513, dtype=np.float32)[None, :], (128, 1))
    cc = np.arange(256)
    a = 2 * np.pi * ((cc[:, None] * cc[None, :]) % 256) / 256.0
    cs = np.concatenate([np.cos(a), -np.sin(a)], axis=1) / 16.0
    c['c_cs256'] = np.ascontiguousarray(cs.reshape(2, 128, 512).transpose(1, 0, 2)).astype(np.float32)

    def dft(L):
        tt = np.arange(L, dtype=np.int64)
        a = 2 * np.pi * ((tt[:, None] * tt[None, :]) % L) / float(L)
        return (np.stack([np.cos(a), np.sin(a)]) / math.sqrt(L)).astype(ml_dtypes.bfloat16)
    c['c_dftp'] = dft(LP)
    c['c_dfts'] = dft(LS)
    return c


def _prep_core(cfg, core, x_prompt, x_sample, cache_k, cache_v, sf_re, sf_im, sb_re, sb_im, c, shared):
    NPS, LP, LS, PAST, DEPTH = cfg['NPS'], cfg['LP'], cfg['LS'], cfg['PAST'], cfg['DEPTH']
    ncore_per_b = cfg.get('CPB', 4)
    b = core // ncore_per_b
    m = dict(shared)
    xp = x_prompt[core * NPS:(core + 1) * NPS].reshape(NPS * LP, D)
    m['x_in'] = np.concatenate([xp, x_sample[b]], axis=0)
    cvs = np.stack([shared['_c_ctx'], c[b]], axis=-1)
    m['cvec'] = np.ascontiguousarray(cvs.reshape(32, 128, 2).transpose(1, 0, 2))
    m['cache_kT'] = np.ascontiguousarray(cache_k[b].transpose(0, 2, 3, 1))
    m['cache_v'] = np.ascontiguousarray(cache_v[b].reshape(DEPTH, PAST, NKV * HD))
    st = np.stack([np.stack([sf_re[b], sf_im[b]], 1), np.stack([sb_re[b], sb_im[b]], 1)], 1)
    st = st.reshape(DEPTH, 2, 2, 32, 128)
    m['st_q'] = np.ascontiguousarray(st.transpose(0, 4, 1, 2, 3))
    del m['_c_ctx']
    return m


def _prep_shared(cfg, c_ctx, norm_g, w_mod, b_mod, w_in, q_norm, k_norm, lam_re, lam_im, log_step,
                 b_re, b_im, c_re, c_im, d_skip, w_glu, w_fft, w_out, final_norm_g):
    DEPTH = cfg['DEPTH']
    f = np.float32
    m = dict(_consts(cfg))
    m['_c_ctx'] = c_ctx
    m['normg_p'] = np.ascontiguousarray(norm_g.reshape(DEPTH, 32, 128).transpose(0, 2, 1))
    m['w_mod'] = w_mod
    m['bmod_ps'] = np.ascontiguousarray(b_mod[:, :2 * D].reshape(DEPTH, 64, 128).transpose(0, 2, 1))
    m['bmodg_rep'] = np.ascontiguousarray(np.broadcast_to(b_mod[:, None, 2 * D:], (DEPTH, 128, D)))
    m['w_in'] = w_in
    m['qk_g'] = np.ascontiguousarray(np.stack([q_norm, k_norm], axis=-1))
    ls_full = np.broadcast_to(log_step[..., None], lam_re.shape)
    lam3 = np.stack([lam_re, lam_im, ls_full], axis=1).reshape(DEPTH, 3, 2, 32, 128)
    m['lam_q'] = np.ascontiguousarray(lam3.transpose(0, 1, 4, 2, 3).reshape(DEPTH, 3, 128, 64))
    lam_flat = np.stack([lam_re, lam_im, ls_full], axis=1).reshape(DEPTH, 3, 1, 2, 4096)
    m['lam_rep'] = np.ascontiguousarray(np.broadcast_to(lam_flat, (DEPTH, 3, 128, 2, 4096)))
    bT = np.zeros((DEPTH, 2, 8, 128, 2, 512), f)
    cT = np.zeros((DEPTH, 2, 8, 128, 2, 4, 128), f)
    for ri, (bb, ccm) in enumerate(((b_re, c_re), (b_im, c_im))):
        for uc in range(8):
            for gl in range(8):
                g = uc * 8 + gl
                bT[:, ri, uc, gl * 16:(gl + 1) * 16, :, gl * 64:(gl + 1) * 64] = bb[:, :, g].transpose(0, 3, 1, 2)
                k, qo = gl // 2, (gl % 2) * 64
                cT[:, ri, uc, qo:qo + 64, :, k, gl * 16:(gl + 1) * 16] = ccm[:, :, g].transpose(0, 3, 1, 2)
    m['bT_pad'] = bT
    m['cT_pad'] = cT
    m['dskip_p'] = np.ascontiguousarray(d_skip.reshape(DEPTH, 8, 128).transpose(0, 2, 1))
    m['w_glu'] = w_glu
    m['w_fft'] = w_fft
    m['w_out'] = w_out
    m['fng_rep'] = np.ascontiguousarray(np.broadcast_to(final_norm_g[None, :], (128, D)))
    return m


_NC_CACHE = {}


def run(cfg, n_cores, inputs):
    g = {k: np.asarray(v) for k, v in inputs.items()}
    key = tuple(sorted(cfg.items()))
    if key not in _NC_CACHE:
        _NC_CACHE[key] = build(cfg)
    nc = _NC_CACHE[key]
    shared = _prep_shared(cfg, g['c_ctx'], g['norm_g'], g['w_mod'], g['b_mod'], g['w_in'], g['q_norm'], g['k_norm'],
                          g['lam_re'], g['lam_im'], g['log_step'], g['b_re'], g['b_im'], g['c_re'], g['c_im'],
                          g['d_skip'], g['w_glu'], g['w_fft'], g['w_out'], g['final_norm_g'])
    in_maps = [_prep_core(cfg, core, g['x_prompt'], g['x_sample'], g['cache_k'], g['cache_v'],
                          g['state_fwd_re'], g['state_fwd_im'], g['state_bwd_re'], g['state_bwd_im'], g['c'], shared)
               for core in range(n_cores)]
    res = run_bass_kernel_spmd(nc, in_maps, core_ids=list(range(n_cores)))
    R = res.results
    DEPTH, NPS, LP, LS = cfg['DEPTH'], cfg['NPS'], cfg['LP'], cfg['LS']
    NTP = NPS * LP
    cpb = cfg.get('CPB', 4)
    y_prompt = np.concatenate([R[c_]['y_out'][:NTP].reshape(NPS, LP, D) for c_ in range(n_cores)], axis=0)
    y_sample = np.stack([R[c_]['y_out'][NTP:] for c_ in range(0, n_cores, cpb)], axis=0)
    new_k = np.concatenate([R[c_]['newk_out'].reshape(NPS, DEPTH, LP, NKV, HD) for c_ in range(n_cores)], axis=0)
    new_v = np.concatenate([R[c_]['newv_out'].reshape(NPS, DEPTH, LP, NKV, HD) for c_ in range(n_cores)], axis=0)
    fins = []
    for c_ in range(n_cores):
        fo = R[c_]['fin_out'].reshape(DEPTH, 128, NPS, 2, 2, 32)
        fins.append(fo.transpose(2, 0, 3, 4, 5, 1).reshape(NPS, DEPTH, 2, 2, 64, 64))
    fo = np.concatenate(fins, axis=0)
    outs = (y_prompt, y_sample, new_k, new_v, fo[:, :, 0, 0], fo[:, :, 0, 1], fo[:, :, 1, 0], fo[:, :, 1, 1])
    return tuple(np.ascontiguousarray(o, dtype=np.float32) for o in outs)


def kernel(**inputs):
    return run(CFG_FULL, 8, inputs)
```

```python
import math
import numpy as np
import ml_dtypes
import concourse.bass as bass
import concourse.mybir as mybir
from concourse.bass_utils import run_bass_kernel_spmd

F32 = mybir.dt.float32
BF16 = mybir.dt.bfloat16
I32 = mybir.dt.int32
ALU = mybir.AluOpType
AF = mybir.ActivationFunctionType

D = 4096
HD = 128
NH = 16
NKV = 4
INW = 9216
EPS = 1e-6
PI = math.pi
TW0 = 512

CFG_FULL = dict(DEPTH=4, NPS=4, LP=256, LS=4096, PAST=512)


class Sched:
    def __init__(self, nc, sems, dma_sems):
        self.nc = nc
        self.eng = dict(pe=nc.tensor, dve=nc.vector, act=nc.scalar, pool=nc.gpsimd, sp=nc.sync)
        self.sem = sems
        self.cnt = {e: 0 for e in sems}
        self.dsem = dma_sems
        self.dcnt = {q: [0] * len(v) for q, v in dma_sems.items()}
        self.dnext = {q: 0 for q in dma_sems}
        self.seen = {e: {} for e in self.eng}
        self.lastw = {}
        self.readers = {}
        self.semobj = {}

    def _need(self, e, r, w):
        need = {}
        def add(tok):
            if tok is None:
                return
            sid, val = tok
            if need.get(sid, 0) < val:
                need[sid] = val
        for k in r:
            add(self.lastw.get(k))
        for k in w:
            add(self.lastw.get(k))
            for t in self.readers.get(k, {}).items():
                add(t)
        eng = self.eng[e]
        for sid, val in need.items():
            if e == 'pe' and sid == 'E_pe':
                continue
            if self.seen[e].get(sid, 0) < val:
                eng.wait_ge(self.semobj[sid], val)
                self.seen[e][sid] = val

    def _record(self, tok, r, w):
        for k in w:
            self.lastw[k] = tok
            self.readers[k] = {}
        for k in r:
            d = self.readers.setdefault(k, {})
            if d.get(tok[0], 0) < tok[1]:
                d[tok[0]] = tok[1]

    def op(self, e, fn, r=(), w=()):
        self._need(e, r, w)
        inst = fn(self.eng[e])
        sid = 'E_' + e
        self.semobj[sid] = self.sem[e]
        self.cnt[e] += 1
        inst.then_inc(self.sem[e], 1)
        self._record((sid, self.cnt[e]), r, w)

    def dma(self, q, out, in_, r=(), w=(), slow=False):
        self._need(q, r, w)
        i = self.dnext[q]
        self.dnext[q] = (i + 1) % len(self.dsem[q])
        sid = 'D_%s_%d' % (q, i)
        self.semobj[sid] = self.dsem[q][i]
        if self.dcnt[q][i] > 0 and self.seen[q].get(sid, 0) < self.dcnt[q][i]:
            self.eng[q].wait_ge(self.dsem[q][i], self.dcnt[q][i])
            self.seen[q][sid] = self.dcnt[q][i]
        self.dcnt[q][i] += 16
        if slow:
            inst = self.eng[q].dma_start(out=out, in_=in_, allow_slow_non_contiguous=True)
        else:
            inst = self.eng[q].dma_start(out=out, in_=in_)
        inst.then_inc(self.dsem[q][i], 16)
        self._record((sid, self.dcnt[q][i]), r, w)

    def drain(self):
        self.finish()
        self.nc.all_engine_barrier()

    def finish(self):
        sp = self.eng['sp']
        for q, lst in self.dsem.items():
            for i, s in enumerate(lst):
                if self.dcnt[q][i] > 0:
                    sp.wait_ge(s, self.dcnt[q][i])
        for e, s in self.sem.items():
            if self.cnt[e] > 0:
                sp.wait_ge(s, self.cnt[e])


def build(cfg):
    DEPTH, NPS, LP, LS, PAST = cfg['DEPTH'], cfg['NPS'], cfg['LP'], cfg['LS'], cfg['PAST']
    NTP = NPS * LP
    NT = NTP + LS
    NKEY = NT + PAST
    assert NTP % TW0 == 0 and LS % TW0 == 0
    NTILE = NT // TW0
    NPT = NTP // TW0
    seqs = [(i * LP, LP, False) for i in range(NPS)] + [(NTP, LS, True)]

    nc = bass.Bass("TRN2", target_bir_lowering=False)

    def din(name, shape, dt=F32):
        return nc.dram_tensor(name, list(shape), dt, kind="ExternalInput").ap()

    def dout(name, shape, dt=F32):
        return nc.dram_tensor(name, list(shape), dt, kind="ExternalOutput").ap()

    def dscr(name, shape, dt=BF16):
        return nc.dram_tensor(name, list(shape), dt, kind="Internal").ap()

    x_in = din("x_in", [NT, D])
    cvec = din("cvec", [128, 32, 2])
    cache_kT = din("cache_kT", [DEPTH, NKV, 128, PAST])
    cache_v = din("cache_v", [DEPTH, PAST, NKV * HD])
    st_q = din("st_q", [DEPTH, 128, 2, 2, 32])
    normg_p = din("normg_p", [DEPTH, 128, 32])
    w_mod = din("w_mod", [DEPTH, D, 3 * D])
    bmod_ps = din("bmod_ps", [DEPTH, 128, 64])
    bmodg_rep = din("bmodg_rep", [DEPTH, 128, D])
    w_in = din("w_in", [DEPTH, D, INW])
    qk_g = din("qk_g", [DEPTH, 128, 2])
    lam_q = din("lam_q", [DEPTH, 3, 128, 64])
    lam_rep = din("lam_rep", [DEPTH, 3, 128, 2, 4096])
    bT_pad = din("bT_pad", [DEPTH, 2, 8, 128, 2, 512])
    cT_pad = din("cT_pad", [DEPTH, 2, 8, 128, 2, 4, 128])
    dskip_p = din("dskip_p", [DEPTH, 128, 8])
    w_glu = din("w_glu", [DEPTH, 1024, 2048])
    w_fft = din("w_fft", [DEPTH, 1024, 1024])
    w_out = din("w_out", [DEPTH, D, D])
    fng_rep = din("fng_rep", [128, D])
    c_ident = din("c_ident", [128, 128])
    c_RT = din("c_RT", [128, 128])
    c_rope = din("c_rope", [2, 128, LS])
    c_jv = din("c_jv", [128, 512])
    c_cs256 = din("c_cs256", [128, 2, 512])
    c_dftp = din("c_dftp", [2, LP, LP], BF16)
    c_dfts = din("c_dfts", [2, LS, LS], BF16)

    y_out = dout("y_out", [NT, D])
    newk_out = dout("newk_out", [NPS, DEPTH, LP, NKV * HD])
    newv_out = dout("newv_out", [NPS, DEPTH, LP, NKV * HD])
    fin_out = dout("fin_out", [DEPTH, 128, NPS * 2 * 2 * 32])

    xres = dscr("xres", [NT, D], F32)
    qT = dscr("qT", [NH, 128, NT])
    kT = dscr("kT", [NKV, 128, NT])
    vS = dscr("vS", [NT, NKV * HD])
    gaT = dscr("gaT", [2048, NT])
    gsT = dscr("gsT", [1024, NT])
    gfT = dscr("gfT", [1024, NT])
    uT = dscr("uT", [1024, NT])
    fT = dscr("fT", [1024, NT])
    yT = dscr("yT", [1024, NT])
    fcs = dscr("fcs", [NT, 4, 512])
    dT = dscr("dT", [1024, NT])
    mixT = dscr("mixT", [D, NT])

    import contextlib
    es = contextlib.ExitStack()
    with es:
        sems = {e: es.enter_context(nc.semaphore("E_" + e)) for e in ('pe', 'dve', 'act', 'pool')}
        dsems = {q: [es.enter_context(nc.semaphore("D_%s_%d" % (q, i))) for i in range(12)] for q in ('sp', 'pool')}
        S = Sched(nc, sems, dsems)

        def sb(name, shape, dt):
            return es.enter_context(nc.sbuf_tensor(name, list(shape), dt))

        ps = [es.enter_context(nc.psum_tensor("ps%d" % i, [128, 512], F32)) for i in range(8)]
        PSK = ['ps%d' % i for i in range(8)]

        ident = sb("ident", [128, 128], F32)
        RTb = sb("RTb", [128, 128], BF16)
        onesb = sb("onesb", [128, 128], BF16)
        onesf = sb("onesf", [128, 128], F32)
        cv = sb("cv", [128, 32, 2], F32)
        s_bf = sb("s_bf", [128, 32, 2], BF16)
        s_f = sb("s_f", [128, 32, 2], F32)
        Amod = sb("Amod", [128, 32, 2], F32)
        Bmod = sb("Bmod", [128, 32, 2], F32)
        modsb = sb("modsb", [128, 64, 2], F32)
        bmp = sb("bmp", [128, 64], F32)
        ngp = sb("ngp", [128, 32], F32)
        qkg = sb("qkg", [128, 2], F32)
        qkg2 = sb("qkg2", [128, 2], F32)
        small = sb("small", [128, 16], F32)
        negpi = sb("negpi", [128, 1], F32)

        S.dma('sp', ident[:], c_ident[:, :], w=['ident'])
        S.dma('pool', RTb[:], c_RT[:, :], w=['RTb'])
        S.dma('sp', cv[:], cvec[:, :, :], w=['cv'])
        S.op('dve', lambda e: e.memset(onesf[:], 1.0), w=['onesf'])
        S.op('dve', lambda e: e.memset(onesb[:], 1.0), w=['onesb'])
        S.op('act', lambda e: e.activation(out=s_f[:], in_=cv[:], func=AF.Silu), r=['cv'], w=['s_f'])
        S.op('dve', lambda e: e.tensor_copy(out=s_bf[:], in_=s_f[:]), r=['s_f'], w=['s_bf'])

        slab = [None, None]
        slab_i = [0]

        def load_slab(src_ap):
            i = slab_i[0] % 2
            slab_i[0] += 1
            S.dma('pool', slab[i][:], src_ap.rearrange("(c p) m -> p c m", p=128), w=['slab%d' % i])
            return slab[i], 'slab%d' % i

        def rstd_from(out_ap, in_ap, scale, keys_r, key_w, tmpk='small'):
            S.op('dve', lambda e: e.tensor_scalar(out=out_ap, in0=in_ap, scalar1=scale, scalar2=EPS,
                                                  op0=ALU.mult, op1=ALU.add), r=keys_r, w=[key_w])
            S.op('act', lambda e: e.activation(out=out_ap, in_=out_ap, func=AF.Sqrt), r=[key_w], w=[key_w])
            S.op('dve', lambda e: e.reciprocal(out=out_ap, in_=out_ap), r=[key_w], w=[key_w])

        def sin_of(out_ap, arg_ap, tmpf, tmpi, shift, keys_r, key_w, kf, ki):
            S.op('dve', lambda e: e.tensor_scalar(out=tmpf, in0=arg_ap, scalar1=shift, scalar2=1.0 / (2 * PI),
                                                  op0=ALU.add, op1=ALU.mult), r=keys_r, w=[kf])
            S.op('dve', lambda e: e.tensor_copy(out=tmpi, in_=tmpf), r=[kf], w=[ki])
            S.op('dve', lambda e: e.tensor_copy(out=tmpf, in_=tmpi), r=[ki], w=[kf])
            S.op('dve', lambda e: e.scalar_tensor_tensor(out=tmpf, in0=tmpf, scalar=-2 * PI, in1=arg_ap,
                                                         op0=ALU.mult, op1=ALU.add), r=[kf] + list(keys_r), w=[kf])
            S.op('dve', lambda e: e.tensor_scalar(out=tmpf, in0=tmpf, scalar1=shift, scalar2=3.1415925,
                                                  op0=ALU.add, op1=ALU.min), r=[kf], w=[kf])
            S.op('dve', lambda e: e.tensor_scalar(out=tmpf, in0=tmpf, scalar1=-3.1415925, scalar2=None,
                                                  op0=ALU.max), r=[kf], w=[kf])
            S.op('act', lambda e: e.activation(out=out_ap, in_=tmpf, func=AF.Sin), r=[kf], w=[key_w])

        for l in range(DEPTH):
            xsrc = x_in if l == 0 else xres
            with contextlib.ExitStack() as ph:
                def sbp(name, shape, dt):
                    return ph.enter_context(nc.sbuf_tensor("%s_u%d" % (name, nc.next_id()), list(shape), dt))
                S.dma('sp', bmp[:], bmod_ps[l], w=['bmp'])
                S.dma('sp', ngp[:], normg_p[l], w=['ngp'])
                S.dma('sp', qkg[:], qk_g[l], w=['qkg'])
                slab[0] = sbp("slabA", [128, 32, 512], BF16)
                slab[1] = sbp("slabB", [128, 32, 512], BF16)
                Srep = sbp("Srep", [128, 2, 32, 128], BF16)
                gbias = sbp("gbias", [128, D], F32)
                S.dma('sp', gbias[:], bmodg_rep[l], w=['gbias'])
                for v in range(2):
                    for c in range(32):
                        S.op('pool', lambda e, v=v, c=c: e.tensor_scalar(out=Srep[:, v, c, :], in0=onesf[:], scalar1=s_f[:, c, v:v + 1],
                                                                        scalar2=None, op0=ALU.mult),
                             r=['onesf', 's_f'], w=['Srep'])
                for si in range(16):
                    sl, sk = load_slab(w_mod[l][:, si * 512:(si + 1) * 512])
                    for j in range(4):
                        blk = si * 4 + j
                        for c in range(32):
                            S.op('pe', lambda e, c=c, j=j, blk=blk, sl=sl: e.matmul(ps[0][:, blk * 2:blk * 2 + 2], lhsT=sl[:, c, j * 128:(j + 1) * 128],
                                                                                  rhs=s_bf[:, c, :], start=(c == 0), stop=(c == 31)),
                                 r=[sk, 's_bf'], w=['ps0'])
                S.op('dve', lambda e: e.tensor_tensor(out=modsb[:], in0=ps[0][:, 0:128].rearrange("p (b v) -> p b v", v=2),
                                                      in1=bmp[:].unsqueeze(2).to_broadcast([128, 64, 2]), op=ALU.add),
                     r=['ps0', 'bmp'], w=['modsb'])
                S.op('dve', lambda e: e.tensor_copy(out=Bmod[:], in_=modsb[:, 0:32, :]), r=['modsb'], w=['Bmod'])
                S.op('dve', lambda e: e.tensor_scalar(out=Amod[:], in0=modsb[:, 32:64, :], scalar1=1.0, scalar2=None, op0=ALU.add),
                     r=['modsb'], w=['Amod'])
                S.op('dve', lambda e: e.tensor_tensor(out=Amod[:], in0=Amod[:], in1=ngp[:].unsqueeze(2).to_broadcast([128, 32, 2]), op=ALU.mult),
                     r=['Amod', 'ngp'], w=['Amod'])
                S.op('dve', lambda e: e.tensor_scalar(out=qkg2[:, 0:1], in0=qkg[:, 0:1], scalar1=HD ** -0.5, scalar2=None, op0=ALU.mult),
                     r=['qkg'], w=['qkg2'])
                S.op('dve', lambda e: e.tensor_copy(out=qkg2[:, 1:2], in_=qkg[:, 1:2]), r=['qkg', 'qkg2'], w=['qkg2'])
                gate_sb = sbp("gate_sb", [128, 2, 512], F32)
                gate_dr = dscr("gate_dr_l%d" % l, [2, 128, D], F32)
                for si in range(16, 24):
                    sl, sk = load_slab(w_mod[l][:, si * 512:(si + 1) * 512])
                    g0 = (si - 16) * 512
                    for v in range(2):
                        for c in range(32):
                            S.op('pe', lambda e, c=c, v=v, sl=sl: e.matmul(ps[1 + v][:, :], lhsT=Srep[:, v, c, :], rhs=sl[:, c, :],
                                                                         start=(c == 0), stop=(c == 31)),
                                 r=[sk, 'Srep'], w=[PSK[1 + v]])
                        S.op('dve', lambda e, v=v, g0=g0: e.tensor_tensor(out=gate_sb[:, v, :], in0=ps[1 + v][:, :], in1=gbias[:, g0:g0 + 512], op=ALU.add),
                             r=[PSK[1 + v], 'gbias'], w=['gate_sb%d' % v])
                        S.dma('sp', gate_dr[v][:, g0:g0 + 512], gate_sb[:, v, :], r=['gate_sb%d' % v], w=['gate_dr'])
            S.drain()
            if cfg.get('STOP', 99) == 0:
                break

            with contextlib.ExitStack() as ph:
                def sbp(name, shape, dt):
                    return ph.enter_context(nc.sbuf_tensor("%s_u%d" % (name, nc.next_id()), list(shape), dt))
                slab[0] = sbp("slabA", [128, 32, 512], BF16)
                slab[1] = sbp("slabB", [128, 32, 512], BF16)
                xs = [sbp("xs%d" % i, [128, D], F32) for i in range(2)]
                junk = sbp("junk", [128, D], BF16)
                hT = sbp("hT", [128, 32, 512], BF16)
                ssq = sbp("ssq", [128, 4], F32)
                sq = sbp("sq", [128, 512], BF16)
                rs = sbp("rs", [128, 512], F32)
                qn = sbp("qn", [128, 512], F32)
                qb = sbp("qb", [128, 512], BF16)
                qo = [sbp("qo%d" % i, [128, 512], BF16) for i in range(2)]
                t1 = sbp("t1", [128, 512], F32)
                t2 = sbp("t2", [128, 512], F32)
                ropec = sbp("ropec", [128, 512], F32)
                ropes = sbp("ropes", [128, 512], F32)
                vb = sbp("vb", [128, 512], BF16)
                vf = sbp("vf", [128, 512], F32)
                ko = sbp("ko", [128, 512], F32)
                ev = [sbp("ev%d" % i, [128, 512], BF16) for i in range(2)]
                evi = 0
                for ti in range(NTILE):
                    t0 = ti * TW0
                    v = 0 if ti < NPT else 1
                    samp = ti >= NPT
                    if samp:
                        p0 = t0 - NTP
                        S.dma('sp', ropec[:], c_rope[0][:, p0:p0 + 512], w=['ropec'])
                        S.dma('sp', ropes[:], c_rope[1][:, p0:p0 + 512], w=['ropes'])
                    for sub in range(4):
                        xt = xs[sub % 2]
                        xk = 'xs%d' % (sub % 2)
                        S.dma('sp', xt[:], xsrc[t0 + sub * 128:t0 + (sub + 1) * 128, :], w=[xk])
                        S.op('act', lambda e, xt=xt, sub=sub: e.activation(out=junk[:], in_=xt[:], func=AF.Square, accum_out=ssq[:, sub:sub + 1]),
                             r=[xk], w=['junk', 'ssq%d' % sub])
                        rstd_from(ssq[:, sub:sub + 1], ssq[:, sub:sub + 1], 1.0 / D, ['ssq%d' % sub], 'ssq%d' % sub)
                        S.op('pool', lambda e, xt=xt, sub=sub: e.tensor_scalar(out=xt[:], in0=xt[:], scalar1=ssq[:, sub:sub + 1], scalar2=None, op0=ALU.mult),
                             r=[xk, 'ssq%d' % sub], w=[xk])
                        for c0 in range(0, 32, 4):
                            bank = 4 + (c0 // 4) % 2
                            for cc in range(4):
                                c = c0 + cc
                                S.op('pe', lambda e, c=c, cc=cc, xt=xt, bank=bank: e.transpose(out=ps[bank][:, cc * 128:(cc + 1) * 128], in_=xt[:, c * 128:(c + 1) * 128], identity=ident[:]),
                                     r=[xk, 'ident'], w=[PSK[bank]])
                            for cc in range(4):
                                c = c0 + cc
                                S.op('dve', lambda e, c=c, cc=cc, bank=bank, sub=sub, v=v: e.tensor_scalar(
                                    out=hT[:, c, sub * 128:(sub + 1) * 128], in0=ps[bank][:, cc * 128:(cc + 1) * 128],
                                    scalar1=Amod[:, c, v:v + 1], scalar2=Bmod[:, c, v:v + 1], op0=ALU.mult, op1=ALU.add),
                                    r=[PSK[bank], 'Amod', 'Bmod'], w=['hT'])
                    if cfg.get('P1STOP', 0) == 1:
                        break
                    for si in range(18):
                        if cfg.get('P1STOP', 0) == 2 + si:
                            break
                        sl, sk = load_slab(w_in[l][:, si * 512:(si + 1) * 512])
                        if si == 5:
                            for sub in range(4):
                                bank = sub % 2
                                for c in range(32):
                                    S.op('pe', lambda e, c=c, sub=sub, sl=sl, bank=bank: e.matmul(ps[bank][:, :], lhsT=hT[:, c, sub * 128:(sub + 1) * 128], rhs=sl[:, c, :],
                                                                                                start=(c == 0), stop=(c == 31)),
                                         r=[sk, 'hT'], w=[PSK[bank]])
                                S.op('dve', lambda e, bank=bank: e.tensor_copy(out=vf[:], in_=ps[bank][:, :]), r=[PSK[bank]], w=['vf'])
                                S.op('act', lambda e: e.activation(out=vb[:], in_=vf[:], func=AF.Copy), r=['vf'], w=['vb'])
                                S.dma('sp', vS[t0 + sub * 128:t0 + (sub + 1) * 128, :], vb[:], r=['vb'], w=['vS%d' % ti])
                                if not samp:
                                    tok = t0 + sub * 128
                                    S.dma('sp', newv_out[tok // LP, l, tok % LP:tok % LP + 128, :], vf[:], r=['vf'], w=['newv'])
                            continue
                        for j in range(4):
                            fb = si * 4 + j
                            bank = fb % 2
                            for c in range(32):
                                S.op('pe', lambda e, c=c, j=j, sl=sl, bank=bank: e.matmul(ps[bank][:, :], lhsT=sl[:, c, j * 128:(j + 1) * 128], rhs=hT[:, c, :],
                                                                                        start=(c == 0), stop=(c == 31)),
                                     r=[sk, 'hT'], w=[PSK[bank]])
                            if si <= 4:
                                isk = si == 4
                                S.op('act', lambda e, bank=bank: e.activation(out=sq[:], in_=ps[bank][:, :], func=AF.Square), r=[PSK[bank]], w=['sq'])
                                S.op('pe', lambda e: e.matmul(ps[2][:, :], lhsT=onesb[:], rhs=sq[:], start=True, stop=True), r=['sq', 'onesb'], w=['ps2'])
                                rstd_from(rs[:], ps[2][:, :], 1.0 / HD, ['ps2'], 'rs')
                                gcol = 1 if isk else 0
                                S.op('dve', lambda e, bank=bank, gcol=gcol: e.scalar_tensor_tensor(out=qn[:], in0=ps[bank][:, :], scalar=qkg2[:, gcol:gcol + 1], in1=rs[:],
                                                                                                 op0=ALU.mult, op1=ALU.mult),
                                     r=[PSK[bank], 'qkg2', 'rs'], w=['qn'])
                                if isk and not samp:
                                    for sub in range(4):
                                        S.op('pe', lambda e, sub=sub: e.transpose(out=ps[3][:, sub * 128:(sub + 1) * 128], in_=qn[:, sub * 128:(sub + 1) * 128], identity=ident[:]),
                                             r=['qn', 'ident'], w=['ps3'])
                                    S.op('dve', lambda e: e.tensor_copy(out=ko[:], in_=ps[3][:, :]), r=['ps3'], w=['ko'])
                                    for sub in range(4):
                                        tok = t0 + sub * 128
                                        S.dma('sp', newk_out[tok // LP, l, tok % LP:tok % LP + 128, j * 128:(j + 1) * 128], ko[:, sub * 128:(sub + 1) * 128],
                                              r=['ko'], w=['newk'])
                                o = qo[evi % 2]
                                ok = 'qo%d' % (evi % 2)
                                evi += 1
                                if samp:
                                    S.op('act', lambda e: e.activation(out=qb[:], in_=qn[:], func=AF.Copy), r=['qn'], w=['qb'])
                                    S.op('pe', lambda e: e.matmul(ps[3][:, :], lhsT=RTb[:], rhs=qb[:], start=True, stop=True), r=['qb', 'RTb'], w=['ps3'])
                                    S.op('pool', lambda e: e.tensor_tensor(out=t1[:], in0=qn[:], in1=ropec[:], op=ALU.mult), r=['qn', 'ropec'], w=['t1'])
                                    S.op('dve', lambda e: e.tensor_tensor(out=t2[:], in0=ps[3][:, :], in1=ropes[:], op=ALU.mult), r=['ps3', 'ropes'], w=['t2'])
                                    S.op('pool', lambda e, o=o: e.tensor_tensor(out=o[:], in0=t1[:], in1=t2[:], op=ALU.add), r=['t1', 't2'], w=[ok])
                                else:
                                    S.op('act', lambda e, o=o: e.activation(out=o[:], in_=qn[:], func=AF.Copy), r=['qn'], w=[ok])
                                if isk:
                                    S.dma('sp', kT[j][:, t0:t0 + 512], o[:], r=[ok], w=['kT%d' % ti])
                                else:
                                    S.dma('sp', qT[fb][:, t0:t0 + 512], o[:], r=[ok], w=['qT%d' % ti])
                            else:
                                o = ev[evi % 2]
                                ok = 'ev%d' % (evi % 2)
                                evi += 1
                                f0 = fb * 128
                                if 3072 <= f0 < 5120:
                                    dst, dk_, fn = gaT[f0 - 3072:f0 - 3072 + 128, t0:t0 + 512], 'gaT%d' % ti, AF.Silu
                                elif 5120 <= f0 < 6144:
                                    dst, dk_, fn = uT[f0 - 5120:f0 - 5120 + 128, t0:t0 + 512], 'uT', AF.Copy
                                elif 6144 <= f0 < 7168:
                                    dst, dk_, fn = gsT[f0 - 6144:f0 - 6144 + 128, t0:t0 + 512], 'gsT%d' % ti, AF.Silu
                                elif 7168 <= f0 < 8192:
                                    dst, dk_, fn = fT[f0 - 7168:f0 - 7168 + 128, t0:t0 + 512], 'fT%d' % ti, AF.Copy
                                else:
                                    dst, dk_, fn = gfT[f0 - 8192:f0 - 8192 + 128, t0:t0 + 512], 'gfT%d' % ti, AF.Silu
                                S.op('act', lambda e, o=o, bank=bank, fn=fn: e.activation(out=o[:], in_=ps[bank][:, :], func=fn), r=[PSK[bank]], w=[ok])
                                S.dma('sp', dst, o[:], r=[ok], w=[dk_])
            S.drain()
            if cfg.get('STOP', 99) == 1:
                break

            with contextlib.ExitStack() as ph:
                def sbp(name, shape, dt):
                    return ph.enter_context(nc.sbuf_tensor("%s_u%d" % (name, nc.next_id()), list(shape), dt))
                NKMAX = LS + PAST
                KTs = sbp("KTs", [128, NKV, NKMAX], BF16)
                Vs = sbp("Vs", [128, NKMAX // 128, NKV * HD], BF16)
                Q4 = [sbp("Q4_%d" % i, [128, 4, 128], BF16) for i in range(2)]
                G4 = [sbp("G4_%d" % i, [128, 4, 128], BF16) for i in range(2)]
                Pb = [sbp("Pb%d" % i, [128, 512], BF16) for i in range(3)]
                rl = sbp("rl", [128, 512], F32)
                pacc = [sbp("pacc%d" % i, [128, 512], F32) for i in range(2)]
                ot = sbp("ot", [128, 512], F32)
                mo = [sbp("mo%d" % i, [128, 4, 128], BF16) for i in range(2)]
                it = 0
                pi_ = 0
                for (s0, L, samp) in seqs:
                    nk = L + (PAST if samp else 0)
                    koff = PAST if samp else 0
                    if samp:
                        for kv in range(NKV):
                            S.dma('pool', KTs[:, kv, 0:PAST], cache_kT[l, kv], w=['KTs'])
                        S.dma('pool', Vs[:, 0:PAST // 128, :], cache_v[l].rearrange("(b p) f -> p b f", p=128), w=['Vs'])
                    for kv in range(NKV):
                        S.dma('sp', KTs[:, kv, koff:koff + L], kT[kv][:, s0:s0 + L], r=['kT%d' % i for i in range(NTILE)], w=['KTs'])
                    S.dma('sp', Vs[:, koff // 128:(koff + L) // 128, :], vS[s0:s0 + L, :].rearrange("(b p) f -> p b f", p=128),
                          r=['vS%d' % i for i in range(NTILE)], w=['Vs'])
                    nkb = nk // 128
                    for kv in range(NKV):
                        for qb_ in range(L // 128):
                            q0 = s0 + qb_ * 128
                            Q = Q4[it % 2]
                            G = G4[it % 2]
                            M = mo[it % 2]
                            qk_ = 'Q4_%d' % (it % 2)
                            gk_ = 'G4_%d' % (it % 2)
                            mk_ = 'mo%d' % (it % 2)
                            po, pl = (4, 5) if it % 2 == 0 else (6, 7)
                            pa, pak = pacc[it % 2], 'pacc%d' % (it % 2)
                            it += 1
                            S.dma('sp', Q[:], qT[4 * kv:4 * kv + 4, :, q0:q0 + 128].rearrange("h d t -> d h t"),
                                  r=['qT%d' % i for i in range(NTILE)], w=[qk_])
                            S.dma('sp', G[:], gaT[kv * 512:(kv + 1) * 512, q0:q0 + 128].rearrange("(h d) t -> d h t", d=128),
                                  r=['gaT%d' % i for i in range(NTILE)], w=[gk_])
                            Qf = Q[:].rearrange("d h t -> d (h t)")
                            def emit_s(kb):
                                sbank = kb % 4
                                S.op('pe', lambda e, kb=kb, kv=kv, sbank=sbank, Qf=Qf: e.matmul(ps[sbank][:, :], lhsT=KTs[:, kv, kb * 128:(kb + 1) * 128], rhs=Qf, start=True, stop=True),
                                     r=['KTs', qk_], w=[PSK[sbank]])
                            emit_s(0)
                            if nkb > 1:
                                emit_s(1)
                            for kb in range(nkb):
                                sbank = kb % 4
                                P = Pb[pi_ % 3]
                                pk_ = 'Pb%d' % (pi_ % 3)
                                pi_ += 1
                                if kb + 2 < nkb:
                                    emit_s(kb + 2)
                                S.op('act', lambda e, P=P, sbank=sbank: e.activation(out=P[:], in_=ps[sbank][:, :], func=AF.Exp), r=[PSK[sbank]], w=[pk_])
                                S.op('pe', lambda e, kb=kb, kv=kv, P=P, po=po: e.matmul(ps[po][:, :], lhsT=Vs[:, kb, kv * 128:(kv + 1) * 128], rhs=P[:], start=(kb == 0), stop=(kb == nkb - 1)),
                                     r=['Vs', pk_], w=[PSK[po]])
                                if kb == 0:
                                    S.op('dve', lambda e, P=P, pa=pa: e.tensor_copy(out=pa[:], in_=P[:]), r=[pk_], w=[pak])
                                else:
                                    S.op('dve', lambda e, P=P, pa=pa: e.tensor_tensor(out=pa[:], in0=pa[:], in1=P[:], op=ALU.add), r=[pk_, pak], w=[pak])
                            S.op('pe', lambda e, pa=pa, pl=pl: e.matmul(ps[pl][:, :], lhsT=onesf[:], rhs=pa[:], start=True, stop=True),
                                 r=['onesf', pak], w=[PSK[pl]])
                            S.op('dve', lambda e, pl=pl: e.reciprocal(out=rl[:], in_=ps[pl][:, :]), r=[PSK[pl]], w=['rl'])
                            S.op('dve', lambda e, po=po: e.tensor_tensor(out=ot[:], in0=ps[po][:, :], in1=rl[:], op=ALU.mult), r=[PSK[po], 'rl'], w=['ot'])
                            S.op('pool', lambda e, M=M, G=G: e.tensor_tensor(out=M[:].rearrange("d h t -> d (h t)"), in0=ot[:], in1=G[:].rearrange("d h t -> d (h t)"), op=ALU.mult),
                                 r=['ot', gk_], w=[mk_])
                            S.dma('sp', mixT[kv * 512:(kv + 1) * 512, q0:q0 + 128].rearrange("(h d) t -> d h t", d=128), M[:], r=[mk_], w=['mixT_a'])
            S.drain()
            if cfg.get('STOP', 99) == 2:
                break

            with contextlib.ExitStack() as ph:
                def sbp(name, shape, dt):
                    return ph.enter_context(nc.sbuf_tensor("%s_u%d" % (name, nc.next_id()), list(shape), dt))
                cs256 = sbp("cs256", [128, 2, 512], BF16)
                S.dma('pool', cs256[:], c_cs256[:, :, :], w=['cs256'])
                fTt = [sbp("fTt%d" % i, [128, 8, 512], BF16) for i in range(2)]
                fco = [sbp("fco%d" % i, [128, 4, 512], BF16) for i in range(2)]
                for ti in range(NTILE):
                    t0 = ti * TW0
                    ft = fTt[ti % 2]
                    fk = 'fTt%d' % (ti % 2)
                    S.dma('sp', ft[:], fT[:, t0:t0 + 512].rearrange("(c p) t -> p c t", p=128), r=['fT%d' % ti], w=[fk])
                    for sub in range(4):
                        fo = fco[sub % 2]
                        fok = 'fco%d' % (sub % 2)
                        for g in range(4):
                            bank = g % 2
                            for cc in range(2):
                                S.op('pe', lambda e, g=g, cc=cc, sub=sub, ft=ft, bank=bank: e.matmul(ps[bank][:, :], lhsT=ft[:, 2 * g + cc, sub * 128:(sub + 1) * 128], rhs=cs256[:, cc, :],
                                                                                                 start=(cc == 0), stop=(cc == 1)),
                                     r=[fk, 'cs256'], w=[PSK[bank]])
                            S.op('act', lambda e, g=g, fo=fo, bank=bank: e.activation(out=fo[:, g, :], in_=ps[bank][:, :], func=AF.Copy), r=[PSK[bank]], w=[fok])
                        S.dma('sp', fcs[t0 + sub * 128:t0 + (sub + 1) * 128, :, :], fo[:], r=[fok], w=['fcs'])
            S.drain()
            if cfg.get('STOP', 99) == 3:
                break
            with contextlib.ExitStack() as ph:
                def sbp(name, shape, dt):
                    return ph.enter_context(nc.sbuf_tensor("%s_u%d" % (name, nc.next_id()), list(shape), dt))
                LMAX = LS
                cosl = sbp("cosl", [128, LMAX // 128, min(512, LMAX)], BF16)
                sinl = sbp("sinl", [128, LMAX // 128, min(512, LMAX)], BF16)
                Fc = [sbp("Fc%d" % i, [128, LMAX // 128, 128], BF16) for i in range(2)]
                Fs = [sbp("Fs%d" % i, [128, LMAX // 128, 128], BF16) for i in range(2)]
                dfo = [sbp("dfo%d" % i, [128, 512], BF16) for i in range(2)]
                it = 0
                for (s0, L, samp) in seqs:
                    dsrc = c_dfts if samp else c_dftp
                    ntb = L // 128
                    KW = min(512, L)
                    for kt in range(L // KW):
                        S.dma('sp', cosl[:, 0:ntb, 0:KW], dsrc[0][:, kt * KW:(kt + 1) * KW].rearrange("(b p) k -> p b k", p=128), w=['cosl'])
                        S.dma('sp', sinl[:, 0:ntb, 0:KW], dsrc[1][:, kt * KW:(kt + 1) * KW].rearrange("(b p) k -> p b k", p=128), w=['sinl'])
                        for mb in range(8):
                            g, half = mb // 2, mb % 2
                            fc_, fs_ = Fc[it % 2], Fs[it % 2]
                            fck, fsk = 'Fc%d' % (it % 2), 'Fs%d' % (it % 2)
                            do_ = dfo[it % 2]
                            dok = 'dfo%d' % (it % 2)
                            bank = 2 + it % 2
                            it += 1
                            S.dma('sp', fc_[:, 0:ntb, :], fcs[s0:s0 + L, g, half * 128:half * 128 + 128].rearrange("(b p) m -> p b m", p=128), r=['fcs'], w=[fck])
                            S.dma('sp', fs_[:, 0:ntb, :], fcs[s0:s0 + L, g, 256 + half * 128:256 + half * 128 + 128].rearrange("(b p) m -> p b m", p=128), r=['fcs'], w=[fsk])
                            for tb in range(ntb):
                                S.op('pe', lambda e, tb=tb, fc_=fc_, bank=bank, KW=KW: e.matmul(ps[bank][:, 0:KW], lhsT=fc_[:, tb, :], rhs=cosl[:, tb, 0:KW], start=(tb == 0), stop=False),
                                     r=[fck, 'cosl'], w=[PSK[bank]])
                                S.op('pe', lambda e, tb=tb, fs_=fs_, bank=bank, KW=KW, ntb=ntb: e.matmul(ps[bank][:, 0:KW], lhsT=fs_[:, tb, :], rhs=sinl[:, tb, 0:KW], start=False, stop=(tb == ntb - 1)),
                                     r=[fsk, 'sinl'], w=[PSK[bank]])
                            S.op('act', lambda e, do_=do_, bank=bank, KW=KW: e.activation(out=do_[:, 0:KW], in_=ps[bank][:, 0:KW], func=AF.Copy), r=[PSK[bank]], w=[dok])
                            S.dma('sp', dT[mb * 128:(mb + 1) * 128, s0 + kt * KW:s0 + (kt + 1) * KW], do_[:, 0:KW], r=[dok], w=['dT'])
            S.drain()
            if cfg.get('STOP', 99) == 4:
                break
            with contextlib.ExitStack() as ph:
                def sbp(name, shape, dt):
                    return ph.enter_context(nc.sbuf_tensor("%s_u%d" % (name, nc.next_id()), list(shape), dt))
                wf = sbp("wf", [128, 8, 1024], BF16)
                S.dma('pool', wf[:], w_fft[l].rearrange("(c p) m -> p c m", p=128), w=['wf'])
                dTt = [sbp("dTt%d" % i, [128, 8, 512], BF16) for i in range(2)]
                gft = [sbp("gft%d" % i, [128, 8, 512], BF16) for i in range(2)]
                mfo = [sbp("mfo%d" % i, [128, 512], BF16) for i in range(2)]
                it = 0
                for ti in range(NTILE):
                    t0 = ti * TW0
                    dt_, gt_ = dTt[ti % 2], gft[ti % 2]
                    dtk, gtk = 'dTt%d' % (ti % 2), 'gft%d' % (ti % 2)
                    S.dma('sp', dt_[:], dT[:, t0:t0 + 512].rearrange("(c p) t -> p c t", p=128), r=['dT'], w=[dtk])
                    S.dma('sp', gt_[:], gfT[:, t0:t0 + 512].rearrange("(c p) t -> p c t", p=128), r=['gfT%d' % ti], w=[gtk])
                    for ob in range(8):
                        bank = it % 2
                        m_ = mfo[it % 2]
                        mk_ = 'mfo%d' % (it % 2)
                        it += 1
                        for mb in range(8):
                            S.op('pe', lambda e, mb=mb, ob=ob, dt_=dt_, bank=bank: e.matmul(ps[bank][:, :], lhsT=wf[:, mb, ob * 128:(ob + 1) * 128], rhs=dt_[:, mb, :], start=(mb == 0), stop=(mb == 7)),
                                 r=['wf', dtk], w=[PSK[bank]])
                        S.op('dve', lambda e, m_=m_, bank=bank, gt_=gt_, ob=ob: e.tensor_tensor(out=m_[:], in0=ps[bank][:, :], in1=gt_[:, ob, :], op=ALU.mult), r=[PSK[bank], gtk], w=[mk_])
                        S.dma('sp', mixT[3072 + ob * 128:3072 + (ob + 1) * 128, t0:t0 + 512], m_[:], r=[mk_], w=['mixT_f'])
            S.drain()
            if cfg.get('STOP', 99) == 5:
                break

            with contextlib.ExitStack() as ph:
                def sbp(name, shape, dt):
                    return ph.enter_context(nc.sbuf_tensor("%s_u%d" % (name, nc.next_id()), list(shape), dt))
                lq = sbp("lq", [128, 3, 64], F32)
                dtq = sbp("dtq", [128, 64], F32)
                r_q = sbp("r_q", [128, 64], F32)
                ang_q = sbp("ang_q", [128, 64], F32)
                stq = sbp("stq", [128, 2, 2, 32], F32)
                dsk = sbp("dsk", [128, 8], F32)
                fin = sbp("fin", [128, NPS, 2, 2, 32], F32)
                jv = sbp("jv", [128, 512], F32)
                S.dma('sp', lq[:], lam_q[l].rearrange("a p c -> p a c"), w=['lq'])
                S.dma('sp', stq[:], st_q[l], w=['stq'])
                S.dma('sp', dsk[:], dskip_p[l], w=['dsk'])
                S.dma('sp', jv[:], c_jv[:, :], w=['jv'])
                S.op('act', lambda e: e.activation(out=dtq[:], in_=lq[:, 2, :], func=AF.Exp), r=['lq'], w=['dtq'])
                S.op('dve', lambda e: e.tensor_tensor(out=r_q[:], in0=lq[:, 0, :], in1=dtq[:], op=ALU.mult), r=['lq', 'dtq'], w=['r_q'])
                S.op('act', lambda e: e.activation(out=r_q[:], in_=r_q[:], func=AF.Exp), r=['r_q'], w=['r_q'])
                S.op('dve', lambda e: e.tensor_tensor(out=ang_q[:], in0=lq[:, 1, :], in1=dtq[:], op=ALU.mult), r=['lq', 'dtq'], w=['ang_q'])

                lr_ = sbp("lr_", [128, 512], F32)
                li_ = sbp("li_", [128, 512], F32)
                ls_ = sbp("ls_", [128, 512], F32)
                z1 = sbp("z1", [128, 512], F32)
                z2 = sbp("z2", [128, 512], F32)
                z3 = sbp("z3", [128, 512], F32)
                z4 = sbp("z4", [128, 512], F32)
                zi = sbp("zi", [128, 512], I32)
                fre = sbp("fre", [128, 512], F32)
                fim = sbp("fim", [128, 512], F32)
                bre = sbp("bre", [128, 512], F32)
                bim = sbp("bim", [128, 512], F32)
                lBr = sbp("lBr", [128, 2, 512], BF16)
                lBi = sbp("lBi", [128, 2, 512], BF16)
                cTr = sbp("cTr", [128, 2, 4, 128], BF16)
                cTi = sbp("cTi", [128, 2, 4, 128], BF16)
                TC = sbp("TC", [128, 8, 512], F32)
                TS = sbp("TS", [128, 8, 512], F32)
                targ = sbp("targ", [128, 512], F32)
                ttf = sbp("ttf", [128, 512], F32)
                tti = sbp("tti", [128, 512], I32)
                uS = sbp("uS", [128, NT], BF16)
                ysb = sbp("ysb", [128, NT], F32)
                ybf = sbp("ybf", [128, NT], BF16)
                car = sbp("car", [128, 4, 2], F32)
                cw = sbp("cw", [128, 8], F32)
                W = {n: sbp("W" + n, [128, 512], F32) for n in ('a', 'b', 'c', 'd', 'wr', 'wi', 'gr', 'gi', 'e', 'f', 'g', 'h')}
                WB = dict(a=(lr_, 'lr_'), b=(li_, 'li_'), c=(ls_, 'ls_'), d=(z1, 'z1'), wr=(z2, 'z2'), wi=(z3, 'z3'), gr=(z4, 'z4'),
                          gi=(fre, 'fre'), e=(fim, 'fim'), f=(bre, 'bre'), g=(bim, 'bim'), h=(ttf, 'ttf'))
                WA = {n: (t, 'W' + n) for n, t in W.items()}
                hrb = [sbp("hrb%d" % i, [128, 512], BF16) for i in range(2)]
                hib = [sbp("hib%d" % i, [128, 512], BF16) for i in range(2)]

                for uc in range(8):
                    S.dma('pool', cTr[:], cT_pad[l, 0, uc], w=['cTr'])
                    S.dma('pool', cTi[:], cT_pad[l, 1, uc], w=['cTi'])
                    S.dma('sp', uS[:], uT[uc * 128:(uc + 1) * 128, :], r=['uT'], w=['uS'])
                    for dz in range(2):
                        S.dma('sp', lr_[:], lam_rep[l, 0][:, dz, uc * 512:(uc + 1) * 512], w=['lr_'])
                        S.dma('sp', li_[:], lam_rep[l, 1][:, dz, uc * 512:(uc + 1) * 512], w=['li_'])
                        S.dma('sp', ls_[:], lam_rep[l, 2][:, dz, uc * 512:(uc + 1) * 512], w=['ls_'])
                        S.dma('sp', bre[:], bT_pad[l, 0, uc][:, dz, :], w=['bre'])
                        S.dma('sp', bim[:], bT_pad[l, 1, uc][:, dz, :], w=['bim'])
                        fl = lambda t: t[:]
                        S.op('act', lambda e: e.activation(out=fl(z1), in_=fl(ls_), func=AF.Exp), r=['ls_'], w=['z1'])
                        S.op('dve', lambda e: e.tensor_tensor(out=fl(z2), in0=fl(lr_), in1=fl(z1), op=ALU.mult), r=['lr_', 'z1'], w=['z2'])
                        S.op('act', lambda e: e.activation(out=fl(z2), in_=fl(z2), func=AF.Exp), r=['z2'], w=['z2'])
                        S.op('dve', lambda e: e.tensor_tensor(out=fl(z1), in0=fl(li_), in1=fl(z1), op=ALU.mult), r=['li_', 'z1'], w=['z1'])
                        sin_of(fl(z3), fl(z1), fl(z4), fl(zi), 0.0, ['z1'], 'z3', 'z4', 'zi')
                        S.op('dve', lambda e: e.tensor_tensor(out=fl(z3), in0=fl(z3), in1=fl(z2), op=ALU.mult), r=['z3', 'z2'], w=['z3'])
                        sin_of(fl(fre), fl(z1), fl(z4), fl(zi), PI / 2, ['z1'], 'fre', 'z4', 'zi')
                        S.op('dve', lambda e: e.tensor_tensor(out=fl(z2), in0=fl(fre), in1=fl(z2), op=ALU.mult), r=['fre', 'z2'], w=['z2'])
                        S.op('dve', lambda e: e.tensor_scalar(out=fl(z2), in0=fl(z2), scalar1=-1.0, scalar2=None, op0=ALU.add), r=['z2'], w=['z2'])
                        S.op('dve', lambda e: e.tensor_tensor(out=fl(z1), in0=fl(lr_), in1=fl(lr_), op=ALU.mult), r=['lr_', 'z1'], w=['z1'])
                        S.op('dve', lambda e: e.tensor_tensor(out=fl(z4), in0=fl(li_), in1=fl(li_), op=ALU.mult), r=['li_'], w=['z4'])
                        S.op('dve', lambda e: e.tensor_tensor(out=fl(z1), in0=fl(z1), in1=fl(z4), op=ALU.add), r=['z1', 'z4'], w=['z1'])
                        S.op('dve', lambda e: e.reciprocal(out=fl(z1), in_=fl(z1)), r=['z1'], w=['z1'])
                        S.op('dve', lambda e: e.tensor_tensor(out=fl(fre), in0=fl(z2), in1=fl(lr_), op=ALU.mult), r=['z2', 'lr_'], w=['fre'])
                        S.op('dve', lambda e: e.tensor_tensor(out=fl(z4), in0=fl(z3), in1=fl(li_), op=ALU.mult), r=['z3', 'li_'], w=['z4'])
                        S.op('dve', lambda e: e.tensor_tensor(out=fl(fre), in0=fl(fre), in1=fl(z4), op=ALU.add), r=['fre', 'z4'], w=['fre'])
                        S.op('dve', lambda e: e.tensor_tensor(out=fl(fre), in0=fl(fre), in1=fl(z1), op=ALU.mult), r=['fre', 'z1'], w=['fre'])
                        S.op('dve', lambda e: e.tensor_tensor(out=fl(fim), in0=fl(z3), in1=fl(lr_), op=ALU.mult), r=['z3', 'lr_'], w=['fim'])
                        S.op('dve', lambda e: e.tensor_tensor(out=fl(z4), in0=fl(z2), in1=fl(li_), op=ALU.mult), r=['z2', 'li_'], w=['z4'])
                        S.op('dve', lambda e: e.tensor_tensor(out=fl(fim), in0=fl(fim), in1=fl(z4), op=ALU.subtract), r=['fim', 'z4'], w=['fim'])
                        S.op('dve', lambda e: e.tensor_tensor(out=fl(fim), in0=fl(fim), in1=fl(z1), op=ALU.mult), r=['fim', 'z1'], w=['fim'])
                        S.op('dve', lambda e: e.tensor_tensor(out=fl(z1), in0=fl(fre), in1=fl(bre), op=ALU.mult), r=['fre', 'bre'], w=['z1'])
                        S.op('dve', lambda e: e.tensor_tensor(out=fl(z2), in0=fl(fim), in1=fl(bim), op=ALU.mult), r=['fim', 'bim'], w=['z2'])
                        S.op('dve', lambda e, dz=dz: e.tensor_tensor(out=lBr[:, dz, :], in0=fl(z1), in1=fl(z2), op=ALU.subtract), r=['z1', 'z2'], w=['lBr'])
                        S.op('dve', lambda e: e.tensor_tensor(out=fl(z1), in0=fl(fre), in1=fl(bim), op=ALU.mult), r=['fre', 'bim'], w=['z1'])
                        S.op('dve', lambda e: e.tensor_tensor(out=fl(z2), in0=fl(fim), in1=fl(bre), op=ALU.mult), r=['fim', 'bre'], w=['z2'])
                        S.op('dve', lambda e, dz=dz: e.tensor_tensor(out=lBi[:, dz, :], in0=fl(z1), in1=fl(z2), op=ALU.add), r=['z1', 'z2'], w=['lBi'])
                    for d in range(2):
                        for k in range(4):
                            dk = d * 4 + k
                            col = d * 32 + uc * 4 + k
                            S.op('dve', lambda e, col=col: e.tensor_scalar(out=targ[:], in0=jv[:], scalar1=ang_q[:, col:col + 1], scalar2=None, op0=ALU.mult),
                                 r=['jv', 'ang_q'], w=['targ'])
                            sin_of(TS[:, dk, :], targ[:], ttf[:], tti[:], 0.0, ['targ'], 'TS%d' % dk, 'ttf', 'tti')
                            sin_of(TC[:, dk, :], targ[:], ttf[:], tti[:], PI / 2, ['targ'], 'TC%d' % dk, 'ttf', 'tti')
                    hi_ = 0
                    yb_ = 0
                    for si_, (s0, L, samp) in enumerate(seqs):
                        TW = min(512, L)
                        ntl = L // TW
                        for d in range(2):
                            order = range(ntl) if d == 0 else range(ntl - 1, -1, -1)
                            for k in range(4):
                                st = uc * 4 + k
                                if samp:
                                    S.op('pool', lambda e, k=k, d=d, st=st: e.tensor_copy(out=car[:, k, :], in_=stq[:, d, :, st]), r=['stq'], w=['car%d' % k])
                                else:
                                    S.op('pool', lambda e, k=k: e.memset(car[:, k, :], 0.0), w=['car%d' % k])
                            its = [(tl, k) for tl in order for k in range(4)]
                            ctxs = []
                            for (tl, k) in its:
                                ctxs.append(dict(tl=tl, k=k, c0=s0 + tl * TW, par=hi_ % 2, ybank=6 + (yb_ % 2)))
                                hi_ += 1
                                if k == 3:
                                    yb_ += 1

                            def stage_a(cx):
                                k, c0, par = cx['k'], cx['c0'], cx['par']
                                dk = d * 4 + k
                                col = d * 32 + uc * 4 + k
                                br, bi = (0, 1) if par == 0 else (2, 3)
                                S.op('pe', lambda e, d=d, k=k, c0=c0, TW=TW, br=br: e.matmul(ps[br][:, 0:TW], lhsT=lBr[:, d, k * 128:(k + 1) * 128], rhs=uS[:, c0:c0 + TW], start=True, stop=True),
                                     r=['lBr', 'uS'], w=[PSK[br]])
                                S.op('pe', lambda e, d=d, k=k, c0=c0, TW=TW, bi=bi: e.matmul(ps[bi][:, 0:TW], lhsT=lBi[:, d, k * 128:(k + 1) * 128], rhs=uS[:, c0:c0 + TW], start=True, stop=True),
                                     r=['lBi', 'uS'], w=[PSK[bi]])
                                if d == 0:
                                    pr, pi2 = ps[br][:, 0:TW], ps[bi][:, 0:TW]
                                else:
                                    pr, pi2 = ps[br][:, 0:TW][:, ::-1], ps[bi][:, 0:TW][:, ::-1]
                                tc_, ts_ = TC[:, dk, 0:TW], TS[:, dk, 0:TW]
                                tck, tsk = 'TC%d' % dk, 'TS%d' % dk
                                WS = WA if par == 0 else WB
                                Wv = {n: t[:, 0:TW] for n, (t, _) in WS.items()}
                                Wk = {n: kk for n, (_, kk) in WS.items()}
                                Wt = {n: t for n, (t, _) in WS.items()}
                                S.op('dve', lambda e, pr=pr, tc_=tc_, Wv=Wv: e.tensor_tensor(out=Wv['a'], in0=pr, in1=tc_, op=ALU.mult), r=[PSK[br], tck], w=[Wk['a']])
                                S.op('dve', lambda e, pi2=pi2, ts_=ts_, Wv=Wv: e.tensor_tensor(out=Wv['b'], in0=pi2, in1=ts_, op=ALU.mult), r=[PSK[bi], tsk], w=[Wk['b']])
                                S.op('pool', lambda e, Wv=Wv: e.tensor_tensor(out=Wv['wr'], in0=Wv['a'], in1=Wv['b'], op=ALU.add), r=[Wk['a'], Wk['b']], w=[Wk['wr']])
                                S.op('dve', lambda e, pi2=pi2, tc_=tc_, Wv=Wv: e.tensor_tensor(out=Wv['c'], in0=pi2, in1=tc_, op=ALU.mult), r=[PSK[bi], tck], w=[Wk['c']])
                                S.op('dve', lambda e, pr=pr, ts_=ts_, Wv=Wv: e.tensor_tensor(out=Wv['d'], in0=pr, in1=ts_, op=ALU.mult), r=[PSK[br], tsk], w=[Wk['d']])
                                S.op('pool', lambda e, Wv=Wv: e.tensor_tensor(out=Wv['wi'], in0=Wv['c'], in1=Wv['d'], op=ALU.subtract), r=[Wk['c'], Wk['d']], w=[Wk['wi']])
                                rb = r_q[:, col:col + 1].to_broadcast([128, TW])
                                S.op('dve', lambda e, Wv=Wv, rb=rb, k=k: e.tensor_tensor_scan(out=Wv['gr'], data0=rb, data1=Wv['wr'], initial=car[:, k, 0:1], op0=ALU.mult, op1=ALU.add),
                                     r=[Wk['wr'], 'r_q', 'car%d' % k], w=[Wk['gr']])
                                S.op('dve', lambda e, Wv=Wv, rb=rb, k=k: e.tensor_tensor_scan(out=Wv['gi'], data0=rb, data1=Wv['wi'], initial=car[:, k, 1:2], op0=ALU.mult, op1=ALU.add),
                                     r=[Wk['wi'], 'r_q', 'car%d' % k], w=[Wk['gi']])

                            def stage_c(cx):
                                k, par = cx['k'], cx['par']
                                dk = d * 4 + k
                                tck, tsk = 'TC%d' % dk, 'TS%d' % dk
                                WS = WA if par == 0 else WB
                                Wk = {n: kk for n, (_, kk) in WS.items()}
                                Wt = {n: t for n, (t, _) in WS.items()}
                                gl_r, gl_i = Wt['gr'][:, TW - 1:TW], Wt['gi'][:, TW - 1:TW]
                                cl, sl_ = TC[:, dk, TW - 1:TW], TS[:, dk, TW - 1:TW]
                                S.op('pool', lambda e, gl_r=gl_r, cl=cl: e.tensor_tensor(out=cw[:, 0:1], in0=gl_r, in1=cl, op=ALU.mult), r=[Wk['gr'], tck], w=['cw0'])
                                S.op('pool', lambda e, gl_i=gl_i, sl_=sl_: e.tensor_tensor(out=cw[:, 1:2], in0=gl_i, in1=sl_, op=ALU.mult), r=[Wk['gi'], tsk], w=['cw1'])
                                S.op('pool', lambda e, gl_r=gl_r, sl_=sl_: e.tensor_tensor(out=cw[:, 2:3], in0=gl_r, in1=sl_, op=ALU.mult), r=[Wk['gr'], tsk], w=['cw2'])
                                S.op('pool', lambda e, gl_i=gl_i, cl=cl: e.tensor_tensor(out=cw[:, 3:4], in0=gl_i, in1=cl, op=ALU.mult), r=[Wk['gi'], tck], w=['cw3'])
                                S.op('pool', lambda e, k=k: e.tensor_tensor(out=car[:, k, 0:1], in0=cw[:, 0:1], in1=cw[:, 1:2], op=ALU.subtract), r=['cw0', 'cw1'], w=['car%d' % k])
                                S.op('pool', lambda e, k=k: e.tensor_tensor(out=car[:, k, 1:2], in0=cw[:, 2:3], in1=cw[:, 3:4], op=ALU.add), r=['cw2', 'cw3', 'car%d' % k], w=['car%d' % k])

                            def stage_b(cx):
                                k, c0, par, ybank = cx['k'], cx['c0'], cx['par'], cx['ybank']
                                dk = d * 4 + k
                                tc_, ts_ = TC[:, dk, 0:TW], TS[:, dk, 0:TW]
                                tck, tsk = 'TC%d' % dk, 'TS%d' % dk
                                WS = WA if par == 0 else WB
                                Wv = {n: t[:, 0:TW] for n, (t, _) in WS.items()}
                                Wk = {n: kk for n, (_, kk) in WS.items()}
                                hr_, hn_ = hrb[par], hib[par]
                                hrk, hnk = 'hrb%d' % par, 'hib%d' % par
                                if d == 0:
                                    hro, hno = hr_[:, 0:TW], hn_[:, 0:TW]
                                else:
                                    hro, hno = hr_[:, 0:TW][:, ::-1], hn_[:, 0:TW][:, ::-1]
                                S.op('pool', lambda e, Wv=Wv, tc_=tc_: e.tensor_tensor(out=Wv['e'], in0=Wv['gr'], in1=tc_, op=ALU.mult), r=[Wk['gr'], tck], w=[Wk['e']])
                                S.op('pool', lambda e, Wv=Wv, ts_=ts_: e.tensor_tensor(out=Wv['f'], in0=Wv['gi'], in1=ts_, op=ALU.mult), r=[Wk['gi'], tsk], w=[Wk['f']])
                                S.op('pool', lambda e, Wv=Wv, hro=hro: e.tensor_tensor(out=hro, in0=Wv['e'], in1=Wv['f'], op=ALU.subtract), r=[Wk['e'], Wk['f']], w=[hrk])
                                S.op('dve', lambda e, Wv=Wv, ts_=ts_: e.tensor_tensor(out=Wv['g'], in0=Wv['gr'], in1=ts_, op=ALU.mult), r=[Wk['gr'], tsk], w=[Wk['g']])
                                S.op('dve', lambda e, Wv=Wv, tc_=tc_: e.tensor_tensor(out=Wv['h'], in0=Wv['gi'], in1=tc_, op=ALU.mult), r=[Wk['gi'], tck], w=[Wk['h']])
                                S.op('dve', lambda e, Wv=Wv, hno=hno: e.scalar_tensor_tensor(out=hno, in0=Wv['g'], scalar=-1.0, in1=Wv['h'], op0=ALU.mult, op1=ALU.subtract),
                                     r=[Wk['g'], Wk['h']], w=[hnk])
                                S.op('pe', lambda e, d=d, k=k, hr_=hr_, TW=TW, ybank=ybank: e.matmul(ps[ybank][:, 0:TW], lhsT=cTr[:, d, k, :], rhs=hr_[:, 0:TW], start=(k == 0), stop=False),
                                     r=['cTr', hrk], w=[PSK[ybank]])
                                S.op('pe', lambda e, d=d, k=k, hn_=hn_, TW=TW, ybank=ybank: e.matmul(ps[ybank][:, 0:TW], lhsT=cTi[:, d, k, :], rhs=hn_[:, 0:TW], start=False, stop=(k == 3)),
                                     r=['cTi', hnk], w=[PSK[ybank]])

                            def stage_e(cx):
                                c0, ybank = cx['c0'], cx['ybank']
                                if d == 0:
                                    S.op('dve', lambda e, c0=c0, TW=TW, ybank=ybank, uc=uc: e.scalar_tensor_tensor(out=ysb[:, c0:c0 + TW], in0=uS[:, c0:c0 + TW], scalar=dsk[:, uc:uc + 1],
                                                                                                               in1=ps[ybank][:, 0:TW], op0=ALU.mult, op1=ALU.add),
                                         r=['uS', 'dsk', PSK[ybank]], w=['ysb'])
                                else:
                                    S.op('dve', lambda e, c0=c0, TW=TW, ybank=ybank: e.tensor_tensor(out=ybf[:, c0:c0 + TW], in0=ysb[:, c0:c0 + TW], in1=ps[ybank][:, 0:TW], op=ALU.add),
                                         r=['ysb', PSK[ybank]], w=['ybf'])

                            stage_a(ctxs[0])
                            stage_c(ctxs[0])
                            for i_ in range(len(ctxs)):
                                if i_ + 1 < len(ctxs):
                                    stage_a(ctxs[i_ + 1])
                                stage_b(ctxs[i_])
                                if i_ + 1 < len(ctxs):
                                    stage_c(ctxs[i_ + 1])
                                if ctxs[i_]['k'] == 3:
                                    stage_e(ctxs[i_])
                            if not samp:
                                for k in range(4):
                                    st = uc * 4 + k
                                    S.op('pool', lambda e, k=k, d=d, st=st, si_=si_: e.tensor_copy(out=fin[:, si_, d, :, st], in_=car[:, k, :]), r=['car%d' % k], w=['fin'])
                    S.dma('sp', yT[uc * 128:(uc + 1) * 128, :], ybf[:], r=['ybf'], w=['yT'])
                S.dma('sp', fin_out[l], fin[:].rearrange("p a b c d -> p (a b c d)"), r=['fin'], w=['fin_out'])
            S.drain()
            if cfg.get('STOP', 99) == 6:
                break

            with contextlib.ExitStack() as ph:
                def sbp(name, shape, dt):
                    return ph.enter_context(nc.sbuf_tensor("%s_u%d" % (name, nc.next_id()), list(shape), dt))
                wg = sbp("wg", [128, 8, 2048], BF16)
                S.dma('pool', wg[:], w_glu[l].rearrange("(c p) m -> p c m", p=128), w=['wg'])
                yTt = [sbp("yTt%d" % i, [128, 8, 512], BF16) for i in range(2)]
                gst = [sbp("gst%d" % i, [128, 8, 512], BF16) for i in range(2)]
                sg = sbp("sg", [128, 512], F32)
                tg = sbp("tg", [128, 512], F32)
                mgo = [sbp("mgo%d" % i, [128, 512], BF16) for i in range(2)]
                it = 0
                for ti in range(NTILE):
                    t0 = ti * TW0
                    yt_, gt_ = yTt[ti % 2], gst[ti % 2]
                    ytk, gtk = 'yTt%d' % (ti % 2), 'gst%d' % (ti % 2)
                    S.dma('sp', yt_[:], yT[:, t0:t0 + 512].rearrange("(c p) t -> p c t", p=128), r=['yT'], w=[ytk])
                    S.dma('sp', gt_[:], gsT[:, t0:t0 + 512].rearrange("(c p) t -> p c t", p=128), r=['gsT%d' % ti], w=[gtk])
                    for ob in range(8):
                        m_ = mgo[it % 2]
                        mk_ = 'mgo%d' % (it % 2)
                        ba, bg = (0, 1) if it % 2 == 0 else (2, 3)
                        it += 1
                        for uc in range(8):
                            S.op('pe', lambda e, uc=uc, ob=ob, yt_=yt_, ba=ba: e.matmul(ps[ba][:, :], lhsT=wg[:, uc, ob * 128:(ob + 1) * 128], rhs=yt_[:, uc, :], start=(uc == 0), stop=(uc == 7)),
                                 r=['wg', ytk], w=[PSK[ba]])
                        for uc in range(8):
                            S.op('pe', lambda e, uc=uc, ob=ob, yt_=yt_, bg=bg: e.matmul(ps[bg][:, :], lhsT=wg[:, uc, 1024 + ob * 128:1024 + (ob + 1) * 128], rhs=yt_[:, uc, :], start=(uc == 0), stop=(uc == 7)),
                                 r=['wg', ytk], w=[PSK[bg]])
                        S.op('act', lambda e, bg=bg: e.activation(out=sg[:], in_=ps[bg][:, :], func=AF.Sigmoid), r=[PSK[bg]], w=['sg'])
                        S.op('dve', lambda e, ba=ba: e.tensor_tensor(out=tg[:], in0=ps[ba][:, :], in1=sg[:], op=ALU.mult), r=[PSK[ba], 'sg'], w=['tg'])
                        S.op('pool', lambda e, m_=m_, gt_=gt_, ob=ob: e.tensor_tensor(out=m_[:], in0=tg[:], in1=gt_[:, ob, :], op=ALU.mult), r=['tg', gtk], w=[mk_])
                        S.dma('sp', mixT[2048 + ob * 128:2048 + (ob + 1) * 128, t0:t0 + 512], m_[:], r=[mk_], w=['mixT_s'])
            S.drain()
            if cfg.get('STOP', 99) == 7:
                break

            with contextlib.ExitStack() as ph:
                def sbp(name, shape, dt):
                    return ph.enter_context(nc.sbuf_tensor("%s_u%d" % (name, nc.next_id()), list(shape), dt))
                slab[0] = sbp("slabA", [128, 32, 512], BF16)
                slab[1] = sbp("slabB", [128, 32, 512], BF16)
                mt = sbp("mt", [128, 32, 512], BF16)
                gbc = sbp("gbc", [128, 2, D], F32)
                S.dma('sp', gbc[:], gate_dr.rearrange("v p f -> p v f"), r=['gate_dr'], w=['gbc'])
                xo = [sbp("xo%d" % i, [128, 512], F32) for i in range(4)]
                xn_ = [sbp("xn%d" % i, [128, 512], F32) for i in range(4)]
                it = 0
                for ti in range(NTILE):
                    t0 = ti * TW0
                    v = 0 if ti < NPT else 1
                    S.dma('sp', mt[:], mixT[:, t0:t0 + 512].rearrange("(c p) t -> p c t", p=128), r=['mixT_a', 'mixT_f', 'mixT_s'], w=['mt'])
                    for so in range(8):
                        sl, sk = load_slab(w_out[l][:, so * 512:(so + 1) * 512])
                        for sub in range(4):
                            bank = it % 4
                            x_, xk = xo[it % 4], 'xo%d' % (it % 4)
                            n_, nk_ = xn_[it % 4], 'xn%d' % (it % 4)
                            it += 1
                            rows = slice(t0 + sub * 128, t0 + (sub + 1) * 128)
                            S.dma('sp', x_[:], xsrc[rows, so * 512:(so + 1) * 512], r=['xres_w'], w=[xk])
                            for c in range(32):
                                S.op('pe', lambda e, c=c, sub=sub, sl=sl, bank=bank: e.matmul(ps[bank][:, :], lhsT=mt[:, c, sub * 128:(sub + 1) * 128], rhs=sl[:, c, :], start=(c == 0), stop=(c == 31)),
                                     r=[sk, 'mt'], w=[PSK[bank]])
                            S.op('dve', lambda e, n_=n_, bank=bank, v=v, so=so: e.tensor_tensor(out=n_[:], in0=ps[bank][:, :], in1=gbc[:, v, so * 512:(so + 1) * 512], op=ALU.mult),
                                 r=[PSK[bank], 'gbc'], w=[nk_])
                            S.op('pool', lambda e, n_=n_, x_=x_: e.tensor_tensor(out=n_[:], in0=n_[:], in1=x_[:], op=ALU.add), r=[nk_, xk], w=[nk_])
                            S.dma('sp', xres[rows, so * 512:(so + 1) * 512], n_[:], r=[nk_], w=['xres_w'])
            S.drain()
            if cfg.get('STOP', 99) == 8:
                break

        with contextlib.ExitStack() as ph:
          if cfg.get('STOP', 99) == 99:
            fg = ph.enter_context(nc.sbuf_tensor("fg", [128, D], F32))
            xf = [ph.enter_context(nc.sbuf_tensor("xf%d" % i, [128, D], F32)) for i in range(2)]
            yo = [ph.enter_context(nc.sbuf_tensor("yo%d" % i, [128, D], F32)) for i in range(2)]
            junk2 = ph.enter_context(nc.sbuf_tensor("junk2", [128, D], BF16))
            ss2 = ph.enter_context(nc.sbuf_tensor("ss2", [128, 2], F32))
            S.dma('sp', fg[:], fng_rep[:, :], w=['fg'])
            for b in range(NT // 128):
                x_, xk = xf[b % 2], 'xf%d' % (b % 2)
                y_, yk = yo[b % 2], 'yo%d' % (b % 2)
                sk_ = 'ss2_%d' % (b % 2)
                S.dma('sp', x_[:], xres[b * 128:(b + 1) * 128, :], r=['xres_w'], w=[xk])
                S.op('act', lambda e, x_=x_, b=b: e.activation(out=junk2[:], in_=x_[:], func=AF.Square, accum_out=ss2[:, b % 2:b % 2 + 1]), r=[xk], w=['junk2', sk_])
                rstd_from(ss2[:, b % 2:b % 2 + 1], ss2[:, b % 2:b % 2 + 1], 1.0 / D, [sk_], sk_)
                S.op('dve', lambda e, x_=x_, y_=y_, b=b: e.scalar_tensor_tensor(out=y_[:], in0=x_[:], scalar=ss2[:, b % 2:b % 2 + 1], in1=fg[:], op0=ALU.mult, op1=ALU.mult),
                     r=[xk, sk_, 'fg'], w=[yk])
                S.dma('sp', y_out[b * 128:(b + 1) * 128, :], y_[:], r=[yk], w=['y_out'])
        S.finish()
    return nc


def _consts(cfg):
    LP, LS = cfg['LP'], cfg['LS']
    c = {}
    c['c_ident'] = np.eye(128, dtype=np.float32)
    R = np.zeros((128, 128), np.float32)
    for base in (0, 64):
        for i in range(32):
            R[base + i, base + 32 + i] = -1.0
            R[base + 32 + i, base + i] = 1.0
    c['c_RT'] = np.ascontiguousarray(R.T)
    t = np.arange(LS)
    inv = 10000.0 ** (-np.arange(32, dtype=np.float32) / 32).astype(np.float32)
    row = (t // 64).astype(np.float32)[:, None] * inv[None, :]
    col = (t % 64).astype(np.float32)[:, None] * inv[None, :]
    ang = np.concatenate([row, row, col, col], axis=1).T.astype(np.float32)
    c['c_rope'] = np.stack([np.cos(ang), np.sin(ang)]).astype(np.float32)
    c['c_jv'] = np.tile(np.arange(1, 513, dtype=np.float32)[None, :], (128, 1))
    cc = np.arange(256)
    a = 2 * np.pi * ((cc[:, None] * cc[None, :]) % 256) / 256.0
    cs = np.concatenate([np.cos(a), -np.sin(a)], axis=1) / 16.0
    c['c_cs256'] = np.ascontiguousarray(cs.reshape(2, 128, 512).transpose(1, 0, 2)).astype(np.float32)

    def dft(L):
        tt = np.arange(L, dtype=np.int64)
        a = 2 * np.pi * ((tt[:, None] * tt[None, :]) % L) / float(L)
        return (np.stack([np.cos(a), np.sin(a)]) / math.sqrt(L)).astype(ml_dtypes.bfloat16)
    c['c_dftp'] = dft(LP)
    c['c_dfts'] = dft(LS)
    return c


def _prep_core(cfg, core, x_prompt, x_sample, cache_k, cache_v, sf_re, sf_im, sb_re, sb_im, c, shared):
    NPS, LP, LS, PAST, DEPTH = cfg['NPS'], cfg['LP'], cfg['LS'], cfg['PAST'], cfg['DEPTH']
    ncore_per_b = cfg.get('CPB', 4)
    b = core // ncore_per_b
    m = dict(shared)
    xp = x_prompt[core * NPS:(core + 1) * NPS].reshape(NPS * LP, D)
    m['x_in'] = np.concatenate([xp, x_sample[b]], axis=0)
    cvs = np.stack([shared['_c_ctx'], c[b]], axis=-1)
    m['cvec'] = np.ascontiguousarray(cvs.reshape(32, 128, 2).transpose(1, 0, 2))
    m['cache_kT'] = np.ascontiguousarray(cache_k[b].transpose(0, 2, 3, 1))
    m['cache_v'] = np.ascontiguousarray(cache_v[b].reshape(DEPTH, PAST, NKV * HD))
    st = np.stack([np.stack([sf_re[b], sf_im[b]], 1), np.stack([sb_re[b], sb_im[b]], 1)], 1)
    st = st.reshape(DEPTH, 2, 2, 32, 128)
    m['st_q'] = np.ascontiguousarray(st.transpose(0, 4, 1, 2, 3))
    del m['_c_ctx']
    return m


def _prep_shared(cfg, c_ctx, norm_g, w_mod, b_mod, w_in, q_norm, k_norm, lam_re, lam_im, log_step,
                 b_re, b_im, c_re, c_im, d_skip, w_glu, w_fft, w_out, final_norm_g):
    DEPTH = cfg['DEPTH']
    f = np.float32
    m = dict(_consts(cfg))
    m['_c_ctx'] = c_ctx
    m['normg_p'] = np.ascontiguousarray(norm_g.reshape(DEPTH, 32, 128).transpose(0, 2, 1))
    m['w_mod'] = w_mod
    m['bmod_ps'] = np.ascontiguousarray(b_mod[:, :2 * D].reshape(DEPTH, 64, 128).transpose(0, 2, 1))
    m['bmodg_rep'] = np.ascontiguousarray(np.broadcast_to(b_mod[:, None, 2 * D:], (DEPTH, 128, D)))
    m['w_in'] = w_in
    m['qk_g'] = np.ascontiguousarray(np.stack([q_norm, k_norm], axis=-1))
    ls_full = np.broadcast_to(log_step[..., None], lam_re.shape)
    lam3 = np.stack([lam_re, lam_im, ls_full], axis=1).reshape(DEPTH, 3, 2, 32, 128)
    m['lam_q'] = np.ascontiguousarray(lam3.transpose(0, 1, 4, 2, 3).reshape(DEPTH, 3, 128, 64))
    lam_flat = np.stack([lam_re, lam_im, ls_full], axis=1).reshape(DEPTH, 3, 1, 2, 4096)
    m['lam_rep'] = np.ascontiguousarray(np.broadcast_to(lam_flat, (DEPTH, 3, 128, 2, 4096)))
    bT = np.zeros((DEPTH, 2, 8, 128, 2, 512), f)
    cT = np.zeros((DEPTH, 2, 8, 128, 2, 4, 128), f)
    for ri, (bb, ccm) in enumerate(((b_re, c_re), (b_im, c_im))):
        for uc in range(8):
            for gl in range(8):
                g = uc * 8 + gl
                bT[:, ri, uc, gl * 16:(gl + 1) * 16, :, gl * 64:(gl + 1) * 64] = bb[:, :, g].transpose(0, 3, 1, 2)
                k, qo = gl // 2, (gl % 2) * 64
                cT[:, ri, uc, qo:qo + 64, :, k, gl * 16:(gl + 1) * 16] = ccm[:, :, g].transpose(0, 3, 1, 2)
    m['bT_pad'] = bT
    m['cT_pad'] = cT
    m['dskip_p'] = np.ascontiguousarray(d_skip.reshape(DEPTH, 8, 128).transpose(0, 2, 1))
    m['w_glu'] = w_glu
    m['w_fft'] = w_fft
    m['w_out'] = w_out
    m['fng_rep'] = np.ascontiguousarray(np.broadcast_to(final_norm_g[None, :], (128, D)))
    return m


_NC_CACHE = {}


def run(cfg, n_cores, inputs):
    g = {k: np.asarray(v) for k, v in inputs.items()}
    key = tuple(sorted(cfg.items()))
    if key not in _NC_CACHE:
        _NC_CACHE[key] = build(cfg)
    nc = _NC_CACHE[key]
    shared = _prep_shared(cfg, g['c_ctx'], g['norm_g'], g['w_mod'], g['b_mod'], g['w_in'], g['q_norm'], g['k_norm'],
                          g['lam_re'], g['lam_im'], g['log_step'], g['b_re'], g['b_im'], g['c_re'], g['c_im'],
                          g['d_skip'], g['w_glu'], g['w_fft'], g['w_out'], g['final_norm_g'])
    in_maps = [_prep_core(cfg, core, g['x_prompt'], g['x_sample'], g['cache_k'], g['cache_v'],
                          g['state_fwd_re'], g['state_fwd_im'], g['state_bwd_re'], g['state_bwd_im'], g['c'], shared)
               for core in range(n_cores)]
    res = run_bass_kernel_spmd(nc, in_maps, core_ids=list(range(n_cores)))
    R = res.results
    DEPTH, NPS, LP, LS = cfg['DEPTH'], cfg['NPS'], cfg['LP'], cfg['LS']
    NTP = NPS * LP
    cpb = cfg.get('CPB', 4)
    y_prompt = np.concatenate([R[c_]['y_out'][:NTP].reshape(NPS, LP, D) for c_ in range(n_cores)], axis=0)
    y_sample = np.stack([R[c_]['y_out'][NTP:] for c_ in range(0, n_cores, cpb)], axis=0)
    new_k = np.concatenate([R[c_]['newk_out'].reshape(NPS, DEPTH, LP, NKV, HD) for c_ in range(n_cores)], axis=0)
    new_v = np.concatenate([R[c_]['newv_out'].reshape(NPS, DEPTH, LP, NKV, HD) for c_ in range(n_cores)], axis=0)
    fins = []
    for c_ in range(n_cores):
        fo = R[c_]['fin_out'].reshape(DEPTH, 128, NPS, 2, 2, 32)
        fins.append(fo.transpose(2, 0, 3, 4, 5, 1).reshape(NPS, DEPTH, 2, 2, 64, 64))
    fo = np.concatenate(fins, axis=0)
    outs = (y_prompt, y_sample, new_k, new_v, fo[:, :, 0, 0], fo[:, :, 0, 1], fo[:, :, 1, 0], fo[:, :, 1, 1])
    return tuple(np.ascontiguousarray(o, dtype=np.float32) for o in outs)


def kernel(**inputs):
    return run(CFG_FULL, 8, inputs)
```

```python
import math
import numpy as np
import ml_dtypes
import concourse.bass as bass
import concourse.mybir as mybir
from concourse.bass_utils import run_bass_kernel_spmd

F32 = mybir.dt.float32
BF16 = mybir.dt.bfloat16
I32 = mybir.dt.int32
ALU = mybir.AluOpType
AF = mybir.ActivationFunctionType

D = 4096
HD = 128
NH = 16
NKV = 4
INW = 9216
EPS = 1e-6
PI = math.pi
TW0 = 512

CFG_FULL = dict(DEPTH=4, NPS=4, LP=256, LS=4096, PAST=512)


class Sched:
    def __init__(self, nc, sems, dma_sems):
        self.nc = nc
        self.eng = dict(pe=nc.tensor, dve=nc.vector, act=nc.scalar, pool=nc.gpsimd, sp=nc.sync)
        self.sem = sems
        self.cnt = {e: 0 for e in sems}
        self.dsem = dma_sems
        self.dcnt = {q: [0] * len(v) for q, v in dma_sems.items()}
        self.dnext = {q: 0 for q in dma_sems}
        self.seen = {e: {} for e in self.eng}
        self.lastw = {}
        self.readers = {}
        self.semobj = {}

    def _need(self, e, r, w):
        need = {}
        def add(tok):
            if tok is None:
                return
            sid, val = tok
            if need.get(sid, 0) < val:
                need[sid] = val
        for k in r:
            add(self.lastw.get(k))
        for k in w:
            add(self.lastw.get(k))
            for t in self.readers.get(k, {}).items():
                add(t)
        eng = self.eng[e]
        for sid, val in need.items():
            if e == 'pe' and sid == 'E_pe':
                continue
            if self.seen[e].get(sid, 0) < val:
                eng.wait_ge(self.semobj[sid], val)
                self.seen[e][sid] = val

    def _record(self, tok, r, w):
        for k in w:
            self.lastw[k] = tok
            self.readers[k] = {}
        for k in r:
            d = self.readers.setdefault(k, {})
            if d.get(tok[0], 0) < tok[1]:
                d[tok[0]] = tok[1]

    def op(self, e, fn, r=(), w=()):
        self._need(e, r, w)
        inst = fn(self.eng[e])
        sid = 'E_' + e
        self.semobj[sid] = self.sem[e]
        self.cnt[e] += 1
        inst.then_inc(self.sem[e], 1)
        self._record((sid, self.cnt[e]), r, w)

    def dma(self, q, out, in_, r=(), w=(), slow=False):
        self._need(q, r, w)
        i = self.dnext[q]
        self.dnext[q] = (i + 1) % len(self.dsem[q])
        sid = 'D_%s_%d' % (q, i)
        self.semobj[sid] = self.dsem[q][i]
        if self.dcnt[q][i] > 0 and self.seen[q].get(sid, 0) < self.dcnt[q][i]:
            self.eng[q].wait_ge(self.dsem[q][i], self.dcnt[q][i])
            self.seen[q][sid] = self.dcnt[q][i]
        self.dcnt[q][i] += 16
        if slow:
            inst = self.eng[q].dma_start(out=out, in_=in_, allow_slow_non_contiguous=True)
        else:
            inst = self.eng[q].dma_start(out=out, in_=in_)
        inst.then_inc(self.dsem[q][i], 16)
        self._record((sid, self.dcnt[q][i]), r, w)

    def drain(self):
        self.finish()
        self.nc.all_engine_barrier()

    def finish(self):
        sp = self.eng['sp']
        for q, lst in self.dsem.items():
            for i, s in enumerate(lst):
                if self.dcnt[q][i] > 0:
                    sp.wait_ge(s, self.dcnt[q][i])
        for e, s in self.sem.items():
            if self.cnt[e] > 0:
                sp.wait_ge(s, self.cnt[e])


def build(cfg):
    DEPTH, NPS, LP, LS, PAST = cfg['DEPTH'], cfg['NPS'], cfg['LP'], cfg['LS'], cfg['PAST']
    NTP = NPS * LP
    NT = NTP + LS
    NKEY = NT + PAST
    assert NTP % TW0 == 0 and LS % TW0 == 0
    NTILE = NT // TW0
    NPT = NTP // TW0
    seqs = [(i * LP, LP, False) for i in range(NPS)] + [(NTP, LS, True)]

    nc = bass.Bass("TRN2", target_bir_lowering=False)

    def din(name, shape, dt=F32):
        return nc.dram_tensor(name, list(shape), dt, kind="ExternalInput").ap()

    def dout(name, shape, dt=F32):
        return nc.dram_tensor(name, list(shape), dt, kind="ExternalOutput").ap()

    def dscr(name, shape, dt=BF16):
        return nc.dram_tensor(name, list(shape), dt, kind="Internal").ap()

    x_in = din("x_in", [NT, D])
    cvec = din("cvec", [128, 32, 2])
    cache_kT = din("cache_kT", [DEPTH, NKV, 128, PAST])
    cache_v = din("cache_v", [DEPTH, PAST, NKV * HD])
    st_q = din("st_q", [DEPTH, 128, 2, 2, 32])
    normg_p = din("normg_p", [DEPTH, 128, 32])
    w_mod = din("w_mod", [DEPTH, D, 3 * D])
    bmod_ps = din("bmod_ps", [DEPTH, 128, 64])
    bmodg_rep = din("bmodg_rep", [DEPTH, 128, D])
    w_in = din("w_in", [DEPTH, D, INW])
    qk_g = din("qk_g", [DEPTH, 128, 2])
    lam_q = din("lam_q", [DEPTH, 3, 128, 64])
    lam_rep = din("lam_rep", [DEPTH, 3, 128, 2, 4096])
    bT_pad = din("bT_pad", [DEPTH, 2, 8, 128, 2, 512])
    cT_pad = din("cT_pad", [DEPTH, 2, 8, 128, 2, 4, 128])
    dskip_p = din("dskip_p", [DEPTH, 128, 8])
    w_glu = din("w_glu", [DEPTH, 1024, 2048])
    w_fft = din("w_fft", [DEPTH, 1024, 1024])
    w_out = din("w_out", [DEPTH, D, D])
    fng_rep = din("fng_rep", [128, D])
    c_ident = din("c_ident", [128, 128])
    c_RT = din("c_RT", [128, 128])
    c_rope = din("c_rope", [2, 128, LS])
    c_jv = din("c_jv", [128, 512])
    c_cs256 = din("c_cs256", [128, 2, 512])
    c_dftp = din("c_dftp", [2, LP, LP], BF16)
    c_dfts = din("c_dfts", [2, LS, LS], BF16)

    y_out = dout("y_out", [NT, D])
    newk_out = dout("newk_out", [NPS, DEPTH, LP, NKV * HD])
    newv_out = dout("newv_out", [NPS, DEPTH, LP, NKV * HD])
    fin_out = dout("fin_out", [DEPTH, 128, NPS * 2 * 2 * 32])

    xres = dscr("xres", [NT, D], F32)
    qT = dscr("qT", [NH, 128, NT])
    kT = dscr("kT", [NKV, 128, NT])
    vS = dscr("vS", [NT, NKV * HD])
    gaT = dscr("gaT", [2048, NT])
    gsT = dscr("gsT", [1024, NT])
    gfT = dscr("gfT", [1024, NT])
    uT = dscr("uT", [1024, NT])
    fT = dscr("fT", [1024, NT])
    yT = dscr("yT", [1024, NT])
    fcs = dscr("fcs", [NT, 4, 512])
    dT = dscr("dT", [1024, NT])
    mixT = dscr("mixT", [D, NT])

    import contextlib
    es = contextlib.ExitStack()
    with es:
        sems = {e: es.enter_context(nc.semaphore("E_" + e)) for e in ('pe', 'dve', 'act', 'pool')}
        dsems = {q: [es.enter_context(nc.semaphore("D_%s_%d" % (q, i))) for i in range(12)] for q in ('sp', 'pool')}
        S = Sched(nc, sems, dsems)

        def sb(name, shape, dt):
            return es.enter_context(nc.sbuf_tensor(name, list(shape), dt))

        ps = [es.enter_context(nc.psum_tensor("ps%d" % i, [128, 512], F32)) for i in range(8)]
        PSK = ['ps%d' % i for i in range(8)]

        ident = sb("ident", [128, 128], F32)
        RTb = sb("RTb", [128, 128], BF16)
        onesb = sb("onesb", [128, 128], BF16)
        onesf = sb("onesf", [128, 128], F32)
        cv = sb("cv", [128, 32, 2], F32)
        s_bf = sb("s_bf", [128, 32, 2], BF16)
        s_f = sb("s_f", [128, 32, 2], F32)
        Amod = sb("Amod", [128, 32, 2], F32)
        Bmod = sb("Bmod", [128, 32, 2], F32)
        modsb = sb("modsb", [128, 64, 2], F32)
        bmp = sb("bmp", [128, 64], F32)
        ngp = sb("ngp", [128, 32], F32)
        qkg = sb("qkg", [128, 2], F32)
        qkg2 = sb("qkg2", [128, 2], F32)
        small = sb("small", [128, 16], F32)
        negpi = sb("negpi", [128, 1], F32)

        S.dma('sp', ident[:], c_ident[:, :], w=['ident'])
        S.dma('pool', RTb[:], c_RT[:, :], w=['RTb'])
        S.dma('sp', cv[:], cvec[:, :, :], w=['cv'])
        S.op('dve', lambda e: e.memset(onesf[:], 1.0), w=['onesf'])
        S.op('dve', lambda e: e.memset(onesb[:], 1.0), w=['onesb'])
        S.op('act', lambda e: e.activation(out=s_f[:], in_=cv[:], func=AF.Silu), r=['cv'], w=['s_f'])
        S.op('dve', lambda e: e.tensor_copy(out=s_bf[:], in_=s_f[:]), r=['s_f'], w=['s_bf'])

        slab = [None, None]
        slab_i = [0]

        def load_slab(src_ap):
            i = slab_i[0] % 2
            slab_i[0] += 1
            S.dma('pool', slab[i][:], src_ap.rearrange("(c p) m -> p c m", p=128), w=['slab%d' % i])
            return slab[i], 'slab%d' % i

        class SlabStream:
            def __init__(self, srcs):
                self.srcs = srcs
                self.loaded = {}

            def get(self, i):
                for j in (i, i + 1):
                    if j < len(self.srcs) and j not in self.loaded:
                        self.loaded[j] = load_slab(self.srcs[j])
                return self.loaded.pop(i)

        def rstd_from(out_ap, in_ap, scale, keys_r, key_w, tmpk='small'):
            S.op('dve', lambda e: e.tensor_scalar(out=out_ap, in0=in_ap, scalar1=scale, scalar2=EPS,
                                                  op0=ALU.mult, op1=ALU.add), r=keys_r, w=[key_w])
            S.op('act', lambda e: e.activation(out=out_ap, in_=out_ap, func=AF.Sqrt), r=[key_w], w=[key_w])
            S.op('dve', lambda e: e.reciprocal(out=out_ap, in_=out_ap), r=[key_w], w=[key_w])

        def sin_of(out_ap, arg_ap, tmpf, tmpi, shift, keys_r, key_w, kf, ki):
            S.op('dve', lambda e: e.tensor_scalar(out=tmpf, in0=arg_ap, scalar1=shift, scalar2=1.0 / (2 * PI),
                                                  op0=ALU.add, op1=ALU.mult), r=keys_r, w=[kf])
            S.op('dve', lambda e: e.tensor_copy(out=tmpi, in_=tmpf), r=[kf], w=[ki])
            S.op('dve', lambda e: e.tensor_copy(out=tmpf, in_=tmpi), r=[ki], w=[kf])
            S.op('dve', lambda e: e.scalar_tensor_tensor(out=tmpf, in0=tmpf, scalar=-2 * PI, in1=arg_ap,
                                                         op0=ALU.mult, op1=ALU.add), r=[kf] + list(keys_r), w=[kf])
            S.op('dve', lambda e: e.tensor_scalar(out=tmpf, in0=tmpf, scalar1=shift, scalar2=3.1415925,
                                                  op0=ALU.add, op1=ALU.min), r=[kf], w=[kf])
            S.op('dve', lambda e: e.tensor_scalar(out=tmpf, in0=tmpf, scalar1=-3.1415925, scalar2=None,
                                                  op0=ALU.max), r=[kf], w=[kf])
            S.op('act', lambda e: e.activation(out=out_ap, in_=tmpf, func=AF.Sin), r=[kf], w=[key_w])

        for l in range(DEPTH):
            xsrc = x_in if l == 0 else xres
            with contextlib.ExitStack() as ph:
                def sbp(name, shape, dt):
                    return ph.enter_context(nc.sbuf_tensor("%s_u%d" % (name, nc.next_id()), list(shape), dt))
                S.dma('sp', bmp[:], bmod_ps[l], w=['bmp'])
                S.dma('sp', ngp[:], normg_p[l], w=['ngp'])
                S.dma('sp', qkg[:], qk_g[l], w=['qkg'])
                slab[0] = sbp("slabA", [128, 32, 512], BF16)
                slab[1] = sbp("slabB", [128, 32, 512], BF16)
                Srep = sbp("Srep", [128, 2, 32, 128], BF16)
                gbias = sbp("gbias", [128, D], F32)
                S.dma('sp', gbias[:], bmodg_rep[l], w=['gbias'])
                for v in range(2):
                    for c in range(32):
                        S.op('pool', lambda e, v=v, c=c: e.tensor_scalar(out=Srep[:, v, c, :], in0=onesf[:], scalar1=s_f[:, c, v:v + 1],
                                                                        scalar2=None, op0=ALU.mult),
                             r=['onesf', 's_f'], w=['Srep'])
                ss0 = SlabStream([w_mod[l][:, si * 512:(si + 1) * 512] for si in range(24)])
                for si in range(16):
                    sl, sk = ss0.get(si)
                    for j in range(4):
                        blk = si * 4 + j
                        for c in range(32):
                            S.op('pe', lambda e, c=c, j=j, blk=blk, sl=sl: e.matmul(ps[0][:, blk * 2:blk * 2 + 2], lhsT=sl[:, c, j * 128:(j + 1) * 128],
                                                                                  rhs=s_bf[:, c, :], start=(c == 0), stop=(c == 31)),
                                 r=[sk, 's_bf'], w=['ps0'])
                S.op('dve', lambda e: e.tensor_tensor(out=modsb[:], in0=ps[0][:, 0:128].rearrange("p (b v) -> p b v", v=2),
                                                      in1=bmp[:].unsqueeze(2).to_broadcast([128, 64, 2]), op=ALU.add),
                     r=['ps0', 'bmp'], w=['modsb'])
                S.op('dve', lambda e: e.tensor_copy(out=Bmod[:], in_=modsb[:, 0:32, :]), r=['modsb'], w=['Bmod'])
                S.op('dve', lambda e: e.tensor_scalar(out=Amod[:], in0=modsb[:, 32:64, :], scalar1=1.0, scalar2=None, op0=ALU.add),
                     r=['modsb'], w=['Amod'])
                S.op('dve', lambda e: e.tensor_tensor(out=Amod[:], in0=Amod[:], in1=ngp[:].unsqueeze(2).to_broadcast([128, 32, 2]), op=ALU.mult),
                     r=['Amod', 'ngp'], w=['Amod'])
                S.op('dve', lambda e: e.tensor_scalar(out=qkg2[:, 0:1], in0=qkg[:, 0:1], scalar1=HD ** -0.5, scalar2=None, op0=ALU.mult),
                     r=['qkg'], w=['qkg2'])
                S.op('dve', lambda e: e.tensor_copy(out=qkg2[:, 1:2], in_=qkg[:, 1:2]), r=['qkg', 'qkg2'], w=['qkg2'])
                gate_sb = sbp("gate_sb", [128, 2, 512], F32)
                gate_dr = dscr("gate_dr_l%d" % l, [2, 128, D], F32)
                for si in range(16, 24):
                    sl, sk = ss0.get(si)
                    g0 = (si - 16) * 512
                    for v in range(2):
                        for c in range(32):
                            S.op('pe', lambda e, c=c, v=v, sl=sl: e.matmul(ps[1 + v][:, :], lhsT=Srep[:, v, c, :], rhs=sl[:, c, :],
                                                                         start=(c == 0), stop=(c == 31)),
                                 r=[sk, 'Srep'], w=[PSK[1 + v]])
                        S.op('dve', lambda e, v=v, g0=g0: e.tensor_tensor(out=gate_sb[:, v, :], in0=ps[1 + v][:, :], in1=gbias[:, g0:g0 + 512], op=ALU.add),
                             r=[PSK[1 + v], 'gbias'], w=['gate_sb%d' % v])
                        S.dma('sp', gate_dr[v][:, g0:g0 + 512], gate_sb[:, v, :], r=['gate_sb%d' % v], w=['gate_dr'])
            S.drain()
            if cfg.get('STOP', 99) == 0:
                break

            with contextlib.ExitStack() as ph:
                def sbp(name, shape, dt):
                    return ph.enter_context(nc.sbuf_tensor("%s_u%d" % (name, nc.next_id()), list(shape), dt))
                slab[0] = sbp("slabA", [128, 32, 512], BF16)
                slab[1] = sbp("slabB", [128, 32, 512], BF16)
                xs = [sbp("xs%d" % i, [128, D], F32) for i in range(2)]
                junk = sbp("junk", [128, D], BF16)
                hT = sbp("hT", [128, 32, 512], BF16)
                ssq = sbp("ssq", [128, 4], F32)
                sq = sbp("sq", [128, 512], BF16)
                rs = sbp("rs", [128, 512], F32)
                qn = sbp("qn", [128, 512], F32)
                qb = sbp("qb", [128, 512], BF16)
                qo = [sbp("qo%d" % i, [128, 512], BF16) for i in range(2)]
                t1 = sbp("t1", [128, 512], F32)
                t2 = sbp("t2", [128, 512], F32)
                ropec = sbp("ropec", [128, 512], F32)
                ropes = sbp("ropes", [128, 512], F32)
                vb = sbp("vb", [128, 512], BF16)
                vf = sbp("vf", [128, 512], F32)
                ko = sbp("ko", [128, 512], F32)
                ev = [sbp("ev%d" % i, [128, 512], BF16) for i in range(2)]
                evi = 0
                ss1 = SlabStream([w_in[l][:, si * 512:(si + 1) * 512] for _ in range(NTILE) for si in range(18)])
                for ti in range(NTILE):
                    t0 = ti * TW0
                    v = 0 if ti < NPT else 1
                    samp = ti >= NPT
                    if samp:
                        p0 = t0 - NTP
                        S.dma('sp', ropec[:], c_rope[0][:, p0:p0 + 512], w=['ropec'])
                        S.dma('sp', ropes[:], c_rope[1][:, p0:p0 + 512], w=['ropes'])
                    for sub in range(4):
                        xt = xs[sub % 2]
                        xk = 'xs%d' % (sub % 2)
                        S.dma('sp', xt[:], xsrc[t0 + sub * 128:t0 + (sub + 1) * 128, :], w=[xk])
                        S.op('act', lambda e, xt=xt, sub=sub: e.activation(out=junk[:], in_=xt[:], func=AF.Square, accum_out=ssq[:, sub:sub + 1]),
                             r=[xk], w=['junk', 'ssq%d' % sub])
                        rstd_from(ssq[:, sub:sub + 1], ssq[:, sub:sub + 1], 1.0 / D, ['ssq%d' % sub], 'ssq%d' % sub)
                        S.op('pool', lambda e, xt=xt, sub=sub: e.tensor_scalar(out=xt[:], in0=xt[:], scalar1=ssq[:, sub:sub + 1], scalar2=None, op0=ALU.mult),
                             r=[xk, 'ssq%d' % sub], w=[xk])
                        for c0 in range(0, 32, 4):
                            bank = 4 + (c0 // 4) % 2
                            for cc in range(4):
                                c = c0 + cc
                                S.op('pe', lambda e, c=c, cc=cc, xt=xt, bank=bank: e.transpose(out=ps[bank][:, cc * 128:(cc + 1) * 128], in_=xt[:, c * 128:(c + 1) * 128], identity=ident[:]),
                                     r=[xk, 'ident'], w=[PSK[bank]])
                            for cc in range(4):
                                c = c0 + cc
                                S.op('dve', lambda e, c=c, cc=cc, bank=bank, sub=sub, v=v: e.tensor_scalar(
                                    out=hT[:, c, sub * 128:(sub + 1) * 128], in0=ps[bank][:, cc * 128:(cc + 1) * 128],
                                    scalar1=Amod[:, c, v:v + 1], scalar2=Bmod[:, c, v:v + 1], op0=ALU.mult, op1=ALU.add),
                                    r=[PSK[bank], 'Amod', 'Bmod'], w=['hT'])
                    if cfg.get('P1STOP', 0) == 1:
                        break
                    for si in range(18):
                        if cfg.get('P1STOP', 0) == 2 + si:
                            break
                        sl, sk = ss1.get(ti * 18 + si)
                        if si == 5:
                            for sub in range(4):
                                bank = sub % 2
                                for c in range(32):
                                    S.op('pe', lambda e, c=c, sub=sub, sl=sl, bank=bank: e.matmul(ps[bank][:, :], lhsT=hT[:, c, sub * 128:(sub + 1) * 128], rhs=sl[:, c, :],
                                                                                                start=(c == 0), stop=(c == 31)),
                                         r=[sk, 'hT'], w=[PSK[bank]])
                                S.op('dve', lambda e, bank=bank: e.tensor_copy(out=vf[:], in_=ps[bank][:, :]), r=[PSK[bank]], w=['vf'])
                                S.op('act', lambda e: e.activation(out=vb[:], in_=vf[:], func=AF.Copy), r=['vf'], w=['vb'])
                                S.dma('sp', vS[t0 + sub * 128:t0 + (sub + 1) * 128, :], vb[:], r=['vb'], w=['vS%d' % ti])
                                if not samp:
                                    tok = t0 + sub * 128
                                    S.dma('sp', newv_out[tok // LP, l, tok % LP:tok % LP + 128, :], vf[:], r=['vf'], w=['newv'])
                            continue
                        for j in range(4):
                            fb = si * 4 + j
                            bank = fb % 2
                            for c in range(32):
                                S.op('pe', lambda e, c=c, j=j, sl=sl, bank=bank: e.matmul(ps[bank][:, :], lhsT=sl[:, c, j * 128:(j + 1) * 128], rhs=hT[:, c, :],
                                                                                        start=(c == 0), stop=(c == 31)),
                                     r=[sk, 'hT'], w=[PSK[bank]])
                            if si <= 4:
                                isk = si == 4
                                S.op('act', lambda e, bank=bank: e.activation(out=sq[:], in_=ps[bank][:, :], func=AF.Square), r=[PSK[bank]], w=['sq'])
                                S.op('pe', lambda e: e.matmul(ps[2][:, :], lhsT=onesb[:], rhs=sq[:], start=True, stop=True), r=['sq', 'onesb'], w=['ps2'])
                                rstd_from(rs[:], ps[2][:, :], 1.0 / HD, ['ps2'], 'rs')
                                gcol = 1 if isk else 0
                                S.op('dve', lambda e, bank=bank, gcol=gcol: e.scalar_tensor_tensor(out=qn[:], in0=ps[bank][:, :], scalar=qkg2[:, gcol:gcol + 1], in1=rs[:],
                                                                                                 op0=ALU.mult, op1=ALU.mult),
                                     r=[PSK[bank], 'qkg2', 'rs'], w=['qn'])
                                if isk and not samp:
                                    for sub in range(4):
                                        S.op('pe', lambda e, sub=sub: e.transpose(out=ps[3][:, sub * 128:(sub + 1) * 128], in_=qn[:, sub * 128:(sub + 1) * 128], identity=ident[:]),
                                             r=['qn', 'ident'], w=['ps3'])
                                    S.op('dve', lambda e: e.tensor_copy(out=ko[:], in_=ps[3][:, :]), r=['ps3'], w=['ko'])
                                    for sub in range(4):
                                        tok = t0 + sub * 128
                                        S.dma('sp', newk_out[tok // LP, l, tok % LP:tok % LP + 128, j * 128:(j + 1) * 128], ko[:, sub * 128:(sub + 1) * 128],
                                              r=['ko'], w=['newk'])
                                o = qo[evi % 2]
                                ok = 'qo%d' % (evi % 2)
                                evi += 1
                                if samp:
                                    S.op('act', lambda e: e.activation(out=qb[:], in_=qn[:], func=AF.Copy), r=['qn'], w=['qb'])
                                    S.op('pe', lambda e: e.matmul(ps[3][:, :], lhsT=RTb[:], rhs=qb[:], start=True, stop=True), r=['qb', 'RTb'], w=['ps3'])
                                    S.op('pool', lambda e: e.tensor_tensor(out=t1[:], in0=qn[:], in1=ropec[:], op=ALU.mult), r=['qn', 'ropec'], w=['t1'])
                                    S.op('dve', lambda e: e.tensor_tensor(out=t2[:], in0=ps[3][:, :], in1=ropes[:], op=ALU.mult), r=['ps3', 'ropes'], w=['t2'])
                                    S.op('pool', lambda e, o=o: e.tensor_tensor(out=o[:], in0=t1[:], in1=t2[:], op=ALU.add), r=['t1', 't2'], w=[ok])
                                else:
                                    S.op('act', lambda e, o=o: e.activation(out=o[:], in_=qn[:], func=AF.Copy), r=['qn'], w=[ok])
                                if isk:
                                    S.dma('sp', kT[j][:, t0:t0 + 512], o[:], r=[ok], w=['kT%d' % ti])
                                else:
                                    S.dma('sp', qT[fb][:, t0:t0 + 512], o[:], r=[ok], w=['qT%d' % ti])
                            else:
                                o = ev[evi % 2]
                                ok = 'ev%d' % (evi % 2)
                                evi += 1
                                f0 = fb * 128
                                if 3072 <= f0 < 5120:
                                    dst, dk_, fn = gaT[f0 - 3072:f0 - 3072 + 128, t0:t0 + 512], 'gaT%d' % ti, AF.Silu
                                elif 5120 <= f0 < 6144:
                                    dst, dk_, fn = uT[f0 - 5120:f0 - 5120 + 128, t0:t0 + 512], 'uT', AF.Copy
                                elif 6144 <= f0 < 7168:
                                    dst, dk_, fn = gsT[f0 - 6144:f0 - 6144 + 128, t0:t0 + 512], 'gsT%d' % ti, AF.Silu
                                elif 7168 <= f0 < 8192:
                                    dst, dk_, fn = fT[f0 - 7168:f0 - 7168 + 128, t0:t0 + 512], 'fT%d' % ti, AF.Copy
                                else:
                                    dst, dk_, fn = gfT[f0 - 8192:f0 - 8192 + 128, t0:t0 + 512], 'gfT%d' % ti, AF.Silu
                                S.op('act', lambda e, o=o, bank=bank, fn=fn: e.activation(out=o[:], in_=ps[bank][:, :], func=fn), r=[PSK[bank]], w=[ok])
                                S.dma('sp', dst, o[:], r=[ok], w=[dk_])
            S.drain()
            if cfg.get('STOP', 99) == 1:
                break

            with contextlib.ExitStack() as ph:
                def sbp(name, shape, dt):
                    return ph.enter_context(nc.sbuf_tensor("%s_u%d" % (name, nc.next_id()), list(shape), dt))
                NKMAX = LS + PAST
                KTs = sbp("KTs", [128, NKV, NKMAX], BF16)
                Vs = sbp("Vs", [128, NKMAX // 128, NKV * HD], BF16)
                Q4 = [sbp("Q4_%d" % i, [128, 4, 128], BF16) for i in range(2)]
                G4 = [sbp("G4_%d" % i, [128, 4, 128], BF16) for i in range(2)]
                Pb = [sbp("Pb%d" % i, [128, 512], BF16) for i in range(3)]
                rl = sbp("rl", [128, 512], F32)
                pacc = [sbp("pacc%d" % i, [128, 512], F32) for i in range(2)]
                ot = sbp("ot", [128, 512], F32)
                mo = [sbp("mo%d" % i, [128, 4, 128], BF16) for i in range(2)]
                it = 0
                pi_ = 0
                for (s0, L, samp) in seqs:
                    nk = L + (PAST if samp else 0)
                    koff = PAST if samp else 0
                    if samp:
                        for kv in range(NKV):
                            S.dma('pool', KTs[:, kv, 0:PAST], cache_kT[l, kv], w=['KTs'])
                        S.dma('pool', Vs[:, 0:PAST // 128, :], cache_v[l].rearrange("(b p) f -> p b f", p=128), w=['Vs'])
                    for kv in range(NKV):
                        S.dma('sp', KTs[:, kv, koff:koff + L], kT[kv][:, s0:s0 + L], r=['kT%d' % i for i in range(NTILE)], w=['KTs'])
                    S.dma('sp', Vs[:, koff // 128:(koff + L) // 128, :], vS[s0:s0 + L, :].rearrange("(b p) f -> p b f", p=128),
                          r=['vS%d' % i for i in range(NTILE)], w=['Vs'])
                    nkb = nk // 128
                    for kv in range(NKV):
                        for qb_ in range(L // 128):
                            q0 = s0 + qb_ * 128
                            Q = Q4[it % 2]
                            G = G4[it % 2]
                            M = mo[it % 2]
                            qk_ = 'Q4_%d' % (it % 2)
                            gk_ = 'G4_%d' % (it % 2)
                            mk_ = 'mo%d' % (it % 2)
                            po, pl = (4, 5) if it % 2 == 0 else (6, 7)
                            pa, pak = pacc[it % 2], 'pacc%d' % (it % 2)
                            it += 1
                            S.dma('sp', Q[:], qT[4 * kv:4 * kv + 4, :, q0:q0 + 128].rearrange("h d t -> d h t"),
                                  r=['qT%d' % i for i in range(NTILE)], w=[qk_])
                            S.dma('sp', G[:], gaT[kv * 512:(kv + 1) * 512, q0:q0 + 128].rearrange("(h d) t -> d h t", d=128),
                                  r=['gaT%d' % i for i in range(NTILE)], w=[gk_])
                            Qf = Q[:].rearrange("d h t -> d (h t)")
                            def emit_s(kb):
                                sbank = kb % 4
                                S.op('pe', lambda e, kb=kb, kv=kv, sbank=sbank, Qf=Qf: e.matmul(ps[sbank][:, :], lhsT=KTs[:, kv, kb * 128:(kb + 1) * 128], rhs=Qf, start=True, stop=True),
                                     r=['KTs', qk_], w=[PSK[sbank]])
                            emit_s(0)
                            if nkb > 1:
                                emit_s(1)
                            for kb in range(nkb):
                                sbank = kb % 4
                                P = Pb[pi_ % 3]
                                pk_ = 'Pb%d' % (pi_ % 3)
                                pi_ += 1
                                if kb + 2 < nkb:
                                    emit_s(kb + 2)
                                S.op('act', lambda e, P=P, sbank=sbank: e.activation(out=P[:], in_=ps[sbank][:, :], func=AF.Exp), r=[PSK[sbank]], w=[pk_])
                                S.op('pe', lambda e, kb=kb, kv=kv, P=P, po=po: e.matmul(ps[po][:, :], lhsT=Vs[:, kb, kv * 128:(kv + 1) * 128], rhs=P[:], start=(kb == 0), stop=(kb == nkb - 1)),
                                     r=['Vs', pk_], w=[PSK[po]])
                                if kb == 0:
                                    S.op('dve', lambda e, P=P, pa=pa: e.tensor_copy(out=pa[:], in_=P[:]), r=[pk_], w=[pak])
                                else:
                                    S.op('dve', lambda e, P=P, pa=pa: e.tensor_tensor(out=pa[:], in0=pa[:], in1=P[:], op=ALU.add), r=[pk_, pak], w=[pak])
                            S.op('pe', lambda e, pa=pa, pl=pl: e.matmul(ps[pl][:, :], lhsT=onesf[:], rhs=pa[:], start=True, stop=True),
                                 r=['onesf', pak], w=[PSK[pl]])
                            S.op('dve', lambda e, pl=pl: e.reciprocal(out=rl[:], in_=ps[pl][:, :]), r=[PSK[pl]], w=['rl'])
                            S.op('dve', lambda e, po=po: e.tensor_tensor(out=ot[:], in0=ps[po][:, :], in1=rl[:], op=ALU.mult), r=[PSK[po], 'rl'], w=['ot'])
                            S.op('pool', lambda e, M=M, G=G: e.tensor_tensor(out=M[:].rearrange("d h t -> d (h t)"), in0=ot[:], in1=G[:].rearrange("d h t -> d (h t)"), op=ALU.mult),
                                 r=['ot', gk_], w=[mk_])
                            S.dma('sp', mixT[kv * 512:(kv + 1) * 512, q0:q0 + 128].rearrange("(h d) t -> d h t", d=128), M[:], r=[mk_], w=['mixT_a'])
            S.drain()
            if cfg.get('STOP', 99) == 2:
                break

            with contextlib.ExitStack() as ph:
                def sbp(name, shape, dt):
                    return ph.enter_context(nc.sbuf_tensor("%s_u%d" % (name, nc.next_id()), list(shape), dt))
                cs256 = sbp("cs256", [128, 2, 512], BF16)
                S.dma('pool', cs256[:], c_cs256[:, :, :], w=['cs256'])
                fTt = [sbp("fTt%d" % i, [128, 8, 512], BF16) for i in range(2)]
                fco = [sbp("fco%d" % i, [128, 4, 512], BF16) for i in range(2)]
                for ti in range(NTILE):
                    t0 = ti * TW0
                    ft = fTt[ti % 2]
                    fk = 'fTt%d' % (ti % 2)
                    S.dma('sp', ft[:], fT[:, t0:t0 + 512].rearrange("(c p) t -> p c t", p=128), r=['fT%d' % ti], w=[fk])
                    for sub in range(4):
                        fo = fco[sub % 2]
                        fok = 'fco%d' % (sub % 2)
                        for g in range(4):
                            bank = g % 2
                            for cc in range(2):
                                S.op('pe', lambda e, g=g, cc=cc, sub=sub, ft=ft, bank=bank: e.matmul(ps[bank][:, :], lhsT=ft[:, 2 * g + cc, sub * 128:(sub + 1) * 128], rhs=cs256[:, cc, :],
                                                                                                 start=(cc == 0), stop=(cc == 1)),
                                     r=[fk, 'cs256'], w=[PSK[bank]])
                            S.op('act', lambda e, g=g, fo=fo, bank=bank: e.activation(out=fo[:, g, :], in_=ps[bank][:, :], func=AF.Copy), r=[PSK[bank]], w=[fok])
                        S.dma('sp', fcs[t0 + sub * 128:t0 + (sub + 1) * 128, :, :], fo[:], r=[fok], w=['fcs'])
            S.drain()
            if cfg.get('STOP', 99) == 3:
                break
            with contextlib.ExitStack() as ph:
                def sbp(name, shape, dt):
                    return ph.enter_context(nc.sbuf_tensor("%s_u%d" % (name, nc.next_id()), list(shape), dt))
                LMAX = LS
                cosl = sbp("cosl", [128, LMAX // 128, min(512, LMAX)], BF16)
                sinl = sbp("sinl", [128, LMAX // 128, min(512, LMAX)], BF16)
                Fc = [sbp("Fc%d" % i, [128, LMAX // 128, 128], BF16) for i in range(2)]
                Fs = [sbp("Fs%d" % i, [128, LMAX // 128, 128], BF16) for i in range(2)]
                dfo = [sbp("dfo%d" % i, [128, 512], BF16) for i in range(2)]
                it = 0
                for (s0, L, samp) in seqs:
                    dsrc = c_dfts if samp else c_dftp
                    ntb = L // 128
                    KW = min(512, L)
                    for kt in range(L // KW):
                        S.dma('sp', cosl[:, 0:ntb, 0:KW], dsrc[0][:, kt * KW:(kt + 1) * KW].rearrange("(b p) k -> p b k", p=128), w=['cosl'])
                        S.dma('sp', sinl[:, 0:ntb, 0:KW], dsrc[1][:, kt * KW:(kt + 1) * KW].rearrange("(b p) k -> p b k", p=128), w=['sinl'])
                        for mb in range(8):
                            g, half = mb // 2, mb % 2
                            fc_, fs_ = Fc[it % 2], Fs[it % 2]
                            fck, fsk = 'Fc%d' % (it % 2), 'Fs%d' % (it % 2)
                            do_ = dfo[it % 2]
                            dok = 'dfo%d' % (it % 2)
                            bank = 2 + it % 2
                            it += 1
                            S.dma('sp', fc_[:, 0:ntb, :], fcs[s0:s0 + L, g, half * 128:half * 128 + 128].rearrange("(b p) m -> p b m", p=128), r=['fcs'], w=[fck])
                            S.dma('sp', fs_[:, 0:ntb, :], fcs[s0:s0 + L, g, 256 + half * 128:256 + half * 128 + 128].rearrange("(b p) m -> p b m", p=128), r=['fcs'], w=[fsk])
                            for tb in range(ntb):
                                S.op('pe', lambda e, tb=tb, fc_=fc_, bank=bank, KW=KW: e.matmul(ps[bank][:, 0:KW], lhsT=fc_[:, tb, :], rhs=cosl[:, tb, 0:KW], start=(tb == 0), stop=False),
                                     r=[fck, 'cosl'], w=[PSK[bank]])
                                S.op('pe', lambda e, tb=tb, fs_=fs_, bank=bank, KW=KW, ntb=ntb: e.matmul(ps[bank][:, 0:KW], lhsT=fs_[:, tb, :], rhs=sinl[:, tb, 0:KW], start=False, stop=(tb == ntb - 1)),
                                     r=[fsk, 'sinl'], w=[PSK[bank]])
                            S.op('act', lambda e, do_=do_, bank=bank, KW=KW: e.activation(out=do_[:, 0:KW], in_=ps[bank][:, 0:KW], func=AF.Copy), r=[PSK[bank]], w=[dok])
                            S.dma('sp', dT[mb * 128:(mb + 1) * 128, s0 + kt * KW:s0 + (kt + 1) * KW], do_[:, 0:KW], r=[dok], w=['dT'])
            S.drain()
            if cfg.get('STOP', 99) == 4:
                break
            with contextlib.ExitStack() as ph:
                def sbp(name, shape, dt):
                    return ph.enter_context(nc.sbuf_tensor("%s_u%d" % (name, nc.next_id()), list(shape), dt))
                wf = sbp("wf", [128, 8, 1024], BF16)
                S.dma('pool', wf[:], w_fft[l].rearrange("(c p) m -> p c m", p=128), w=['wf'])
                dTt = [sbp("dTt%d" % i, [128, 8, 512], BF16) for i in range(2)]
                gft = [sbp("gft%d" % i, [128, 8, 512], BF16) for i in range(2)]
                mfo = [sbp("mfo%d" % i, [128, 512], BF16) for i in range(2)]
                it = 0
                for ti in range(NTILE):
                    t0 = ti * TW0
                    dt_, gt_ = dTt[ti % 2], gft[ti % 2]
                    dtk, gtk = 'dTt%d' % (ti % 2), 'gft%d' % (ti % 2)
                    S.dma('sp', dt_[:], dT[:, t0:t0 + 512].rearrange("(c p) t -> p c t", p=128), r=['dT'], w=[dtk])
                    S.dma('sp', gt_[:], gfT[:, t0:t0 + 512].rearrange("(c p) t -> p c t", p=128), r=['gfT%d' % ti], w=[gtk])
                    for ob in range(8):
                        bank = it % 2
                        m_ = mfo[it % 2]
                        mk_ = 'mfo%d' % (it % 2)
                        it += 1
                        for mb in range(8):
                            S.op('pe', lambda e, mb=mb, ob=ob, dt_=dt_, bank=bank: e.matmul(ps[bank][:, :], lhsT=wf[:, mb, ob * 128:(ob + 1) * 128], rhs=dt_[:, mb, :], start=(mb == 0), stop=(mb == 7)),
                                 r=['wf', dtk], w=[PSK[bank]])
                        S.op('dve', lambda e, m_=m_, bank=bank, gt_=gt_, ob=ob: e.tensor_tensor(out=m_[:], in0=ps[bank][:, :], in1=gt_[:, ob, :], op=ALU.mult), r=[PSK[bank], gtk], w=[mk_])
                        S.dma('sp', mixT[3072 + ob * 128:3072 + (ob + 1) * 128, t0:t0 + 512], m_[:], r=[mk_], w=['mixT_f'])
            S.drain()
            if cfg.get('STOP', 99) == 5:
                break

            with contextlib.ExitStack() as ph:
                def sbp(name, shape, dt):
                    return ph.enter_context(nc.sbuf_tensor("%s_u%d" % (name, nc.next_id()), list(shape), dt))
                lq = sbp("lq", [128, 3, 64], F32)
                dtq = sbp("dtq", [128, 64], F32)
                r_q = sbp("r_q", [128, 64], F32)
                ang_q = sbp("ang_q", [128, 64], F32)
                stq = sbp("stq", [128, 2, 2, 32], F32)
                dsk = sbp("dsk", [128, 8], F32)
                fin = sbp("fin", [128, NPS, 2, 2, 32], F32)
                jv = sbp("jv", [128, 512], F32)
                S.dma('sp', lq[:], lam_q[l].rearrange("a p c -> p a c"), w=['lq'])
                S.dma('sp', stq[:], st_q[l], w=['stq'])
                S.dma('sp', dsk[:], dskip_p[l], w=['dsk'])
                S.dma('sp', jv[:], c_jv[:, :], w=['jv'])
                S.op('act', lambda e: e.activation(out=dtq[:], in_=lq[:, 2, :], func=AF.Exp), r=['lq'], w=['dtq'])
                S.op('dve', lambda e: e.tensor_tensor(out=r_q[:], in0=lq[:, 0, :], in1=dtq[:], op=ALU.mult), r=['lq', 'dtq'], w=['r_q'])
                S.op('act', lambda e: e.activation(out=r_q[:], in_=r_q[:], func=AF.Exp), r=['r_q'], w=['r_q'])
                S.op('dve', lambda e: e.tensor_tensor(out=ang_q[:], in0=lq[:, 1, :], in1=dtq[:], op=ALU.mult), r=['lq', 'dtq'], w=['ang_q'])

                lr_ = sbp("lr_", [128, 512], F32)
                li_ = sbp("li_", [128, 512], F32)
                ls_ = sbp("ls_", [128, 512], F32)
                z1 = sbp("z1", [128, 512], F32)
                z2 = sbp("z2", [128, 512], F32)
                z3 = sbp("z3", [128, 512], F32)
                z4 = sbp("z4", [128, 512], F32)
                zi = sbp("zi", [128, 512], I32)
                fre = sbp("fre", [128, 512], F32)
                fim = sbp("fim", [128, 512], F32)
                bre = sbp("bre", [128, 512], F32)
                bim = sbp("bim", [128, 512], F32)
                lBr = sbp("lBr", [128, 2, 512], BF16)
                lBi = sbp("lBi", [128, 2, 512], BF16)
                cTr = sbp("cTr", [128, 2, 4, 128], BF16)
                cTi = sbp("cTi", [128, 2, 4, 128], BF16)
                TC = sbp("TC", [128, 8, 512], F32)
                TS = sbp("TS", [128, 8, 512], F32)
                targ = sbp("targ", [128, 512], F32)
                ttf = sbp("ttf", [128, 512], F32)
                tti = sbp("tti", [128, 512], I32)
                uS = sbp("uS", [128, NT], BF16)
                ysb = sbp("ysb", [128, NT], F32)
                ybf = sbp("ybf", [128, NT], BF16)
                car = sbp("car", [128, 4, 2], F32)
                cw = sbp("cw", [128, 8], F32)
                W = {n: sbp("W" + n, [128, 512], F32) for n in ('a', 'b', 'c', 'd', 'wr', 'wi', 'gr', 'gi', 'e', 'f', 'g', 'h')}
                WB = dict(a=(lr_, 'lr_'), b=(li_, 'li_'), c=(ls_, 'ls_'), d=(z1, 'z1'), wr=(z2, 'z2'), wi=(z3, 'z3'), gr=(z4, 'z4'),
                          gi=(fre, 'fre'), e=(fim, 'fim'), f=(bre, 'bre'), g=(bim, 'bim'), h=(ttf, 'ttf'))
                WA = {n: (t, 'W' + n) for n, t in W.items()}
                hrb = [sbp("hrb%d" % i, [128, 512], BF16) for i in range(2)]
                hib = [sbp("hib%d" % i, [128, 512], BF16) for i in range(2)]

                for uc in range(8):
                    S.dma('pool', cTr[:], cT_pad[l, 0, uc], w=['cTr'])
                    S.dma('pool', cTi[:], cT_pad[l, 1, uc], w=['cTi'])
                    S.dma('sp', uS[:], uT[uc * 128:(uc + 1) * 128, :], r=['uT'], w=['uS'])
                    for dz in range(2):
                        S.dma('sp', lr_[:], lam_rep[l, 0][:, dz, uc * 512:(uc + 1) * 512], w=['lr_'])
                        S.dma('sp', li_[:], lam_rep[l, 1][:, dz, uc * 512:(uc + 1) * 512], w=['li_'])
                        S.dma('sp', ls_[:], lam_rep[l, 2][:, dz, uc * 512:(uc + 1) * 512], w=['ls_'])
                        S.dma('sp', bre[:], bT_pad[l, 0, uc][:, dz, :], w=['bre'])
                        S.dma('sp', bim[:], bT_pad[l, 1, uc][:, dz, :], w=['bim'])
                        fl = lambda t: t[:]
                        S.op('act', lambda e: e.activation(out=fl(z1), in_=fl(ls_), func=AF.Exp), r=['ls_'], w=['z1'])
                        S.op('dve', lambda e: e.tensor_tensor(out=fl(z2), in0=fl(lr_), in1=fl(z1), op=ALU.mult), r=['lr_', 'z1'], w=['z2'])
                        S.op('act', lambda e: e.activation(out=fl(z2), in_=fl(z2), func=AF.Exp), r=['z2'], w=['z2'])
                        S.op('dve', lambda e: e.tensor_tensor(out=fl(z1), in0=fl(li_), in1=fl(z1), op=ALU.mult), r=['li_', 'z1'], w=['z1'])
                        sin_of(fl(z3), fl(z1), fl(z4), fl(zi), 0.0, ['z1'], 'z3', 'z4', 'zi')
                        S.op('dve', lambda e: e.tensor_tensor(out=fl(z3), in0=fl(z3), in1=fl(z2), op=ALU.mult), r=['z3', 'z2'], w=['z3'])
                        sin_of(fl(fre), fl(z1), fl(z4), fl(zi), PI / 2, ['z1'], 'fre', 'z4', 'zi')
                        S.op('dve', lambda e: e.tensor_tensor(out=fl(z2), in0=fl(fre), in1=fl(z2), op=ALU.mult), r=['fre', 'z2'], w=['z2'])
                        S.op('dve', lambda e: e.tensor_scalar(out=fl(z2), in0=fl(z2), scalar1=-1.0, scalar2=None, op0=ALU.add), r=['z2'], w=['z2'])
                        S.op('dve', lambda e: e.tensor_tensor(out=fl(z1), in0=fl(lr_), in1=fl(lr_), op=ALU.mult), r=['lr_', 'z1'], w=['z1'])
                        S.op('dve', lambda e: e.tensor_tensor(out=fl(z4), in0=fl(li_), in1=fl(li_), op=ALU.mult), r=['li_'], w=['z4'])
                        S.op('dve', lambda e: e.tensor_tensor(out=fl(z1), in0=fl(z1), in1=fl(z4), op=ALU.add), r=['z1', 'z4'], w=['z1'])
                        S.op('dve', lambda e: e.reciprocal(out=fl(z1), in_=fl(z1)), r=['z1'], w=['z1'])
                        S.op('dve', lambda e: e.tensor_tensor(out=fl(fre), in0=fl(z2), in1=fl(lr_), op=ALU.mult), r=['z2', 'lr_'], w=['fre'])
                        S.op('dve', lambda e: e.tensor_tensor(out=fl(z4), in0=fl(z3), in1=fl(li_), op=ALU.mult), r=['z3', 'li_'], w=['z4'])
                        S.op('dve', lambda e: e.tensor_tensor(out=fl(fre), in0=fl(fre), in1=fl(z4), op=ALU.add), r=['fre', 'z4'], w=['fre'])
                        S.op('dve', lambda e: e.tensor_tensor(out=fl(fre), in0=fl(fre), in1=fl(z1), op=ALU.mult), r=['fre', 'z1'], w=['fre'])
                        S.op('dve', lambda e: e.tensor_tensor(out=fl(fim), in0=fl(z3), in1=fl(lr_), op=ALU.mult), r=['z3', 'lr_'], w=['fim'])
                        S.op('dve', lambda e: e.tensor_tensor(out=fl(z4), in0=fl(z2), in1=fl(li_), op=ALU.mult), r=['z2', 'li_'], w=['z4'])
                        S.op('dve', lambda e: e.tensor_tensor(out=fl(fim), in0=fl(fim), in1=fl(z4), op=ALU.subtract), r=['fim', 'z4'], w=['fim'])
                        S.op('dve', lambda e: e.tensor_tensor(out=fl(fim), in0=fl(fim), in1=fl(z1), op=ALU.mult), r=['fim', 'z1'], w=['fim'])
                        S.op('dve', lambda e: e.tensor_tensor(out=fl(z1), in0=fl(fre), in1=fl(bre), op=ALU.mult), r=['fre', 'bre'], w=['z1'])
                        S.op('dve', lambda e: e.tensor_tensor(out=fl(z2), in0=fl(fim), in1=fl(bim), op=ALU.mult), r=['fim', 'bim'], w=['z2'])
                        S.op('dve', lambda e, dz=dz: e.tensor_tensor(out=lBr[:, dz, :], in0=fl(z1), in1=fl(z2), op=ALU.subtract), r=['z1', 'z2'], w=['lBr'])
                        S.op('dve', lambda e: e.tensor_tensor(out=fl(z1), in0=fl(fre), in1=fl(bim), op=ALU.mult), r=['fre', 'bim'], w=['z1'])
                        S.op('dve', lambda e: e.tensor_tensor(out=fl(z2), in0=fl(fim), in1=fl(bre), op=ALU.mult), r=['fim', 'bre'], w=['z2'])
                        S.op('dve', lambda e, dz=dz: e.tensor_tensor(out=lBi[:, dz, :], in0=fl(z1), in1=fl(z2), op=ALU.add), r=['z1', 'z2'], w=['lBi'])
                    for d in range(2):
                        for k in range(4):
                            dk = d * 4 + k
                            col = d * 32 + uc * 4 + k
                            S.op('dve', lambda e, col=col: e.tensor_scalar(out=targ[:], in0=jv[:], scalar1=ang_q[:, col:col + 1], scalar2=None, op0=ALU.mult),
                                 r=['jv', 'ang_q'], w=['targ'])
                            sin_of(TS[:, dk, :], targ[:], ttf[:], tti[:], 0.0, ['targ'], 'TS%d' % dk, 'ttf', 'tti')
                            sin_of(TC[:, dk, :], targ[:], ttf[:], tti[:], PI / 2, ['targ'], 'TC%d' % dk, 'ttf', 'tti')
                    hi_ = 0
                    yb_ = 0
                    for si_, (s0, L, samp) in enumerate(seqs):
                        TW = min(512, L)
                        ntl = L // TW
                        for d in range(2):
                            order = range(ntl) if d == 0 else range(ntl - 1, -1, -1)
                            for k in range(4):
                                st = uc * 4 + k
                                if samp:
                                    S.op('pool', lambda e, k=k, d=d, st=st: e.tensor_copy(out=car[:, k, :], in_=stq[:, d, :, st]), r=['stq'], w=['car%d' % k])
                                else:
                                    S.op('pool', lambda e, k=k: e.memset(car[:, k, :], 0.0), w=['car%d' % k])
                            its = [(tl, k) for tl in order for k in range(4)]
                            ctxs = []
                            for (tl, k) in its:
                                ctxs.append(dict(tl=tl, k=k, c0=s0 + tl * TW, par=hi_ % 2, ybank=6 + (yb_ % 2)))
                                hi_ += 1
                                if k == 3:
                                    yb_ += 1

                            def stage_a(cx):
                                k, c0, par = cx['k'], cx['c0'], cx['par']
                                dk = d * 4 + k
                                col = d * 32 + uc * 4 + k
                                br, bi = (0, 1) if par == 0 else (2, 3)
                                S.op('pe', lambda e, d=d, k=k, c0=c0, TW=TW, br=br: e.matmul(ps[br][:, 0:TW], lhsT=lBr[:, d, k * 128:(k + 1) * 128], rhs=uS[:, c0:c0 + TW], start=True, stop=True),
                                     r=['lBr', 'uS'], w=[PSK[br]])
                                S.op('pe', lambda e, d=d, k=k, c0=c0, TW=TW, bi=bi: e.matmul(ps[bi][:, 0:TW], lhsT=lBi[:, d, k * 128:(k + 1) * 128], rhs=uS[:, c0:c0 + TW], start=True, stop=True),
                                     r=['lBi', 'uS'], w=[PSK[bi]])
                                if d == 0:
                                    pr, pi2 = ps[br][:, 0:TW], ps[bi][:, 0:TW]
                                else:
                                    pr, pi2 = ps[br][:, 0:TW][:, ::-1], ps[bi][:, 0:TW][:, ::-1]
                                tc_, ts_ = TC[:, dk, 0:TW], TS[:, dk, 0:TW]
                                tck, tsk = 'TC%d' % dk, 'TS%d' % dk
                                WS = WA if par == 0 else WB
                                Wv = {n: t[:, 0:TW] for n, (t, _) in WS.items()}
                                Wk = {n: kk for n, (_, kk) in WS.items()}
                                Wt = {n: t for n, (t, _) in WS.items()}
                                S.op('dve', lambda e, pr=pr, tc_=tc_, Wv=Wv: e.tensor_tensor(out=Wv['a'], in0=pr, in1=tc_, op=ALU.mult), r=[PSK[br], tck], w=[Wk['a']])
                                S.op('dve', lambda e, pi2=pi2, ts_=ts_, Wv=Wv: e.tensor_tensor(out=Wv['b'], in0=pi2, in1=ts_, op=ALU.mult), r=[PSK[bi], tsk], w=[Wk['b']])
                                S.op('pool', lambda e, Wv=Wv: e.tensor_tensor(out=Wv['wr'], in0=Wv['a'], in1=Wv['b'], op=ALU.add), r=[Wk['a'], Wk['b']], w=[Wk['wr']])
                                S.op('dve', lambda e, pi2=pi2, tc_=tc_, Wv=Wv: e.tensor_tensor(out=Wv['c'], in0=pi2, in1=tc_, op=ALU.mult), r=[PSK[bi], tck], w=[Wk['c']])
                                S.op('dve', lambda e, pr=pr, ts_=ts_, Wv=Wv: e.tensor_tensor(out=Wv['d'], in0=pr, in1=ts_, op=ALU.mult), r=[PSK[br], tsk], w=[Wk['d']])
                                S.op('pool', lambda e, Wv=Wv: e.tensor_tensor(out=Wv['wi'], in0=Wv['c'], in1=Wv['d'], op=ALU.subtract), r=[Wk['c'], Wk['d']], w=[Wk['wi']])
                                rb = r_q[:, col:col + 1].to_broadcast([128, TW])
                                S.op('dve', lambda e, Wv=Wv, rb=rb, k=k: e.tensor_tensor_scan(out=Wv['gr'], data0=rb, data1=Wv['wr'], initial=car[:, k, 0:1], op0=ALU.mult, op1=ALU.add),
                                     r=[Wk['wr'], 'r_q', 'car%d' % k], w=[Wk['gr']])
                                S.op('dve', lambda e, Wv=Wv, rb=rb, k=k: e.tensor_tensor_scan(out=Wv['gi'], data0=rb, data1=Wv['wi'], initial=car[:, k, 1:2], op0=ALU.mult, op1=ALU.add),
                                     r=[Wk['wi'], 'r_q', 'car%d' % k], w=[Wk['gi']])

                            def stage_c(cx):
                                k, par = cx['k'], cx['par']
                                dk = d * 4 + k
                                tck, tsk = 'TC%d' % dk, 'TS%d' % dk
                                WS = WA if par == 0 else WB
                                Wk = {n: kk for n, (_, kk) in WS.items()}
                                Wt = {n: t for n, (t, _) in WS.items()}
                                gl_r, gl_i = Wt['gr'][:, TW - 1:TW], Wt['gi'][:, TW - 1:TW]
                                cl, sl_ = TC[:, dk, TW - 1:TW], TS[:, dk, TW - 1:TW]
                                S.op('pool', lambda e, gl_r=gl_r, cl=cl: e.tensor_tensor(out=cw[:, 0:1], in0=gl_r, in1=cl, op=ALU.mult), r=[Wk['gr'], tck], w=['cw0'])
                                S.op('pool', lambda e, gl_i=gl_i, sl_=sl_: e.tensor_tensor(out=cw[:, 1:2], in0=gl_i, in1=sl_, op=ALU.mult), r=[Wk['gi'], tsk], w=['cw1'])
                                S.op('pool', lambda e, gl_r=gl_r, sl_=sl_: e.tensor_tensor(out=cw[:, 2:3], in0=gl_r, in1=sl_, op=ALU.mult), r=[Wk['gr'], tsk], w=['cw2'])
                                S.op('pool', lambda e, gl_i=gl_i, cl=cl: e.tensor_tensor(out=cw[:, 3:4], in0=gl_i, in1=cl, op=ALU.mult), r=[Wk['gi'], tck], w=['cw3'])
                                S.op('pool', lambda e, k=k: e.tensor_tensor(out=car[:, k, 0:1], in0=cw[:, 0:1], in1=cw[:, 1:2], op=ALU.subtract), r=['cw0', 'cw1'], w=['car%d' % k])
                                S.op('pool', lambda e, k=k: e.tensor_tensor(out=car[:, k, 1:2], in0=cw[:, 2:3], in1=cw[:, 3:4], op=ALU.add), r=['cw2', 'cw3', 'car%d' % k], w=['car%d' % k])

                            def stage_b(cx):
                                k, c0, par, ybank = cx['k'], cx['c0'], cx['par'], cx['ybank']
                                dk = d * 4 + k
                                tc_, ts_ = TC[:, dk, 0:TW], TS[:, dk, 0:TW]
                                tck, tsk = 'TC%d' % dk, 'TS%d' % dk
                                WS = WA if par == 0 else WB
                                Wv = {n: t[:, 0:TW] for n, (t, _) in WS.items()}
                                Wk = {n: kk for n, (_, kk) in WS.items()}
                                hr_, hn_ = hrb[par], hib[par]
                                hrk, hnk = 'hrb%d' % par, 'hib%d' % par
                                if d == 0:
                                    hro, hno = hr_[:, 0:TW], hn_[:, 0:TW]
                                else:
                                    hro, hno = hr_[:, 0:TW][:, ::-1], hn_[:, 0:TW][:, ::-1]
                                S.op('pool', lambda e, Wv=Wv, tc_=tc_: e.tensor_tensor(out=Wv['e'], in0=Wv['gr'], in1=tc_, op=ALU.mult), r=[Wk['gr'], tck], w=[Wk['e']])
                                S.op('pool', lambda e, Wv=Wv, ts_=ts_: e.tensor_tensor(out=Wv['f'], in0=Wv['gi'], in1=ts_, op=ALU.mult), r=[Wk['gi'], tsk], w=[Wk['f']])
                                S.op('pool', lambda e, Wv=Wv, hro=hro: e.tensor_tensor(out=hro, in0=Wv['e'], in1=Wv['f'], op=ALU.subtract), r=[Wk['e'], Wk['f']], w=[hrk])
                                S.op('dve', lambda e, Wv=Wv, ts_=ts_: e.tensor_tensor(out=Wv['g'], in0=Wv['gr'], in1=ts_, op=ALU.mult), r=[Wk['gr'], tsk], w=[Wk['g']])
                                S.op('dve', lambda e, Wv=Wv, tc_=tc_: e.tensor_tensor(out=Wv['h'], in0=Wv['gi'], in1=tc_, op=ALU.mult), r=[Wk['gi'], tck], w=[Wk['h']])
                                S.op('dve', lambda e, Wv=Wv, hno=hno: e.scalar_tensor_tensor(out=hno, in0=Wv['g'], scalar=-1.0, in1=Wv['h'], op0=ALU.mult, op1=ALU.subtract),
                                     r=[Wk['g'], Wk['h']], w=[hnk])
                                S.op('pe', lambda e, d=d, k=k, hr_=hr_, TW=TW, ybank=ybank: e.matmul(ps[ybank][:, 0:TW], lhsT=cTr[:, d, k, :], rhs=hr_[:, 0:TW], start=(k == 0), stop=False),
                                     r=['cTr', hrk], w=[PSK[ybank]])
                                S.op('pe', lambda e, d=d, k=k, hn_=hn_, TW=TW, ybank=ybank: e.matmul(ps[ybank][:, 0:TW], lhsT=cTi[:, d, k, :], rhs=hn_[:, 0:TW], start=False, stop=(k == 3)),
                                     r=['cTi', hnk], w=[PSK[ybank]])

                            def stage_e(cx):
                                c0, ybank = cx['c0'], cx['ybank']
                                if d == 0:
                                    S.op('dve', lambda e, c0=c0, TW=TW, ybank=ybank, uc=uc: e.scalar_tensor_tensor(out=ysb[:, c0:c0 + TW], in0=uS[:, c0:c0 + TW], scalar=dsk[:, uc:uc + 1],
                                                                                                               in1=ps[ybank][:, 0:TW], op0=ALU.mult, op1=ALU.add),
                                         r=['uS', 'dsk', PSK[ybank]], w=['ysb'])
                                else:
                                    S.op('dve', lambda e, c0=c0, TW=TW, ybank=ybank: e.tensor_tensor(out=ybf[:, c0:c0 + TW], in0=ysb[:, c0:c0 + TW], in1=ps[ybank][:, 0:TW], op=ALU.add),
                                         r=['ysb', PSK[ybank]], w=['ybf'])

                            stage_a(ctxs[0])
                            stage_c(ctxs[0])
                            for i_ in range(len(ctxs)):
                                if i_ + 1 < len(ctxs):
                                    stage_a(ctxs[i_ + 1])
                                stage_b(ctxs[i_])
                                if i_ + 1 < len(ctxs):
                                    stage_c(ctxs[i_ + 1])
                                if ctxs[i_]['k'] == 3:
                                    stage_e(ctxs[i_])
                            if not samp:
                                for k in range(4):
                                    st = uc * 4 + k
                                    S.op('pool', lambda e, k=k, d=d, st=st, si_=si_: e.tensor_copy(out=fin[:, si_, d, :, st], in_=car[:, k, :]), r=['car%d' % k], w=['fin'])
                    S.dma('sp', yT[uc * 128:(uc + 1) * 128, :], ybf[:], r=['ybf'], w=['yT'])
                S.dma('sp', fin_out[l], fin[:].rearrange("p a b c d -> p (a b c d)"), r=['fin'], w=['fin_out'])
            S.drain()
            if cfg.get('STOP', 99) == 6:
                break

            with contextlib.ExitStack() as ph:
                def sbp(name, shape, dt):
                    return ph.enter_context(nc.sbuf_tensor("%s_u%d" % (name, nc.next_id()), list(shape), dt))
                wg = sbp("wg", [128, 8, 2048], BF16)
                S.dma('pool', wg[:], w_glu[l].rearrange("(c p) m -> p c m", p=128), w=['wg'])
                yTt = [sbp("yTt%d" % i, [128, 8, 512], BF16) for i in range(2)]
                gst = [sbp("gst%d" % i, [128, 8, 512], BF16) for i in range(2)]
                sg = sbp("sg", [128, 512], F32)
                tg = sbp("tg", [128, 512], F32)
                mgo = [sbp("mgo%d" % i, [128, 512], BF16) for i in range(2)]
                it = 0
                for ti in range(NTILE):
                    t0 = ti * TW0
                    yt_, gt_ = yTt[ti % 2], gst[ti % 2]
                    ytk, gtk = 'yTt%d' % (ti % 2), 'gst%d' % (ti % 2)
                    S.dma('sp', yt_[:], yT[:, t0:t0 + 512].rearrange("(c p) t -> p c t", p=128), r=['yT'], w=[ytk])
                    S.dma('sp', gt_[:], gsT[:, t0:t0 + 512].rearrange("(c p) t -> p c t", p=128), r=['gsT%d' % ti], w=[gtk])
                    for ob in range(8):
                        m_ = mgo[it % 2]
                        mk_ = 'mgo%d' % (it % 2)
                        ba, bg = (0, 1) if it % 2 == 0 else (2, 3)
                        it += 1
                        for uc in range(8):
                            S.op('pe', lambda e, uc=uc, ob=ob, yt_=yt_, ba=ba: e.matmul(ps[ba][:, :], lhsT=wg[:, uc, ob * 128:(ob + 1) * 128], rhs=yt_[:, uc, :], start=(uc == 0), stop=(uc == 7)),
                                 r=['wg', ytk], w=[PSK[ba]])
                        for uc in range(8):
                            S.op('pe', lambda e, uc=uc, ob=ob, yt_=yt_, bg=bg: e.matmul(ps[bg][:, :], lhsT=wg[:, uc, 1024 + ob * 128:1024 + (ob + 1) * 128], rhs=yt_[:, uc, :], start=(uc == 0), stop=(uc == 7)),
                                 r=['wg', ytk], w=[PSK[bg]])
                        S.op('act', lambda e, bg=bg: e.activation(out=sg[:], in_=ps[bg][:, :], func=AF.Sigmoid), r=[PSK[bg]], w=['sg'])
                        S.op('dve', lambda e, ba=ba: e.tensor_tensor(out=tg[:], in0=ps[ba][:, :], in1=sg[:], op=ALU.mult), r=[PSK[ba], 'sg'], w=['tg'])
                        S.op('pool', lambda e, m_=m_, gt_=gt_, ob=ob: e.tensor_tensor(out=m_[:], in0=tg[:], in1=gt_[:, ob, :], op=ALU.mult), r=['tg', gtk], w=[mk_])
                        S.dma('sp', mixT[2048 + ob * 128:2048 + (ob + 1) * 128, t0:t0 + 512], m_[:], r=[mk_], w=['mixT_s'])
            S.drain()
            if cfg.get('STOP', 99) == 7:
                break

            with contextlib.ExitStack() as ph:
                def sbp(name, shape, dt):
                    return ph.enter_context(nc.sbuf_tensor("%s_u%d" % (name, nc.next_id()), list(shape), dt))
                slab[0] = sbp("slabA", [128, 32, 512], BF16)
                slab[1] = sbp("slabB", [128, 32, 512], BF16)
                mt = sbp("mt", [128, 32, 512], BF16)
                gbc = sbp("gbc", [128, 2, D], F32)
                S.dma('sp', gbc[:], gate_dr.rearrange("v p f -> p v f"), r=['gate_dr'], w=['gbc'])
                xo = [sbp("xo%d" % i, [128, 512], F32) for i in range(4)]
                xn_ = [sbp("xn%d" % i, [128, 512], F32) for i in range(4)]
                it = 0
                ss3 = SlabStream([w_out[l][:, so * 512:(so + 1) * 512] for _ in range(NTILE) for so in range(8)])
                for ti in range(NTILE):
                    t0 = ti * TW0
                    v = 0 if ti < NPT else 1
                    S.dma('sp', mt[:], mixT[:, t0:t0 + 512].rearrange("(c p) t -> p c t", p=128), r=['mixT_a', 'mixT_f', 'mixT_s'], w=['mt'])
                    for so in range(8):
                        sl, sk = ss3.get(ti * 8 + so)
                        for sub in range(4):
                            bank = it % 4
                            x_, xk = xo[it % 4], 'xo%d' % (it % 4)
                            n_, nk_ = xn_[it % 4], 'xn%d' % (it % 4)
                            it += 1
                            rows = slice(t0 + sub * 128, t0 + (sub + 1) * 128)
                            S.dma('sp', x_[:], xsrc[rows, so * 512:(so + 1) * 512], r=['xres_w'], w=[xk])
                            for c in range(32):
                                S.op('pe', lambda e, c=c, sub=sub, sl=sl, bank=bank: e.matmul(ps[bank][:, :], lhsT=mt[:, c, sub * 128:(sub + 1) * 128], rhs=sl[:, c, :], start=(c == 0), stop=(c == 31)),
                                     r=[sk, 'mt'], w=[PSK[bank]])
                            S.op('dve', lambda e, n_=n_, bank=bank, v=v, so=so: e.tensor_tensor(out=n_[:], in0=ps[bank][:, :], in1=gbc[:, v, so * 512:(so + 1) * 512], op=ALU.mult),
                                 r=[PSK[bank], 'gbc'], w=[nk_])
                            S.op('pool', lambda e, n_=n_, x_=x_: e.tensor_tensor(out=n_[:], in0=n_[:], in1=x_[:], op=ALU.add), r=[nk_, xk], w=[nk_])
                            S.dma('sp', xres[rows, so * 512:(so + 1) * 512], n_[:], r=[nk_], w=['xres_w'])
            S.drain()
            if cfg.get('STOP', 99) == 8:
                break

        with contextlib.ExitStack() as ph:
          if cfg.get('STOP', 99) == 99:
            fg = ph.enter_context(nc.sbuf_tensor("fg", [128, D], F32))
            xf = [ph.enter_context(nc.sbuf_tensor("xf%d" % i, [128, D], F32)) for i in range(2)]
            yo = [ph.enter_context(nc.sbuf_tensor("yo%d" % i, [128, D], F32)) for i in range(2)]
            junk2 = ph.enter_context(nc.sbuf_tensor("junk2", [128, D], BF16))
            ss2 = ph.enter_context(nc.sbuf_tensor("ss2", [128, 2], F32))
            S.dma('sp', fg[:], fng_rep[:, :], w=['fg'])
            for b in range(NT // 128):
                x_, xk = xf[b % 2], 'xf%d' % (b % 2)
                y_, yk = yo[b % 2], 'yo%d' % (b % 2)
                sk_ = 'ss2_%d' % (b % 2)
                S.dma('sp', x_[:], xres[b * 128:(b + 1) * 128, :], r=['xres_w'], w=[xk])
                S.op('act', lambda e, x_=x_, b=b: e.activation(out=junk2[:], in_=x_[:], func=AF.Square, accum_out=ss2[:, b % 2:b % 2 + 1]), r=[xk], w=['junk2', sk_])
                rstd_from(ss2[:, b % 2:b % 2 + 1], ss2[:, b % 2:b % 2 + 1], 1.0 / D, [sk_], sk_)
                S.op('dve', lambda e, x_=x_, y_=y_, b=b: e.scalar_tensor_tensor(out=y_[:], in0=x_[:], scalar=ss2[:, b % 2:b % 2 + 1], in1=fg[:], op0=ALU.mult, op1=ALU.mult),
                     r=[xk, sk_, 'fg'], w=[yk])
                S.dma('sp', y_out[b * 128:(b + 1) * 128, :], y_[:], r=[yk], w=['y_out'])
        S.finish()
    return nc


def _consts(cfg):
    LP, LS = cfg['LP'], cfg['LS']
    c = {}
    c['c_ident'] = np.eye(128, dtype=np.float32)
    R = np.zeros((128, 128), np.float32)
    for base in (0, 64):
        for i in range(32):
            R[base + i, base + 32 + i] = -1.0
            R[base + 32 + i, base + i] = 1.0
    c['c_RT'] = np.ascontiguousarray(R.T)
    t = np.arange(LS)
    inv = 10000.0 ** (-np.arange(32, dtype=np.float32) / 32).astype(np.float32)
    row = (t // 64).astype(np.float32)[:, None] * inv[None, :]
    col = (t % 64).astype(np.float32)[:, None] * inv[None, :]
    ang = np.concatenate([row, row, col, col], axis=1).T.astype(np.float32)
    c['c_rope'] = np.stack([np.cos(ang), np.sin(ang)]).astype(np.float32)
    c['c_jv'] = np.tile(np.arange(1, 513, dtype=np.float32)[None, :], (128, 1))
    cc = np.arange(256)
    a = 2 * np.pi * ((cc[:, None] * cc[None, :]) % 256) / 256.0
    cs = np.concatenate([np.cos(a), -np.sin(a)], axis=1) / 16.0
    c['c_cs256'] = np.ascontiguousarray(cs.reshape(2, 128, 512).transpose(1, 0, 2)).astype(np.float32)

    def dft(L):
        tt = np.arange(L, dtype=np.int64)
        a = 2 * np.pi * ((tt[:, None] * tt[None, :]) % L) / float(L)
        return (np.stack([np.cos(a), np.sin(a)]) / math.sqrt(L)).astype(ml_dtypes.bfloat16)
    c['c_dftp'] = dft(LP)
    c['c_dfts'] = dft(LS)
    return c


def _prep_core(cfg, core, x_prompt, x_sample, cache_k, cache_v, sf_re, sf_im, sb_re, sb_im, c, shared):
    NPS, LP, LS, PAST, DEPTH = cfg['NPS'], cfg['LP'], cfg['LS'], cfg['PAST'], cfg['DEPTH']
    ncore_per_b = cfg.get('CPB', 4)
    b = core // ncore_per_b
    m = dict(shared)
    xp = x_prompt[core * NPS:(core + 1) * NPS].reshape(NPS * LP, D)
    m['x_in'] = np.concatenate([xp, x_sample[b]], axis=0)
    cvs = np.stack([shared['_c_ctx'], c[b]], axis=-1)
    m['cvec'] = np.ascontiguousarray(cvs.reshape(32, 128, 2).transpose(1, 0, 2))
    m['cache_kT'] = np.ascontiguousarray(cache_k[b].transpose(0, 2, 3, 1))
    m['cache_v'] = np.ascontiguousarray(cache_v[b].reshape(DEPTH, PAST, NKV * HD))
    st = np.stack([np.stack([sf_re[b], sf_im[b]], 1), np.stack([sb_re[b], sb_im[b]], 1)], 1)
    st = st.reshape(DEPTH, 2, 2, 32, 128)
    m['st_q'] = np.ascontiguousarray(st.transpose(0, 4, 1, 2, 3))
    del m['_c_ctx']
    return m


def _prep_shared(cfg, c_ctx, norm_g, w_mod, b_mod, w_in, q_norm, k_norm, lam_re, lam_im, log_step,
                 b_re, b_im, c_re, c_im, d_skip, w_glu, w_fft, w_out, final_norm_g):
    DEPTH = cfg['DEPTH']
    f = np.float32
    m = dict(_consts(cfg))
    m['_c_ctx'] = c_ctx
    m['normg_p'] = np.ascontiguousarray(norm_g.reshape(DEPTH, 32, 128).transpose(0, 2, 1))
    m['w_mod'] = w_mod
    m['bmod_ps'] = np.ascontiguousarray(b_mod[:, :2 * D].reshape(DEPTH, 64, 128).transpose(0, 2, 1))
    m['bmodg_rep'] = np.ascontiguousarray(np.broadcast_to(b_mod[:, None, 2 * D:], (DEPTH, 128, D)))
    m['w_in'] = w_in
    m['qk_g'] = np.ascontiguousarray(np.stack([q_norm, k_norm], axis=-1))
    ls_full = np.broadcast_to(log_step[..., None], lam_re.shape)
    lam3 = np.stack([lam_re, lam_im, ls_full], axis=1).reshape(DEPTH, 3, 2, 32, 128)
    m['lam_q'] = np.ascontiguousarray(lam3.transpose(0, 1, 4, 2, 3).reshape(DEPTH, 3, 128, 64))
    lam_flat = np.stack([lam_re, lam_im, ls_full], axis=1).reshape(DEPTH, 3, 1, 2, 4096)
    m['lam_rep'] = np.ascontiguousarray(np.broadcast_to(lam_flat, (DEPTH, 3, 128, 2, 4096)))
    bT = np.zeros((DEPTH, 2, 8, 128, 2, 512), f)
    cT = np.zeros((DEPTH, 2, 8, 128, 2, 4, 128), f)
    for ri, (bb, ccm) in enumerate(((b_re, c_re), (b_im, c_im))):
        for uc in range(8):
            for gl in range(8):
                g = uc * 8 + gl
                bT[:, ri, uc, gl * 16:(gl + 1) * 16, :, gl * 64:(gl + 1) * 64] = bb[:, :, g].transpose(0, 3, 1, 2)
                k, qo = gl // 2, (gl % 2) * 64
                cT[:, ri, uc, qo:qo + 64, :, k, gl * 16:(gl + 1) * 16] = ccm[:, :, g].transpose(0, 3, 1, 2)
    m['bT_pad'] = bT
    m['cT_pad'] = cT
    m['dskip_p'] = np.ascontiguousarray(d_skip.reshape(DEPTH, 8, 128).transpose(0, 2, 1))
    m['w_glu'] = w_glu
    m['w_fft'] = w_fft
    m['w_out'] = w_out
    m['fng_rep'] = np.ascontiguousarray(np.broadcast_to(final_norm_g[None, :], (128, D)))
    return m


_NC_CACHE = {}


def run(cfg, n_cores, inputs):
    g = {k: np.asarray(v) for k, v in inputs.items()}
    key = tuple(sorted(cfg.items()))
    if key not in _NC_CACHE:
        _NC_CACHE[key] = build(cfg)
    nc = _NC_CACHE[key]
    shared = _prep_shared(cfg, g['c_ctx'], g['norm_g'], g['w_mod'], g['b_mod'], g['w_in'], g['q_norm'], g['k_norm'],
                          g['lam_re'], g['lam_im'], g['log_step'], g['b_re'], g['b_im'], g['c_re'], g['c_im'],
                          g['d_skip'], g['w_glu'], g['w_fft'], g['w_out'], g['final_norm_g'])
    in_maps = [_prep_core(cfg, core, g['x_prompt'], g['x_sample'], g['cache_k'], g['cache_v'],
                          g['state_fwd_re'], g['state_fwd_im'], g['state_bwd_re'], g['state_bwd_im'], g['c'], shared)
               for core in range(n_cores)]
    res = run_bass_kernel_spmd(nc, in_maps, core_ids=list(range(n_cores)))
    R = res.results
    DEPTH, NPS, LP, LS = cfg['DEPTH'], cfg['NPS'], cfg['LP'], cfg['LS']
    NTP = NPS * LP
    cpb = cfg.get('CPB', 4)
    y_prompt = np.concatenate([R[c_]['y_out'][:NTP].reshape(NPS, LP, D) for c_ in range(n_cores)], axis=0)
    y_sample = np.stack([R[c_]['y_out'][NTP:] for c_ in range(0, n_cores, cpb)], axis=0)
    new_k = np.concatenate([R[c_]['newk_out'].reshape(NPS, DEPTH, LP, NKV, HD) for c_ in range(n_cores)], axis=0)
    new_v = np.concatenate([R[c_]['newv_out'].reshape(NPS, DEPTH, LP, NKV, HD) for c_ in range(n_cores)], axis=0)
    fins = []
    for c_ in range(n_cores):
        fo = R[c_]['fin_out'].reshape(DEPTH, 128, NPS, 2, 2, 32)
        fins.append(fo.transpose(2, 0, 3, 4, 5, 1).reshape(NPS, DEPTH, 2, 2, 64, 64))
    fo = np.concatenate(fins, axis=0)
    outs = (y_prompt, y_sample, new_k, new_v, fo[:, :, 0, 0], fo[:, :, 0, 1], fo[:, :, 1, 0], fo[:, :, 1, 1])
    return tuple(np.ascontiguousarray(o, dtype=np.float32) for o in outs)


def kernel(**inputs):
    return run(CFG_FULL, 8, inputs)
```

```python
import math
import numpy as np
import ml_dtypes
import concourse.bass as bass
import concourse.mybir as mybir
from concourse.bass_utils import run_bass_kernel_spmd

F32 = mybir.dt.float32
BF16 = mybir.dt.bfloat16
I32 = mybir.dt.int32
ALU = mybir.AluOpType
AF = mybir.ActivationFunctionType

D = 4096
HD = 128
NH = 16
NKV = 4
INW = 9216
EPS = 1e-6
PI = math.pi
TW0 = 512

CFG_FULL = dict(DEPTH=4, NPS=4, LP=256, LS=4096, PAST=512)


class Sched:
    def __init__(self, nc, sems, dma_sems):
        self.nc = nc
        self.eng = dict(pe=nc.tensor, dve=nc.vector, act=nc.scalar, pool=nc.gpsimd, sp=nc.sync)
        self.sem = sems
        self.cnt = {e: 0 for e in sems}
        self.dsem = dma_sems
        self.dcnt = {q: [0] * len(v) for q, v in dma_sems.items()}
        self.dnext = {q: 0 for q in dma_sems}
        self.seen = {e: {} for e in self.eng}
        self.lastw = {}
        self.readers = {}
        self.semobj = {}

    def _need(self, e, r, w):
        need = {}
        def add(tok):
            if tok is None:
                return
            sid, val = tok
            if need.get(sid, 0) < val:
                need[sid] = val
        for k in r:
            add(self.lastw.get(k))
        for k in w:
            add(self.lastw.get(k))
            for t in self.readers.get(k, {}).items():
                add(t)
        eng = self.eng[e]
        for sid, val in need.items():
            if e == 'pe' and sid == 'E_pe':
                continue
            if self.seen[e].get(sid, 0) < val:
                eng.wait_ge(self.semobj[sid], val)
                self.seen[e][sid] = val

    def _record(self, tok, r, w):
        for k in w:
            self.lastw[k] = tok
            self.readers[k] = {}
        for k in r:
            d = self.readers.setdefault(k, {})
            if d.get(tok[0], 0) < tok[1]:
                d[tok[0]] = tok[1]

    def op(self, e, fn, r=(), w=()):
        self._need(e, r, w)
        inst = fn(self.eng[e])
        sid = 'E_' + e
        self.semobj[sid] = self.sem[e]
        self.cnt[e] += 1
        inst.then_inc(self.sem[e], 1)
        self._record((sid, self.cnt[e]), r, w)

    def dma(self, q, out, in_, r=(), w=(), slow=False):
        self._need(q, r, w)
        i = self.dnext[q]
        self.dnext[q] = (i + 1) % len(self.dsem[q])
        sid = 'D_%s_%d' % (q, i)
        self.semobj[sid] = self.dsem[q][i]
        if self.dcnt[q][i] > 0 and self.seen[q].get(sid, 0) < self.dcnt[q][i]:
            self.eng[q].wait_ge(self.dsem[q][i], self.dcnt[q][i])
            self.seen[q][sid] = self.dcnt[q][i]
        self.dcnt[q][i] += 16
        if slow:
            inst = self.eng[q].dma_start(out=out, in_=in_, allow_slow_non_contiguous=True)
        else:
            inst = self.eng[q].dma_start(out=out, in_=in_)
        inst.then_inc(self.dsem[q][i], 16)
        self._record((sid, self.dcnt[q][i]), r, w)

    def drain(self):
        self.finish()
        self.nc.all_engine_barrier()

    def finish(self):
        sp = self.eng['sp']
        for q, lst in self.dsem.items():
            for i, s in enumerate(lst):
                if self.dcnt[q][i] > 0:
                    sp.wait_ge(s, self.dcnt[q][i])
        for e, s in self.sem.items():
            if self.cnt[e] > 0:
                sp.wait_ge(s, self.cnt[e])


def build(cfg):
    DEPTH, NPS, LP, LS, PAST = cfg['DEPTH'], cfg['NPS'], cfg['LP'], cfg['LS'], cfg['PAST']
    NTP = NPS * LP
    NT = NTP + LS
    NKEY = NT + PAST
    assert NTP % TW0 == 0 and LS % TW0 == 0
    NTILE = NT // TW0
    NPT = NTP // TW0
    seqs = [(i * LP, LP, False) for i in range(NPS)] + [(NTP, LS, True)]

    nc = bass.Bass("TRN2", target_bir_lowering=False)

    def din(name, shape, dt=F32):
        return nc.dram_tensor(name, list(shape), dt, kind="ExternalInput").ap()

    def dout(name, shape, dt=F32):
        return nc.dram_tensor(name, list(shape), dt, kind="ExternalOutput").ap()

    def dscr(name, shape, dt=BF16):
        return nc.dram_tensor(name, list(shape), dt, kind="Internal").ap()

    x_in = din("x_in", [NT, D])
    cvec = din("cvec", [128, 32, 2])
    cache_kT = din("cache_kT", [DEPTH, NKV, 128, PAST])
    cache_v = din("cache_v", [DEPTH, PAST, NKV * HD])
    st_q = din("st_q", [DEPTH, 128, 2, 2, 32])
    normg_p = din("normg_p", [DEPTH, 128, 32])
    w_mod = din("w_mod", [DEPTH, D, 3 * D])
    bmod_ps = din("bmod_ps", [DEPTH, 128, 64])
    bmodg_rep = din("bmodg_rep", [DEPTH, 128, D])
    w_in = din("w_in", [DEPTH, D, INW])
    qk_g = din("qk_g", [DEPTH, 128, 2])
    lam_q = din("lam_q", [DEPTH, 3, 128, 64])
    lam_rep = din("lam_rep", [DEPTH, 3, 128, 2, 4096])
    bT_pad = din("bT_pad", [DEPTH, 2, 8, 128, 2, 512])
    cT_pad = din("cT_pad", [DEPTH, 2, 8, 128, 2, 4, 128])
    dskip_p = din("dskip_p", [DEPTH, 128, 8])
    w_glu = din("w_glu", [DEPTH, 1024, 2048])
    w_fft = din("w_fft", [DEPTH, 1024, 1024])
    w_out = din("w_out", [DEPTH, D, D])
    fng_rep = din("fng_rep", [128, D])
    c_ident = din("c_ident", [128, 128])
    c_RT = din("c_RT", [128, 128])
    c_rope = din("c_rope", [2, 128, LS])
    c_jv = din("c_jv", [128, 512])
    c_cs256 = din("c_cs256", [128, 2, 512])
    c_dftp = din("c_dftp", [2, LP, LP], BF16)
    c_dfts = din("c_dfts", [2, LS, LS], BF16)

    y_out = dout("y_out", [NT, D])
    newk_out = dout("newk_out", [NPS, DEPTH, LP, NKV * HD])
    newv_out = dout("newv_out", [NPS, DEPTH, LP, NKV * HD])
    fin_out = dout("fin_out", [DEPTH, 128, NPS * 2 * 2 * 32])

    xres = dscr("xres", [NT, D], F32)
    qT = dscr("qT", [NH, 128, NT])
    kT = dscr("kT", [NKV, 128, NT])
    vS = dscr("vS", [NT, NKV * HD])
    gaT = dscr("gaT", [2048, NT])
    gsT = dscr("gsT", [1024, NT])
    gfT = dscr("gfT", [1024, NT])
    uT = dscr("uT", [1024, NT])
    fT = dscr("fT", [1024, NT])
    yT = dscr("yT", [1024, NT])
    fcs = dscr("fcs", [NT, 4, 512])
    dT = dscr("dT", [1024, NT])
    mixT = dscr("mixT", [D, NT])

    import contextlib
    es = contextlib.ExitStack()
    with es:
        sems = {e: es.enter_context(nc.semaphore("E_" + e)) for e in ('pe', 'dve', 'act', 'pool')}
        dsems = {q: [es.enter_context(nc.semaphore("D_%s_%d" % (q, i))) for i in range(12)] for q in ('sp', 'pool')}
        S = Sched(nc, sems, dsems)

        def sb(name, shape, dt):
            return es.enter_context(nc.sbuf_tensor(name, list(shape), dt))

        ps = [es.enter_context(nc.psum_tensor("ps%d" % i, [128, 512], F32)) for i in range(8)]
        PSK = ['ps%d' % i for i in range(8)]

        ident = sb("ident", [128, 128], F32)
        RTb = sb("RTb", [128, 128], BF16)
        onesb = sb("onesb", [128, 128], BF16)
        onesf = sb("onesf", [128, 128], F32)
        cv = sb("cv", [128, 32, 2], F32)
        s_bf = sb("s_bf", [128, 32, 2], BF16)
        s_f = sb("s_f", [128, 32, 2], F32)
        Amod = sb("Amod", [128, 32, 2], F32)
        Bmod = sb("Bmod", [128, 32, 2], F32)
        modsb = sb("modsb", [128, 64, 2], F32)
        bmp = sb("bmp", [128, 64], F32)
        ngp = sb("ngp", [128, 32], F32)
        qkg = sb("qkg", [128, 2], F32)
        qkg2 = sb("qkg2", [128, 2], F32)
        small = sb("small", [128, 16], F32)
        negpi = sb("negpi", [128, 1], F32)

        S.dma('sp', ident[:], c_ident[:, :], w=['ident'])
        S.dma('pool', RTb[:], c_RT[:, :], w=['RTb'])
        S.dma('sp', cv[:], cvec[:, :, :], w=['cv'])
        S.op('dve', lambda e: e.memset(onesf[:], 1.0), w=['onesf'])
        S.op('dve', lambda e: e.memset(onesb[:], 1.0), w=['onesb'])
        S.op('act', lambda e: e.activation(out=s_f[:], in_=cv[:], func=AF.Silu), r=['cv'], w=['s_f'])
        S.op('dve', lambda e: e.tensor_copy(out=s_bf[:], in_=s_f[:]), r=['s_f'], w=['s_bf'])

        slab = [None, None]
        slab_i = [0]

        def load_slab(src_ap):
            i = slab_i[0] % 2
            slab_i[0] += 1
            S.dma('pool', slab[i][:], src_ap.rearrange("(c p) m -> p c m", p=128), w=['slab%d' % i])
            return slab[i], 'slab%d' % i

        class SlabStream:
            def __init__(self, srcs):
                self.srcs = srcs
                self.loaded = {}

            def get(self, i):
                for j in (i, i + 1):
                    if j < len(self.srcs) and j not in self.loaded:
                        self.loaded[j] = load_slab(self.srcs[j])
                return self.loaded.pop(i)

        def rstd_from(out_ap, in_ap, scale, keys_r, key_w, tmpk='small'):
            S.op('dve', lambda e: e.tensor_scalar(out=out_ap, in0=in_ap, scalar1=scale, scalar2=EPS,
                                                  op0=ALU.mult, op1=ALU.add), r=keys_r, w=[key_w])
            S.op('act', lambda e: e.activation(out=out_ap, in_=out_ap, func=AF.Sqrt), r=[key_w], w=[key_w])
            S.op('dve', lambda e: e.reciprocal(out=out_ap, in_=out_ap), r=[key_w], w=[key_w])

        def sin_of(out_ap, arg_ap, tmpf, tmpi, shift, keys_r, key_w, kf, ki):
            S.op('dve', lambda e: e.tensor_scalar(out=tmpf, in0=arg_ap, scalar1=shift, scalar2=1.0 / (2 * PI),
                                                  op0=ALU.add, op1=ALU.mult), r=keys_r, w=[kf])
            S.op('dve', lambda e: e.tensor_copy(out=tmpi, in_=tmpf), r=[kf], w=[ki])
            S.op('dve', lambda e: e.tensor_copy(out=tmpf, in_=tmpi), r=[ki], w=[kf])
            S.op('dve', lambda e: e.scalar_tensor_tensor(out=tmpf, in0=tmpf, scalar=-2 * PI, in1=arg_ap,
                                                         op0=ALU.mult, op1=ALU.add), r=[kf] + list(keys_r), w=[kf])
            S.op('dve', lambda e: e.tensor_scalar(out=tmpf, in0=tmpf, scalar1=shift, scalar2=3.1415925,
                                                  op0=ALU.add, op1=ALU.min), r=[kf], w=[kf])
            S.op('dve', lambda e: e.tensor_scalar(out=tmpf, in0=tmpf, scalar1=-3.1415925, scalar2=None,
                                                  op0=ALU.max), r=[kf], w=[kf])
            S.op('act', lambda e: e.activation(out=out_ap, in_=tmpf, func=AF.Sin), r=[kf], w=[key_w])

        for l in range(DEPTH):
            xsrc = x_in if l == 0 else xres
            with contextlib.ExitStack() as ph:
                def sbp(name, shape, dt):
                    return ph.enter_context(nc.sbuf_tensor("%s_u%d" % (name, nc.next_id()), list(shape), dt))
                S.dma('sp', bmp[:], bmod_ps[l], w=['bmp'])
                S.dma('sp', ngp[:], normg_p[l], w=['ngp'])
                S.dma('sp', qkg[:], qk_g[l], w=['qkg'])
                slab[0] = sbp("slabA", [128, 32, 512], BF16)
                slab[1] = sbp("slabB", [128, 32, 512], BF16)
                Srep = sbp("Srep", [128, 2, 32, 128], BF16)
                gbias = sbp("gbias", [128, D], F32)
                S.dma('sp', gbias[:], bmodg_rep[l], w=['gbias'])
                for v in range(2):
                    for c in range(32):
                        S.op('pool', lambda e, v=v, c=c: e.tensor_scalar(out=Srep[:, v, c, :], in0=onesf[:], scalar1=s_f[:, c, v:v + 1],
                                                                        scalar2=None, op0=ALU.mult),
                             r=['onesf', 's_f'], w=['Srep'])
                ss0 = SlabStream([w_mod[l][:, si * 512:(si + 1) * 512] for si in range(24)])
                for si in range(16):
                    sl, sk = ss0.get(si)
                    for j in range(4):
                        blk = si * 4 + j
                        for c in range(32):
                            S.op('pe', lambda e, c=c, j=j, blk=blk, sl=sl: e.matmul(ps[0][:, blk * 2:blk * 2 + 2], lhsT=sl[:, c, j * 128:(j + 1) * 128],
                                                                                  rhs=s_bf[:, c, :], start=(c == 0), stop=(c == 31)),
                                 r=[sk, 's_bf'], w=['ps0'])
                S.op('dve', lambda e: e.tensor_tensor(out=modsb[:], in0=ps[0][:, 0:128].rearrange("p (b v) -> p b v", v=2),
                                                      in1=bmp[:].unsqueeze(2).to_broadcast([128, 64, 2]), op=ALU.add),
                     r=['ps0', 'bmp'], w=['modsb'])
                S.op('dve', lambda e: e.tensor_copy(out=Bmod[:], in_=modsb[:, 0:32, :]), r=['modsb'], w=['Bmod'])
                S.op('dve', lambda e: e.tensor_scalar(out=Amod[:], in0=modsb[:, 32:64, :], scalar1=1.0, scalar2=None, op0=ALU.add),
                     r=['modsb'], w=['Amod'])
                S.op('dve', lambda e: e.tensor_tensor(out=Amod[:], in0=Amod[:], in1=ngp[:].unsqueeze(2).to_broadcast([128, 32, 2]), op=ALU.mult),
                     r=['Amod', 'ngp'], w=['Amod'])
                S.op('dve', lambda e: e.tensor_scalar(out=qkg2[:, 0:1], in0=qkg[:, 0:1], scalar1=HD ** -0.5, scalar2=None, op0=ALU.mult),
                     r=['qkg'], w=['qkg2'])
                S.op('dve', lambda e: e.tensor_copy(out=qkg2[:, 1:2], in_=qkg[:, 1:2]), r=['qkg', 'qkg2'], w=['qkg2'])
                gate_sb = sbp("gate_sb", [128, 2, 512], F32)
                gate_dr = dscr("gate_dr_l%d" % l, [2, 128, D], F32)
                for si in range(16, 24):
                    sl, sk = ss0.get(si)
                    g0 = (si - 16) * 512
                    for v in range(2):
                        for c in range(32):
                            S.op('pe', lambda e, c=c, v=v, sl=sl: e.matmul(ps[1 + v][:, :], lhsT=Srep[:, v, c, :], rhs=sl[:, c, :],
                                                                         start=(c == 0), stop=(c == 31)),
                                 r=[sk, 'Srep'], w=[PSK[1 + v]])
                        S.op('dve', lambda e, v=v, g0=g0: e.tensor_tensor(out=gate_sb[:, v, :], in0=ps[1 + v][:, :], in1=gbias[:, g0:g0 + 512], op=ALU.add),
                             r=[PSK[1 + v], 'gbias'], w=['gate_sb%d' % v])
                        S.dma('sp', gate_dr[v][:, g0:g0 + 512], gate_sb[:, v, :], r=['gate_sb%d' % v], w=['gate_dr'])
            S.drain()
            if cfg.get('STOP', 99) == 0:
                break

            with contextlib.ExitStack() as ph:
                def sbp(name, shape, dt):
                    return ph.enter_context(nc.sbuf_tensor("%s_u%d" % (name, nc.next_id()), list(shape), dt))
                slab[0] = sbp("slabA", [128, 32, 512], BF16)
                slab[1] = sbp("slabB", [128, 32, 512], BF16)
                xs = [sbp("xs%d" % i, [128, D], F32) for i in range(2)]
                junk = sbp("junk", [128, D], BF16)
                hT = sbp("hT", [128, 32, 512], BF16)
                ssq = sbp("ssq", [128, 4], F32)
                sq = sbp("sq", [128, 512], BF16)
                rs = sbp("rs", [128, 512], F32)
                qn = sbp("qn", [128, 512], F32)
                qb = sbp("qb", [128, 512], BF16)
                qo = [sbp("qo%d" % i, [128, 512], BF16) for i in range(2)]
                t1 = sbp("t1", [128, 512], F32)
                t2 = sbp("t2", [128, 512], F32)
                ropec = sbp("ropec", [128, 512], F32)
                ropes = sbp("ropes", [128, 512], F32)
                vb = sbp("vb", [128, 512], BF16)
                vf = sbp("vf", [128, 512], F32)
                ko = sbp("ko", [128, 512], F32)
                ev = [sbp("ev%d" % i, [128, 512], BF16) for i in range(2)]
                evi = 0
                ss1 = SlabStream([w_in[l][:, si * 512:(si + 1) * 512] for _ in range(NTILE) for si in range(18)])
                for ti in range(NTILE):
                    t0 = ti * TW0
                    v = 0 if ti < NPT else 1
                    samp = ti >= NPT
                    if samp:
                        p0 = t0 - NTP
                        S.dma('sp', ropec[:], c_rope[0][:, p0:p0 + 512], w=['ropec'])
                        S.dma('sp', ropes[:], c_rope[1][:, p0:p0 + 512], w=['ropes'])
                    for sub in range(4):
                        xt = xs[sub % 2]
                        xk = 'xs%d' % (sub % 2)
                        S.dma('sp', xt[:], xsrc[t0 + sub * 128:t0 + (sub + 1) * 128, :], w=[xk])
                        S.op('act', lambda e, xt=xt, sub=sub: e.activation(out=junk[:], in_=xt[:], func=AF.Square, accum_out=ssq[:, sub:sub + 1]),
                             r=[xk], w=['junk', 'ssq%d' % sub])
                        rstd_from(ssq[:, sub:sub + 1], ssq[:, sub:sub + 1], 1.0 / D, ['ssq%d' % sub], 'ssq%d' % sub)
                        S.op('pool', lambda e, xt=xt, sub=sub: e.tensor_scalar(out=xt[:], in0=xt[:], scalar1=ssq[:, sub:sub + 1], scalar2=None, op0=ALU.mult),
                             r=[xk, 'ssq%d' % sub], w=[xk])
                        for c0 in range(0, 32, 4):
                            bank = 4 + (c0 // 4) % 2
                            for cc in range(4):
                                c = c0 + cc
                                S.op('pe', lambda e, c=c, cc=cc, xt=xt, bank=bank: e.transpose(out=ps[bank][:, cc * 128:(cc + 1) * 128], in_=xt[:, c * 128:(c + 1) * 128], identity=ident[:]),
                                     r=[xk, 'ident'], w=[PSK[bank]])
                            for cc in range(4):
                                c = c0 + cc
                                S.op('dve', lambda e, c=c, cc=cc, bank=bank, sub=sub, v=v: e.tensor_scalar(
                                    out=hT[:, c, sub * 128:(sub + 1) * 128], in0=ps[bank][:, cc * 128:(cc + 1) * 128],
                                    scalar1=Amod[:, c, v:v + 1], scalar2=Bmod[:, c, v:v + 1], op0=ALU.mult, op1=ALU.add),
                                    r=[PSK[bank], 'Amod', 'Bmod'], w=['hT'])
                    if cfg.get('P1STOP', 0) == 1:
                        break
                    for si in range(18):
                        if cfg.get('P1STOP', 0) == 2 + si:
                            break
                        sl, sk = ss1.get(ti * 18 + si)
                        if si == 5:
                            for sub in range(4):
                                bank = sub % 2
                                for c in range(32):
                                    S.op('pe', lambda e, c=c, sub=sub, sl=sl, bank=bank: e.matmul(ps[bank][:, :], lhsT=hT[:, c, sub * 128:(sub + 1) * 128], rhs=sl[:, c, :],
                                                                                                start=(c == 0), stop=(c == 31)),
                                         r=[sk, 'hT'], w=[PSK[bank]])
                                S.op('dve', lambda e, bank=bank: e.tensor_copy(out=vf[:], in_=ps[bank][:, :]), r=[PSK[bank]], w=['vf'])
                                S.op('act', lambda e: e.activation(out=vb[:], in_=vf[:], func=AF.Copy), r=['vf'], w=['vb'])
                                S.dma('sp', vS[t0 + sub * 128:t0 + (sub + 1) * 128, :], vb[:], r=['vb'], w=['vS%d' % ti])
                                if not samp:
                                    tok = t0 + sub * 128
                                    S.dma('sp', newv_out[tok // LP, l, tok % LP:tok % LP + 128, :], vf[:], r=['vf'], w=['newv'])
                            continue
                        for j in range(4):
                            fb = si * 4 + j
                            bank = fb % 2
                            for c in range(32):
                                S.op('pe', lambda e, c=c, j=j, sl=sl, bank=bank: e.matmul(ps[bank][:, :], lhsT=sl[:, c, j * 128:(j + 1) * 128], rhs=hT[:, c, :],
                                                                                        start=(c == 0), stop=(c == 31)),
                                     r=[sk, 'hT'], w=[PSK[bank]])
                            if si <= 4:
                                isk = si == 4
                                S.op('act', lambda e, bank=bank: e.activation(out=sq[:], in_=ps[bank][:, :], func=AF.Square), r=[PSK[bank]], w=['sq'])
                                S.op('pe', lambda e: e.matmul(ps[2][:, :], lhsT=onesb[:], rhs=sq[:], start=True, stop=True), r=['sq', 'onesb'], w=['ps2'])
                                rstd_from(rs[:], ps[2][:, :], 1.0 / HD, ['ps2'], 'rs')
                                gcol = 1 if isk else 0
                                S.op('dve', lambda e, bank=bank, gcol=gcol: e.scalar_tensor_tensor(out=qn[:], in0=ps[bank][:, :], scalar=qkg2[:, gcol:gcol + 1], in1=rs[:],
                                                                                                 op0=ALU.mult, op1=ALU.mult),
                                     r=[PSK[bank], 'qkg2', 'rs'], w=['qn'])
                                if isk and not samp:
                                    for sub in range(4):
                                        S.op('pe', lambda e, sub=sub: e.transpose(out=ps[3][:, sub * 128:(sub + 1) * 128], in_=qn[:, sub * 128:(sub + 1) * 128], identity=ident[:]),
                                             r=['qn', 'ident'], w=['ps3'])
                                    S.op('dve', lambda e: e.tensor_copy(out=ko[:], in_=ps[3][:, :]), r=['ps3'], w=['ko'])
                                    for sub in range(4):
                                        tok = t0 + sub * 128
                                        S.dma('sp', newk_out[tok // LP, l, tok % LP:tok % LP + 128, j * 128:(j + 1) * 128], ko[:, sub * 128:(sub + 1) * 128],
                                              r=['ko'], w=['newk'])
                                o = qo[evi % 2]
                                ok = 'qo%d' % (evi % 2)
                                evi += 1
                                if samp:
                                    S.op('act', lambda e: e.activation(out=qb[:], in_=qn[:], func=AF.Copy), r=['qn'], w=['qb'])
                                    S.op('pe', lambda e: e.matmul(ps[3][:, :], lhsT=RTb[:], rhs=qb[:], start=True, stop=True), r=['qb', 'RTb'], w=['ps3'])
                                    S.op('pool', lambda e: e.tensor_tensor(out=t1[:], in0=qn[:], in1=ropec[:], op=ALU.mult), r=['qn', 'ropec'], w=['t1'])
                                    S.op('dve', lambda e: e.tensor_tensor(out=t2[:], in0=ps[3][:, :], in1=ropes[:], op=ALU.mult), r=['ps3', 'ropes'], w=['t2'])
                                    S.op('pool', lambda e, o=o: e.tensor_tensor(out=o[:], in0=t1[:], in1=t2[:], op=ALU.add), r=['t1', 't2'], w=[ok])
                                else:
                                    S.op('act', lambda e, o=o: e.activation(out=o[:], in_=qn[:], func=AF.Copy), r=['qn'], w=[ok])
                                if isk:
                                    S.dma('sp', kT[j][:, t0:t0 + 512], o[:], r=[ok], w=['kT%d' % ti])
                                else:
                                    S.dma('sp', qT[fb][:, t0:t0 + 512], o[:], r=[ok], w=['qT%d' % ti])
                            else:
                                o = ev[evi % 2]
                                ok = 'ev%d' % (evi % 2)
                                evi += 1
                                f0 = fb * 128
                                if 3072 <= f0 < 5120:
                                    dst, dk_, fn = gaT[f0 - 3072:f0 - 3072 + 128, t0:t0 + 512], 'gaT%d' % ti, AF.Silu
                                elif 5120 <= f0 < 6144:
                                    dst, dk_, fn = uT[f0 - 5120:f0 - 5120 + 128, t0:t0 + 512], 'uT', AF.Copy
                                elif 6144 <= f0 < 7168:
                                    dst, dk_, fn = gsT[f0 - 6144:f0 - 6144 + 128, t0:t0 + 512], 'gsT%d' % ti, AF.Silu
                                elif 7168 <= f0 < 8192:
                                    dst, dk_, fn = fT[f0 - 7168:f0 - 7168 + 128, t0:t0 + 512], 'fT%d' % ti, AF.Copy
                                else:
                                    dst, dk_, fn = gfT[f0 - 8192:f0 - 8192 + 128, t0:t0 + 512], 'gfT%d' % ti, AF.Silu
                                S.op('act', lambda e, o=o, bank=bank, fn=fn: e.activation(out=o[:], in_=ps[bank][:, :], func=fn), r=[PSK[bank]], w=[ok])
                                S.dma('sp', dst, o[:], r=[ok], w=[dk_])
            S.drain()
            if cfg.get('STOP', 99) == 1:
                break

            with contextlib.ExitStack() as ph:
                def sbp(name, shape, dt):
                    return ph.enter_context(nc.sbuf_tensor("%s_u%d" % (name, nc.next_id()), list(shape), dt))
                NKMAX = LS + PAST
                KTs = sbp("KTs", [128, NKV, NKMAX], BF16)
                Vs = sbp("Vs", [128, NKMAX // 128, NKV * HD], BF16)
                Q4 = [sbp("Q4_%d" % i, [128, 4, 128], BF16) for i in range(2)]
                G4 = [sbp("G4_%d" % i, [128, 4, 128], BF16) for i in range(2)]
                Pb = [sbp("Pb%d" % i, [128, 512], BF16) for i in range(3)]
                rl = sbp("rl", [128, 512], F32)
                pacc = [sbp("pacc%d" % i, [128, 512], F32) for i in range(2)]
                ot = sbp("ot", [128, 512], F32)
                mo = [sbp("mo%d" % i, [128, 4, 128], BF16) for i in range(2)]
                it = 0
                pi_ = 0
                for (s0, L, samp) in seqs:
                    nk = L + (PAST if samp else 0)
                    koff = PAST if samp else 0
                    if samp:
                        for kv in range(NKV):
                            S.dma('pool', KTs[:, kv, 0:PAST], cache_kT[l, kv], w=['KTs'])
                        S.dma('pool', Vs[:, 0:PAST // 128, :], cache_v[l].rearrange("(b p) f -> p b f", p=128), w=['Vs'])
                    for kv in range(NKV):
                        S.dma('sp', KTs[:, kv, koff:koff + L], kT[kv][:, s0:s0 + L], r=['kT%d' % i for i in range(NTILE)], w=['KTs'])
                    S.dma('sp', Vs[:, koff // 128:(koff + L) // 128, :], vS[s0:s0 + L, :].rearrange("(b p) f -> p b f", p=128),
                          r=['vS%d' % i for i in range(NTILE)], w=['Vs'])
                    nkb = nk // 128
                    def ld_qg(idx, kv, qb_):
                        q0_ = s0 + qb_ * 128
                        S.dma('sp', Q4[idx % 2][:], qT[4 * kv:4 * kv + 4, :, q0_:q0_ + 128].rearrange("h d t -> d h t"),
                              r=['qT%d' % i for i in range(NTILE)], w=['Q4_%d' % (idx % 2)])
                        S.dma('sp', G4[idx % 2][:], gaT[kv * 512:(kv + 1) * 512, q0_:q0_ + 128].rearrange("(h d) t -> d h t", d=128),
                              r=['gaT%d' % i for i in range(NTILE)], w=['G4_%d' % (idx % 2)])
                    iters_ = [(kv, qb_) for kv in range(NKV) for qb_ in range(L // 128)]
                    ld_qg(it, *iters_[0])
                    for j_, (kv, qb_) in enumerate(iters_):
                        if True:
                            q0 = s0 + qb_ * 128
                            if j_ + 1 < len(iters_):
                                ld_qg(it + 1, *iters_[j_ + 1])
                            Q = Q4[it % 2]
                            G = G4[it % 2]
                            M = mo[it % 2]
                            qk_ = 'Q4_%d' % (it % 2)
                            gk_ = 'G4_%d' % (it % 2)
                            mk_ = 'mo%d' % (it % 2)
                            po, pl = (4, 5) if it % 2 == 0 else (6, 7)
                            pa, pak = pacc[it % 2], 'pacc%d' % (it % 2)
                            it += 1
                            Qf = Q[:].rearrange("d h t -> d (h t)")
                            def emit_s(kb):
                                sbank = kb % 4
                                S.op('pe', lambda e, kb=kb, kv=kv, sbank=sbank, Qf=Qf: e.matmul(ps[sbank][:, :], lhsT=KTs[:, kv, kb * 128:(kb + 1) * 128], rhs=Qf, start=True, stop=True),
                                     r=['KTs', qk_], w=[PSK[sbank]])
                            emit_s(0)
                            if nkb > 1:
                                emit_s(1)
                            for kb in range(nkb):
                                sbank = kb % 4
                                P = Pb[pi_ % 3]
                                pk_ = 'Pb%d' % (pi_ % 3)
                                pi_ += 1
                                if kb + 2 < nkb:
                                    emit_s(kb + 2)
                                S.op('act', lambda e, P=P, sbank=sbank: e.activation(out=P[:], in_=ps[sbank][:, :], func=AF.Exp), r=[PSK[sbank]], w=[pk_])
                                S.op('pe', lambda e, kb=kb, kv=kv, P=P, po=po: e.matmul(ps[po][:, :], lhsT=Vs[:, kb, kv * 128:(kv + 1) * 128], rhs=P[:], start=(kb == 0), stop=(kb == nkb - 1)),
                                     r=['Vs', pk_], w=[PSK[po]])
                                if kb == 0:
                                    S.op('dve', lambda e, P=P, pa=pa: e.tensor_copy(out=pa[:], in_=P[:]), r=[pk_], w=[pak])
                                else:
                                    S.op('dve', lambda e, P=P, pa=pa: e.tensor_tensor(out=pa[:], in0=pa[:], in1=P[:], op=ALU.add), r=[pk_, pak], w=[pak])
                            S.op('pe', lambda e, pa=pa, pl=pl: e.matmul(ps[pl][:, :], lhsT=onesf[:], rhs=pa[:], start=True, stop=True),
                                 r=['onesf', pak], w=[PSK[pl]])
                            S.op('dve', lambda e, pl=pl: e.reciprocal(out=rl[:], in_=ps[pl][:, :]), r=[PSK[pl]], w=['rl'])
                            S.op('dve', lambda e, po=po: e.tensor_tensor(out=ot[:], in0=ps[po][:, :], in1=rl[:], op=ALU.mult), r=[PSK[po], 'rl'], w=['ot'])
                            S.op('pool', lambda e, M=M, G=G: e.tensor_tensor(out=M[:].rearrange("d h t -> d (h t)"), in0=ot[:], in1=G[:].rearrange("d h t -> d (h t)"), op=ALU.mult),
                                 r=['ot', gk_], w=[mk_])
                            S.dma('sp', mixT[kv * 512:(kv + 1) * 512, q0:q0 + 128].rearrange("(h d) t -> d h t", d=128), M[:], r=[mk_], w=['mixT_a'])
            S.drain()
            if cfg.get('STOP', 99) == 2:
                break

            with contextlib.ExitStack() as ph:
                def sbp(name, shape, dt):
                    return ph.enter_context(nc.sbuf_tensor("%s_u%d" % (name, nc.next_id()), list(shape), dt))
                cs256 = sbp("cs256", [128, 2, 512], BF16)
                S.dma('pool', cs256[:], c_cs256[:, :, :], w=['cs256'])
                fTt = [sbp("fTt%d" % i, [128, 8, 512], BF16) for i in range(2)]
                fco = [sbp("fco%d" % i, [128, 4, 512], BF16) for i in range(2)]
                for ti in range(NTILE):
                    t0 = ti * TW0
                    ft = fTt[ti % 2]
                    fk = 'fTt%d' % (ti % 2)
                    S.dma('sp', ft[:], fT[:, t0:t0 + 512].rearrange("(c p) t -> p c t", p=128), r=['fT%d' % ti], w=[fk])
                    for sub in range(4):
                        fo = fco[sub % 2]
                        fok = 'fco%d' % (sub % 2)
                        for g in range(4):
                            bank = g % 2
                            for cc in range(2):
                                S.op('pe', lambda e, g=g, cc=cc, sub=sub, ft=ft, bank=bank: e.matmul(ps[bank][:, :], lhsT=ft[:, 2 * g + cc, sub * 128:(sub + 1) * 128], rhs=cs256[:, cc, :],
                                                                                                 start=(cc == 0), stop=(cc == 1)),
                                     r=[fk, 'cs256'], w=[PSK[bank]])
                            S.op('act', lambda e, g=g, fo=fo, bank=bank: e.activation(out=fo[:, g, :], in_=ps[bank][:, :], func=AF.Copy), r=[PSK[bank]], w=[fok])
                        S.dma('sp', fcs[t0 + sub * 128:t0 + (sub + 1) * 128, :, :], fo[:], r=[fok], w=['fcs'])
            S.drain()
            if cfg.get('STOP', 99) == 3:
                break
            with contextlib.ExitStack() as ph:
                def sbp(name, shape, dt):
                    return ph.enter_context(nc.sbuf_tensor("%s_u%d" % (name, nc.next_id()), list(shape), dt))
                LMAX = LS
                cosl = sbp("cosl", [128, LMAX // 128, min(512, LMAX)], BF16)
                sinl = sbp("sinl", [128, LMAX // 128, min(512, LMAX)], BF16)
                Fc = [sbp("Fc%d" % i, [128, LMAX // 128, 128], BF16) for i in range(2)]
                Fs = [sbp("Fs%d" % i, [128, LMAX // 128, 128], BF16) for i in range(2)]
                dfo = [sbp("dfo%d" % i, [128, 512], BF16) for i in range(2)]
                it = 0
                for (s0, L, samp) in seqs:
                    dsrc = c_dfts if samp else c_dftp
                    ntb = L // 128
                    KW = min(512, L)
                    for kt in range(L // KW):
                        S.dma('sp', cosl[:, 0:ntb, 0:KW], dsrc[0][:, kt * KW:(kt + 1) * KW].rearrange("(b p) k -> p b k", p=128), w=['cosl'])
                        S.dma('sp', sinl[:, 0:ntb, 0:KW], dsrc[1][:, kt * KW:(kt + 1) * KW].rearrange("(b p) k -> p b k", p=128), w=['sinl'])
                        for mb in range(8):
                            g, half = mb // 2, mb % 2
                            fc_, fs_ = Fc[it % 2], Fs[it % 2]
                            fck, fsk = 'Fc%d' % (it % 2), 'Fs%d' % (it % 2)
                            do_ = dfo[it % 2]
                            dok = 'dfo%d' % (it % 2)
                            bank = 2 + it % 2
                            it += 1
                            S.dma('sp', fc_[:, 0:ntb, :], fcs[s0:s0 + L, g, half * 128:half * 128 + 128].rearrange("(b p) m -> p b m", p=128), r=['fcs'], w=[fck])
                            S.dma('sp', fs_[:, 0:ntb, :], fcs[s0:s0 + L, g, 256 + half * 128:256 + half * 128 + 128].rearrange("(b p) m -> p b m", p=128), r=['fcs'], w=[fsk])
                            for tb in range(ntb):
                                S.op('pe', lambda e, tb=tb, fc_=fc_, bank=bank, KW=KW: e.matmul(ps[bank][:, 0:KW], lhsT=fc_[:, tb, :], rhs=cosl[:, tb, 0:KW], start=(tb == 0), stop=False),
                                     r=[fck, 'cosl'], w=[PSK[bank]])
                                S.op('pe', lambda e, tb=tb, fs_=fs_, bank=bank, KW=KW, ntb=ntb: e.matmul(ps[bank][:, 0:KW], lhsT=fs_[:, tb, :], rhs=sinl[:, tb, 0:KW], start=False, stop=(tb == ntb - 1)),
                                     r=[fsk, 'sinl'], w=[PSK[bank]])
                            S.op('act', lambda e, do_=do_, bank=bank, KW=KW: e.activation(out=do_[:, 0:KW], in_=ps[bank][:, 0:KW], func=AF.Copy), r=[PSK[bank]], w=[dok])
                            S.dma('sp', dT[mb * 128:(mb + 1) * 128, s0 + kt * KW:s0 + (kt + 1) * KW], do_[:, 0:KW], r=[dok], w=['dT'])
            S.drain()
            if cfg.get('STOP', 99) == 4:
                break
            with contextlib.ExitStack() as ph:
                def sbp(name, shape, dt):
                    return ph.enter_context(nc.sbuf_tensor("%s_u%d" % (name, nc.next_id()), list(shape), dt))
                wf = sbp("wf", [128, 8, 1024], BF16)
                S.dma('pool', wf[:], w_fft[l].rearrange("(c p) m -> p c m", p=128), w=['wf'])
                dTt = [sbp("dTt%d" % i, [128, 8, 512], BF16) for i in range(2)]
                gft = [sbp("gft%d" % i, [128, 8, 512], BF16) for i in range(2)]
                mfo = [sbp("mfo%d" % i, [128, 512], BF16) for i in range(2)]
                it = 0
                for ti in range(NTILE):
                    t0 = ti * TW0
                    dt_, gt_ = dTt[ti % 2], gft[ti % 2]
                    dtk, gtk = 'dTt%d' % (ti % 2), 'gft%d' % (ti % 2)
                    S.dma('sp', dt_[:], dT[:, t0:t0 + 512].rearrange("(c p) t -> p c t", p=128), r=['dT'], w=[dtk])
                    S.dma('sp', gt_[:], gfT[:, t0:t0 + 512].rearrange("(c p) t -> p c t", p=128), r=['gfT%d' % ti], w=[gtk])
                    for ob in range(8):
                        bank = it % 2
                        m_ = mfo[it % 2]
                        mk_ = 'mfo%d' % (it % 2)
                        it += 1
                        for mb in range(8):
                            S.op('pe', lambda e, mb=mb, ob=ob, dt_=dt_, bank=bank: e.matmul(ps[bank][:, :], lhsT=wf[:, mb, ob * 128:(ob + 1) * 128], rhs=dt_[:, mb, :], start=(mb == 0), stop=(mb == 7)),
                                 r=['wf', dtk], w=[PSK[bank]])
                        S.op('dve', lambda e, m_=m_, bank=bank, gt_=gt_, ob=ob: e.tensor_tensor(out=m_[:], in0=ps[bank][:, :], in1=gt_[:, ob, :], op=ALU.mult), r=[PSK[bank], gtk], w=[mk_])
                        S.dma('sp', mixT[3072 + ob * 128:3072 + (ob + 1) * 128, t0:t0 + 512], m_[:], r=[mk_], w=['mixT_f'])
            S.drain()
            if cfg.get('STOP', 99) == 5:
                break

            with contextlib.ExitStack() as ph:
                def sbp(name, shape, dt):
                    return ph.enter_context(nc.sbuf_tensor("%s_u%d" % (name, nc.next_id()), list(shape), dt))
                lq = sbp("lq", [128, 3, 64], F32)
                dtq = sbp("dtq", [128, 64], F32)
                r_q = sbp("r_q", [128, 64], F32)
                ang_q = sbp("ang_q", [128, 64], F32)
                stq = sbp("stq", [128, 2, 2, 32], F32)
                dsk = sbp("dsk", [128, 8], F32)
                fin = sbp("fin", [128, NPS, 2, 2, 32], F32)
                jv = sbp("jv", [128, 512], F32)
                S.dma('sp', lq[:], lam_q[l].rearrange("a p c -> p a c"), w=['lq'])
                S.dma('sp', stq[:], st_q[l], w=['stq'])
                S.dma('sp', dsk[:], dskip_p[l], w=['dsk'])
                S.dma('sp', jv[:], c_jv[:, :], w=['jv'])
                S.op('act', lambda e: e.activation(out=dtq[:], in_=lq[:, 2, :], func=AF.Exp), r=['lq'], w=['dtq'])
                S.op('dve', lambda e: e.tensor_tensor(out=r_q[:], in0=lq[:, 0, :], in1=dtq[:], op=ALU.mult), r=['lq', 'dtq'], w=['r_q'])
                S.op('act', lambda e: e.activation(out=r_q[:], in_=r_q[:], func=AF.Exp), r=['r_q'], w=['r_q'])
                S.op('dve', lambda e: e.tensor_tensor(out=ang_q[:], in0=lq[:, 1, :], in1=dtq[:], op=ALU.mult), r=['lq', 'dtq'], w=['ang_q'])

                lr_ = sbp("lr_", [128, 512], F32)
                li_ = sbp("li_", [128, 512], F32)
                ls_ = sbp("ls_", [128, 512], F32)
                z1 = sbp("z1", [128, 512], F32)
                z2 = sbp("z2", [128, 512], F32)
                z3 = sbp("z3", [128, 512], F32)
                z4 = sbp("z4", [128, 512], F32)
                zi = sbp("zi", [128, 512], I32)
                fre = sbp("fre", [128, 512], F32)
                fim = sbp("fim", [128, 512], F32)
                bre = sbp("bre", [128, 512], F32)
                bim = sbp("bim", [128, 512], F32)
                lBr = sbp("lBr", [128, 2, 512], BF16)
                lBi = sbp("lBi", [128, 2, 512], BF16)
                cTr = sbp("cTr", [128, 2, 4, 128], BF16)
                cTi = sbp("cTi", [128, 2, 4, 128], BF16)
                TC = sbp("TC", [128, 8, 512], F32)
                TS = sbp("TS", [128, 8, 512], F32)
                targ = sbp("targ", [128, 512], F32)
                ttf = sbp("ttf", [128, 512], F32)
                tti = sbp("tti", [128, 512], I32)
                uS = sbp("uS", [128, NT], BF16)
                ysb = sbp("ysb", [128, NT], F32)
                ybf = sbp("ybf", [128, NT], BF16)
                car = sbp("car", [128, 4, 2], F32)
                cw = sbp("cw", [128, 8], F32)
                W = {n: sbp("W" + n, [128, 512], F32) for n in ('a', 'b', 'c', 'd', 'wr', 'wi', 'gr', 'gi', 'e', 'f', 'g', 'h')}
                WB = dict(a=(lr_, 'lr_'), b=(li_, 'li_'), c=(ls_, 'ls_'), d=(z1, 'z1'), wr=(z2, 'z2'), wi=(z3, 'z3'), gr=(z4, 'z4'),
                          gi=(fre, 'fre'), e=(fim, 'fim'), f=(bre, 'bre'), g=(bim, 'bim'), h=(ttf, 'ttf'))
                WA = {n: (t, 'W' + n) for n, t in W.items()}
                hrb = [sbp("hrb%d" % i, [128, 512], BF16) for i in range(2)]
                hib = [sbp("hib%d" % i, [128, 512], BF16) for i in range(2)]

                for uc in range(8):
                    S.dma('pool', cTr[:], cT_pad[l, 0, uc], w=['cTr'])
                    S.dma('pool', cTi[:], cT_pad[l, 1, uc], w=['cTi'])
                    S.dma('sp', uS[:], uT[uc * 128:(uc + 1) * 128, :], r=['uT'], w=['uS'])
                    for dz in range(2):
                        S.dma('sp', lr_[:], lam_rep[l, 0][:, dz, uc * 512:(uc + 1) * 512], w=['lr_'])
                        S.dma('sp', li_[:], lam_rep[l, 1][:, dz, uc * 512:(uc + 1) * 512], w=['li_'])
                        S.dma('sp', ls_[:], lam_rep[l, 2][:, dz, uc * 512:(uc + 1) * 512], w=['ls_'])
                        S.dma('sp', bre[:], bT_pad[l, 0, uc][:, dz, :], w=['bre'])
                        S.dma('sp', bim[:], bT_pad[l, 1, uc][:, dz, :], w=['bim'])
                        fl = lambda t: t[:]
                        S.op('act', lambda e: e.activation(out=fl(z1), in_=fl(ls_), func=AF.Exp), r=['ls_'], w=['z1'])
                        S.op('dve', lambda e: e.tensor_tensor(out=fl(z2), in0=fl(lr_), in1=fl(z1), op=ALU.mult), r=['lr_', 'z1'], w=['z2'])
                        S.op('act', lambda e: e.activation(out=fl(z2), in_=fl(z2), func=AF.Exp), r=['z2'], w=['z2'])
                        S.op('dve', lambda e: e.tensor_tensor(out=fl(z1), in0=fl(li_), in1=fl(z1), op=ALU.mult), r=['li_', 'z1'], w=['z1'])
                        sin_of(fl(z3), fl(z1), fl(z4), fl(zi), 0.0, ['z1'], 'z3', 'z4', 'zi')
                        S.op('dve', lambda e: e.tensor_tensor(out=fl(z3), in0=fl(z3), in1=fl(z2), op=ALU.mult), r=['z3', 'z2'], w=['z3'])
                        sin_of(fl(fre), fl(z1), fl(z4), fl(zi), PI / 2, ['z1'], 'fre', 'z4', 'zi')
                        S.op('dve', lambda e: e.tensor_tensor(out=fl(z2), in0=fl(fre), in1=fl(z2), op=ALU.mult), r=['fre', 'z2'], w=['z2'])
                        S.op('dve', lambda e: e.tensor_scalar(out=fl(z2), in0=fl(z2), scalar1=-1.0, scalar2=None, op0=ALU.add), r=['z2'], w=['z2'])
                        S.op('dve', lambda e: e.tensor_tensor(out=fl(z1), in0=fl(lr_), in1=fl(lr_), op=ALU.mult), r=['lr_', 'z1'], w=['z1'])
                        S.op('dve', lambda e: e.tensor_tensor(out=fl(z4), in0=fl(li_), in1=fl(li_), op=ALU.mult), r=['li_'], w=['z4'])
                        S.op('dve', lambda e: e.tensor_tensor(out=fl(z1), in0=fl(z1), in1=fl(z4), op=ALU.add), r=['z1', 'z4'], w=['z1'])
                        S.op('dve', lambda e: e.reciprocal(out=fl(z1), in_=fl(z1)), r=['z1'], w=['z1'])
                        S.op('dve', lambda e: e.tensor_tensor(out=fl(fre), in0=fl(z2), in1=fl(lr_), op=ALU.mult), r=['z2', 'lr_'], w=['fre'])
                        S.op('dve', lambda e: e.tensor_tensor(out=fl(z4), in0=fl(z3), in1=fl(li_), op=ALU.mult), r=['z3', 'li_'], w=['z4'])
                        S.op('dve', lambda e: e.tensor_tensor(out=fl(fre), in0=fl(fre), in1=fl(z4), op=ALU.add), r=['fre', 'z4'], w=['fre'])
                        S.op('dve', lambda e: e.tensor_tensor(out=fl(fre), in0=fl(fre), in1=fl(z1), op=ALU.mult), r=['fre', 'z1'], w=['fre'])
                        S.op('dve', lambda e: e.tensor_tensor(out=fl(fim), in0=fl(z3), in1=fl(lr_), op=ALU.mult), r=['z3', 'lr_'], w=['fim'])
                        S.op('dve', lambda e: e.tensor_tensor(out=fl(z4), in0=fl(z2), in1=fl(li_), op=ALU.mult), r=['z2', 'li_'], w=['z4'])
                        S.op('dve', lambda e: e.tensor_tensor(out=fl(fim), in0=fl(fim), in1=fl(z4), op=ALU.subtract), r=['fim', 'z4'], w=['fim'])
                        S.op('dve', lambda e: e.tensor_tensor(out=fl(fim), in0=fl(fim), in1=fl(z1), op=ALU.mult), r=['fim', 'z1'], w=['fim'])
                        S.op('dve', lambda e: e.tensor_tensor(out=fl(z1), in0=fl(fre), in1=fl(bre), op=ALU.mult), r=['fre', 'bre'], w=['z1'])
                        S.op('dve', lambda e: e.tensor_tensor(out=fl(z2), in0=fl(fim), in1=fl(bim), op=ALU.mult), r=['fim', 'bim'], w=['z2'])
                        S.op('dve', lambda e, dz=dz: e.tensor_tensor(out=lBr[:, dz, :], in0=fl(z1), in1=fl(z2), op=ALU.subtract), r=['z1', 'z2'], w=['lBr'])
                        S.op('dve', lambda e: e.tensor_tensor(out=fl(z1), in0=fl(fre), in1=fl(bim), op=ALU.mult), r=['fre', 'bim'], w=['z1'])
                        S.op('dve', lambda e: e.tensor_tensor(out=fl(z2), in0=fl(fim), in1=fl(bre), op=ALU.mult), r=['fim', 'bre'], w=['z2'])
                        S.op('dve', lambda e, dz=dz: e.tensor_tensor(out=lBi[:, dz, :], in0=fl(z1), in1=fl(z2), op=ALU.add), r=['z1', 'z2'], w=['lBi'])
                    for d in range(2):
                        for k in range(4):
                            dk = d * 4 + k
                            col = d * 32 + uc * 4 + k
                            S.op('dve', lambda e, col=col: e.tensor_scalar(out=targ[:], in0=jv[:], scalar1=ang_q[:, col:col + 1], scalar2=None, op0=ALU.mult),
                                 r=['jv', 'ang_q'], w=['targ'])
                            sin_of(TS[:, dk, :], targ[:], ttf[:], tti[:], 0.0, ['targ'], 'TS%d' % dk, 'ttf', 'tti')
                            sin_of(TC[:, dk, :], targ[:], ttf[:], tti[:], PI / 2, ['targ'], 'TC%d' % dk, 'ttf', 'tti')
                    hi_ = 0
                    yb_ = 0
                    for si_, (s0, L, samp) in enumerate(seqs):
                        TW = min(512, L)
                        ntl = L // TW
                        for d in range(2):
                            order = range(ntl) if d == 0 else range(ntl - 1, -1, -1)
                            for k in range(4):
                                st = uc * 4 + k
                                if samp:
                                    S.op('pool', lambda e, k=k, d=d, st=st: e.tensor_copy(out=car[:, k, :], in_=stq[:, d, :, st]), r=['stq'], w=['car%d' % k])
                                else:
                                    S.op('pool', lambda e, k=k: e.memset(car[:, k, :], 0.0), w=['car%d' % k])
                            its = [(tl, k) for tl in order for k in range(4)]
                            ctxs = []
                            for (tl, k) in its:
                                ctxs.append(dict(tl=tl, k=k, c0=s0 + tl * TW, par=hi_ % 2, ybank=6 + (yb_ % 2)))
                                hi_ += 1
                                if k == 3:
                                    yb_ += 1

                            def stage_a(cx):
                                k, c0, par = cx['k'], cx['c0'], cx['par']
                                dk = d * 4 + k
                                col = d * 32 + uc * 4 + k
                                br, bi = (0, 1) if par == 0 else (2, 3)
                                S.op('pe', lambda e, d=d, k=k, c0=c0, TW=TW, br=br: e.matmul(ps[br][:, 0:TW], lhsT=lBr[:, d, k * 128:(k + 1) * 128], rhs=uS[:, c0:c0 + TW], start=True, stop=True),
                                     r=['lBr', 'uS'], w=[PSK[br]])
                                S.op('pe', lambda e, d=d, k=k, c0=c0, TW=TW, bi=bi: e.matmul(ps[bi][:, 0:TW], lhsT=lBi[:, d, k * 128:(k + 1) * 128], rhs=uS[:, c0:c0 + TW], start=True, stop=True),
                                     r=['lBi', 'uS'], w=[PSK[bi]])
                                if d == 0:
                                    pr, pi2 = ps[br][:, 0:TW], ps[bi][:, 0:TW]
                                else:
                                    pr, pi2 = ps[br][:, 0:TW][:, ::-1], ps[bi][:, 0:TW][:, ::-1]
                                tc_, ts_ = TC[:, dk, 0:TW], TS[:, dk, 0:TW]
                                tck, tsk = 'TC%d' % dk, 'TS%d' % dk
                                WS = WA if par == 0 else WB
                                Wv = {n: t[:, 0:TW] for n, (t, _) in WS.items()}
                                Wk = {n: kk for n, (_, kk) in WS.items()}
                                Wt = {n: t for n, (t, _) in WS.items()}
                                S.op('dve', lambda e, pr=pr, tc_=tc_, Wv=Wv: e.tensor_tensor(out=Wv['a'], in0=pr, in1=tc_, op=ALU.mult), r=[PSK[br], tck], w=[Wk['a']])
                                S.op('dve', lambda e, pi2=pi2, ts_=ts_, Wv=Wv: e.tensor_tensor(out=Wv['b'], in0=pi2, in1=ts_, op=ALU.mult), r=[PSK[bi], tsk], w=[Wk['b']])
                                S.op('pool', lambda e, Wv=Wv: e.tensor_tensor(out=Wv['wr'], in0=Wv['a'], in1=Wv['b'], op=ALU.add), r=[Wk['a'], Wk['b']], w=[Wk['wr']])
                                S.op('dve', lambda e, pi2=pi2, tc_=tc_, Wv=Wv: e.tensor_tensor(out=Wv['c'], in0=pi2, in1=tc_, op=ALU.mult), r=[PSK[bi], tck], w=[Wk['c']])
                                S.op('dve', lambda e, pr=pr, ts_=ts_, Wv=Wv: e.tensor_tensor(out=Wv['d'], in0=pr, in1=ts_, op=ALU.mult), r=[PSK[br], tsk], w=[Wk['d']])
                                S.op('pool', lambda e, Wv=Wv: e.tensor_tensor(out=Wv['wi'], in0=Wv['c'], in1=Wv['d'], op=ALU.subtract), r=[Wk['c'], Wk['d']], w=[Wk['wi']])
                                rb = r_q[:, col:col + 1].to_broadcast([128, TW])
                                S.op('dve', lambda e, Wv=Wv, rb=rb, k=k: e.tensor_tensor_scan(out=Wv['gr'], data0=rb, data1=Wv['wr'], initial=car[:, k, 0:1], op0=ALU.mult, op1=ALU.add),
                                     r=[Wk['wr'], 'r_q', 'car%d' % k], w=[Wk['gr']])
                                S.op('dve', lambda e, Wv=Wv, rb=rb, k=k: e.tensor_tensor_scan(out=Wv['gi'], data0=rb, data1=Wv['wi'], initial=car[:, k, 1:2], op0=ALU.mult, op1=ALU.add),
                                     r=[Wk['wi'], 'r_q', 'car%d' % k], w=[Wk['gi']])

                            def stage_c(cx):
                                k, par = cx['k'], cx['par']
                                dk = d * 4 + k
                                tck, tsk = 'TC%d' % dk, 'TS%d' % dk
                                WS = WA if par == 0 else WB
                                Wk = {n: kk for n, (_, kk) in WS.items()}
                                Wt = {n: t for n, (t, _) in WS.items()}
                                gl_r, gl_i = Wt['gr'][:, TW - 1:TW], Wt['gi'][:, TW - 1:TW]
                                cl, sl_ = TC[:, dk, TW - 1:TW], TS[:, dk, TW - 1:TW]
                                S.op('pool', lambda e, gl_r=gl_r, cl=cl: e.tensor_tensor(out=cw[:, 0:1], in0=gl_r, in1=cl, op=ALU.mult), r=[Wk['gr'], tck], w=['cw0'])
                                S.op('pool', lambda e, gl_i=gl_i, sl_=sl_: e.tensor_tensor(out=cw[:, 1:2], in0=gl_i, in1=sl_, op=ALU.mult), r=[Wk['gi'], tsk], w=['cw1'])
                                S.op('pool', lambda e, gl_r=gl_r, sl_=sl_: e.tensor_tensor(out=cw[:, 2:3], in0=gl_r, in1=sl_, op=ALU.mult), r=[Wk['gr'], tsk], w=['cw2'])
                                S.op('pool', lambda e, gl_i=gl_i, cl=cl: e.tensor_tensor(out=cw[:, 3:4], in0=gl_i, in1=cl, op=ALU.mult), r=[Wk['gi'], tck], w=['cw3'])
                                S.op('pool', lambda e, k=k: e.tensor_tensor(out=car[:, k, 0:1], in0=cw[:, 0:1], in1=cw[:, 1:2], op=ALU.subtract), r=['cw0', 'cw1'], w=['car%d' % k])
                                S.op('pool', lambda e, k=k: e.tensor_tensor(out=car[:, k, 1:2], in0=cw[:, 2:3], in1=cw[:, 3:4], op=ALU.add), r=['cw2', 'cw3', 'car%d' % k], w=['car%d' % k])

                            def stage_b(cx):
                                k, c0, par, ybank = cx['k'], cx['c0'], cx['par'], cx['ybank']
                                dk = d * 4 + k
                                tc_, ts_ = TC[:, dk, 0:TW], TS[:, dk, 0:TW]
                                tck, tsk = 'TC%d' % dk, 'TS%d' % dk
                                WS = WA if par == 0 else WB
                                Wv = {n: t[:, 0:TW] for n, (t, _) in WS.items()}
                                Wk = {n: kk for n, (_, kk) in WS.items()}
                                hr_, hn_ = hrb[par], hib[par]
                                hrk, hnk = 'hrb%d' % par, 'hib%d' % par
                                if d == 0:
                                    hro, hno = hr_[:, 0:TW], hn_[:, 0:TW]
                                else:
                                    hro, hno = hr_[:, 0:TW][:, ::-1], hn_[:, 0:TW][:, ::-1]
                                S.op('pool', lambda e, Wv=Wv, tc_=tc_: e.tensor_tensor(out=Wv['e'], in0=Wv['gr'], in1=tc_, op=ALU.mult), r=[Wk['gr'], tck], w=[Wk['e']])
                                S.op('pool', lambda e, Wv=Wv, ts_=ts_: e.tensor_tensor(out=Wv['f'], in0=Wv['gi'], in1=ts_, op=ALU.mult), r=[Wk['gi'], tsk], w=[Wk['f']])
                                S.op('pool', lambda e, Wv=Wv, hro=hro: e.tensor_tensor(out=hro, in0=Wv['e'], in1=Wv['f'], op=ALU.subtract), r=[Wk['e'], Wk['f']], w=[hrk])
                                S.op('dve', lambda e, Wv=Wv, ts_=ts_: e.tensor_tensor(out=Wv['g'], in0=Wv['gr'], in1=ts_, op=ALU.mult), r=[Wk['gr'], tsk], w=[Wk['g']])
                                S.op('dve', lambda e, Wv=Wv, tc_=tc_: e.tensor_tensor(out=Wv['h'], in0=Wv['gi'], in1=tc_, op=ALU.mult), r=[Wk['gi'], tck], w=[Wk['h']])
                                S.op('dve', lambda e, Wv=Wv, hno=hno: e.scalar_tensor_tensor(out=hno, in0=Wv['g'], scalar=-1.0, in1=Wv['h'], op0=ALU.mult, op1=ALU.subtract),
                                     r=[Wk['g'], Wk['h']], w=[hnk])
                                S.op('pe', lambda e, d=d, k=k, hr_=hr_, TW=TW, ybank=ybank: e.matmul(ps[ybank][:, 0:TW], lhsT=cTr[:, d, k, :], rhs=hr_[:, 0:TW], start=(k == 0), stop=False),
                                     r=['cTr', hrk], w=[PSK[ybank]])
                                S.op('pe', lambda e, d=d, k=k, hn_=hn_, TW=TW, ybank=ybank: e.matmul(ps[ybank][:, 0:TW], lhsT=cTi[:, d, k, :], rhs=hn_[:, 0:TW], start=False, stop=(k == 3)),
                                     r=['cTi', hnk], w=[PSK[ybank]])

                            def stage_e(cx):
                                c0, ybank = cx['c0'], cx['ybank']
                                if d == 0:
                                    S.op('dve', lambda e, c0=c0, TW=TW, ybank=ybank, uc=uc: e.scalar_tensor_tensor(out=ysb[:, c0:c0 + TW], in0=uS[:, c0:c0 + TW], scalar=dsk[:, uc:uc + 1],
                                                                                                               in1=ps[ybank][:, 0:TW], op0=ALU.mult, op1=ALU.add),
                                         r=['uS', 'dsk', PSK[ybank]], w=['ysb'])
                                else:
                                    S.op('dve', lambda e, c0=c0, TW=TW, ybank=ybank: e.tensor_tensor(out=ybf[:, c0:c0 + TW], in0=ysb[:, c0:c0 + TW], in1=ps[ybank][:, 0:TW], op=ALU.add),
                                         r=['ysb', PSK[ybank]], w=['ybf'])

                            stage_a(ctxs[0])
                            stage_c(ctxs[0])
                            for i_ in range(len(ctxs)):
                                if i_ + 1 < len(ctxs):
                                    stage_a(ctxs[i_ + 1])
                                stage_b(ctxs[i_])
                                if i_ + 1 < len(ctxs):
                                    stage_c(ctxs[i_ + 1])
                                if ctxs[i_]['k'] == 3:
                                    stage_e(ctxs[i_])
                            if not samp:
                                for k in range(4):
                                    st = uc * 4 + k
                                    S.op('pool', lambda e, k=k, d=d, st=st, si_=si_: e.tensor_copy(out=fin[:, si_, d, :, st], in_=car[:, k, :]), r=['car%d' % k], w=['fin'])
                    S.dma('sp', yT[uc * 128:(uc + 1) * 128, :], ybf[:], r=['ybf'], w=['yT'])
                S.dma('sp', fin_out[l], fin[:].rearrange("p a b c d -> p (a b c d)"), r=['fin'], w=['fin_out'])
            S.drain()
            if cfg.get('STOP', 99) == 6:
                break

            with contextlib.ExitStack() as ph:
                def sbp(name, shape, dt):
                    return ph.enter_context(nc.sbuf_tensor("%s_u%d" % (name, nc.next_id()), list(shape), dt))
                wg = sbp("wg", [128, 8, 2048], BF16)
                S.dma('pool', wg[:], w_glu[l].rearrange("(c p) m -> p c m", p=128), w=['wg'])
                yTt = [sbp("yTt%d" % i, [128, 8, 512], BF16) for i in range(2)]
                gst = [sbp("gst%d" % i, [128, 8, 512], BF16) for i in range(2)]
                sg = sbp("sg", [128, 512], F32)
                tg = sbp("tg", [128, 512], F32)
                mgo = [sbp("mgo%d" % i, [128, 512], BF16) for i in range(2)]
                it = 0
                for ti in range(NTILE):
                    t0 = ti * TW0
                    yt_, gt_ = yTt[ti % 2], gst[ti % 2]
                    ytk, gtk = 'yTt%d' % (ti % 2), 'gst%d' % (ti % 2)
                    S.dma('sp', yt_[:], yT[:, t0:t0 + 512].rearrange("(c p) t -> p c t", p=128), r=['yT'], w=[ytk])
                    S.dma('sp', gt_[:], gsT[:, t0:t0 + 512].rearrange("(c p) t -> p c t", p=128), r=['gsT%d' % ti], w=[gtk])
                    for ob in range(8):
                        m_ = mgo[it % 2]
                        mk_ = 'mgo%d' % (it % 2)
                        ba, bg = (0, 1) if it % 2 == 0 else (2, 3)
                        it += 1
                        for uc in range(8):
                            S.op('pe', lambda e, uc=uc, ob=ob, yt_=yt_, ba=ba: e.matmul(ps[ba][:, :], lhsT=wg[:, uc, ob * 128:(ob + 1) * 128], rhs=yt_[:, uc, :], start=(uc == 0), stop=(uc == 7)),
                                 r=['wg', ytk], w=[PSK[ba]])
                        for uc in range(8):
                            S.op('pe', lambda e, uc=uc, ob=ob, yt_=yt_, bg=bg: e.matmul(ps[bg][:, :], lhsT=wg[:, uc, 1024 + ob * 128:1024 + (ob + 1) * 128], rhs=yt_[:, uc, :], start=(uc == 0), stop=(uc == 7)),
                                 r=['wg', ytk], w=[PSK[bg]])
                        S.op('act', lambda e, bg=bg: e.activation(out=sg[:], in_=ps[bg][:, :], func=AF.Sigmoid), r=[PSK[bg]], w=['sg'])
                        S.op('dve', lambda e, ba=ba: e.tensor_tensor(out=tg[:], in0=ps[ba][:, :], in1=sg[:], op=ALU.mult), r=[PSK[ba], 'sg'], w=['tg'])
                        S.op('pool', lambda e, m_=m_, gt_=gt_, ob=ob: e.tensor_tensor(out=m_[:], in0=tg[:], in1=gt_[:, ob, :], op=ALU.mult), r=['tg', gtk], w=[mk_])
                        S.dma('sp', mixT[2048 + ob * 128:2048 + (ob + 1) * 128, t0:t0 + 512], m_[:], r=[mk_], w=['mixT_s'])
            S.drain()
            if cfg.get('STOP', 99) == 7:
                break

            with contextlib.ExitStack() as ph:
                def sbp(name, shape, dt):
                    return ph.enter_context(nc.sbuf_tensor("%s_u%d" % (name, nc.next_id()), list(shape), dt))
                slab[0] = sbp("slabA", [128, 32, 512], BF16)
                slab[1] = sbp("slabB", [128, 32, 512], BF16)
                mt = sbp("mt", [128, 32, 512], BF16)
                gbc = sbp("gbc", [128, 2, D], F32)
                S.dma('sp', gbc[:], gate_dr.rearrange("v p f -> p v f"), r=['gate_dr'], w=['gbc'])
                xo = [sbp("xo%d" % i, [128, 512], F32) for i in range(4)]
                xn_ = [sbp("xn%d" % i, [128, 512], F32) for i in range(4)]
                it = 0
                ss3 = SlabStream([w_out[l][:, so * 512:(so + 1) * 512] for _ in range(NTILE) for so in range(8)])
                for ti in range(NTILE):
                    t0 = ti * TW0
                    v = 0 if ti < NPT else 1
                    S.dma('sp', mt[:], mixT[:, t0:t0 + 512].rearrange("(c p) t -> p c t", p=128), r=['mixT_a', 'mixT_f', 'mixT_s'], w=['mt'])
                    for so in range(8):
                        sl, sk = ss3.get(ti * 8 + so)
                        for sub in range(4):
                            bank = it % 4
                            x_, xk = xo[it % 4], 'xo%d' % (it % 4)
                            n_, nk_ = xn_[it % 4], 'xn%d' % (it % 4)
                            it += 1
                            rows = slice(t0 + sub * 128, t0 + (sub + 1) * 128)
                            S.dma('sp', x_[:], xsrc[rows, so * 512:(so + 1) * 512], r=['xres_w'], w=[xk])
                            for c in range(32):
                                S.op('pe', lambda e, c=c, sub=sub, sl=sl, bank=bank: e.matmul(ps[bank][:, :], lhsT=mt[:, c, sub * 128:(sub + 1) * 128], rhs=sl[:, c, :], start=(c == 0), stop=(c == 31)),
                                     r=[sk, 'mt'], w=[PSK[bank]])
                            S.op('dve', lambda e, n_=n_, bank=bank, v=v, so=so: e.tensor_tensor(out=n_[:], in0=ps[bank][:, :], in1=gbc[:, v, so * 512:(so + 1) * 512], op=ALU.mult),
                                 r=[PSK[bank], 'gbc'], w=[nk_])
                            S.op('pool', lambda e, n_=n_, x_=x_: e.tensor_tensor(out=n_[:], in0=n_[:], in1=x_[:], op=ALU.add), r=[nk_, xk], w=[nk_])
                            S.dma('sp', xres[rows, so * 512:(so + 1) * 512], n_[:], r=[nk_], w=['xres_w'])
            S.drain()
            if cfg.get('STOP', 99) == 8:
                break

        with contextlib.ExitStack() as ph:
          if cfg.get('STOP', 99) == 99:
            fg = ph.enter_context(nc.sbuf_tensor("fg", [128, D], F32))
            xf = [ph.enter_context(nc.sbuf_tensor("xf%d" % i, [128, D], F32)) for i in range(2)]
            yo = [ph.enter_context(nc.sbuf_tensor("yo%d" % i, [128, D], F32)) for i in range(2)]
            junk2 = ph.enter_context(nc.sbuf_tensor("junk2", [128, D], BF16))
            ss2 = ph.enter_context(nc.sbuf_tensor("ss2", [128, 2], F32))
            S.dma('sp', fg[:], fng_rep[:, :], w=['fg'])
            for b in range(NT // 128):
                x_, xk = xf[b % 2], 'xf%d' % (b % 2)
                y_, yk = yo[b % 2], 'yo%d' % (b % 2)
                sk_ = 'ss2_%d' % (b % 2)
                S.dma('sp', x_[:], xres[b * 128:(b + 1) * 128, :], r=['xres_w'], w=[xk])
                S.op('act', lambda e, x_=x_, b=b: e.activation(out=junk2[:], in_=x_[:], func=AF.Square, accum_out=ss2[:, b % 2:b % 2 + 1]), r=[xk], w=['junk2', sk_])
                rstd_from(ss2[:, b % 2:b % 2 + 1], ss2[:, b % 2:b % 2 + 1], 1.0 / D, [sk_], sk_)
                S.op('dve', lambda e, x_=x_, y_=y_, b=b: e.scalar_tensor_tensor(out=y_[:], in0=x_[:], scalar=ss2[:, b % 2:b % 2 + 1], in1=fg[:], op0=ALU.mult, op1=ALU.mult),
                     r=[xk, sk_, 'fg'], w=[yk])
                S.dma('sp', y_out[b * 128:(b + 1) * 128, :], y_[:], r=[yk], w=['y_out'])
        S.finish()
    return nc


def _consts(cfg):
    LP, LS = cfg['LP'], cfg['LS']
    c = {}
    c['c_ident'] = np.eye(128, dtype=np.float32)
    R = np.zeros((128, 128), np.float32)
    for base in (0, 64):
        for i in range(32):
            R[base + i, base + 32 + i] = -1.0
            R[base + 32 + i, base + i] = 1.0
    c['c_RT'] = np.ascontiguousarray(R.T)
    t = np.arange(LS)
    inv = 10000.0 ** (-np.arange(32, dtype=np.float32) / 32).astype(np.float32)
    row = (t // 64).astype(np.float32)[:, None] * inv[None, :]
    col = (t % 64).astype(np.float32)[:, None] * inv[None, :]
    ang = np.concatenate([row, row, col, col], axis=1).T.astype(np.float32)
    c['c_rope'] = np.stack([np.cos(ang), np.sin(ang)]).astype(np.float32)
    c['c_jv'] = np.tile(np.arange(1, 513, dtype=np.float32)[None, :], (128, 1))
    cc = np.arange(256)
    a = 2 * np.pi * ((cc[:, None] * cc[None, :]) % 256) / 256.0
    cs = np.concatenate([np.cos(a), -np.sin(a)], axis=1) / 16.0
    c['c_cs256'] = np.ascontiguousarray(cs.reshape(2, 128, 512).transpose(1, 0, 2)).astype(np.float32)

    def dft(L):
        tt = np.arange(L, dtype=np.int64)
        a = 2 * np.pi * ((tt[:, None] * tt[None, :]) % L) / float(L)
        return (np.stack([np.cos(a), np.sin(a)]) / math.sqrt(L)).astype(ml_dtypes.bfloat16)
    c['c_dftp'] = dft(LP)
    c['c_dfts'] = dft(LS)
    return c


def _prep_core(cfg, core, x_prompt, x_sample, cache_k, cache_v, sf_re, sf_im, sb_re, sb_im, c, shared):
    NPS, LP, LS, PAST, DEPTH = cfg['NPS'], cfg['LP'], cfg['LS'], cfg['PAST'], cfg['DEPTH']
    ncore_per_b = cfg.get('CPB', 4)
    b = core // ncore_per_b
    m = dict(shared)
    xp = x_prompt[core * NPS:(core + 1) * NPS].reshape(NPS * LP, D)
    m['x_in'] = np.concatenate([xp, x_sample[b]], axis=0)
    cvs = np.stack([shared['_c_ctx'], c[b]], axis=-1)
    m['cvec'] = np.ascontiguousarray(cvs.reshape(32, 128, 2).transpose(1, 0, 2))
    m['cache_kT'] = np.ascontiguousarray(cache_k[b].transpose(0, 2, 3, 1))
    m['cache_v'] = np.ascontiguousarray(cache_v[b].reshape(DEPTH, PAST, NKV * HD))
    st = np.stack([np.stack([sf_re[b], sf_im[b]], 1), np.stack([sb_re[b], sb_im[b]], 1)], 1)
    st = st.reshape(DEPTH, 2, 2, 32, 128)
    m['st_q'] = np.ascontiguousarray(st.transpose(0, 4, 1, 2, 3))
    del m['_c_ctx']
    return m


def _prep_shared(cfg, c_ctx, norm_g, w_mod, b_mod, w_in, q_norm, k_norm, lam_re, lam_im, log_step,
                 b_re, b_im, c_re, c_im, d_skip, w_glu, w_fft, w_out, final_norm_g):
    DEPTH = cfg['DEPTH']
    f = np.float32
    m = dict(_consts(cfg))
    m['_c_ctx'] = c_ctx
    m['normg_p'] = np.ascontiguousarray(norm_g.reshape(DEPTH, 32, 128).transpose(0, 2, 1))
    m['w_mod'] = w_mod
    m['bmod_ps'] = np.ascontiguousarray(b_mod[:, :2 * D].reshape(DEPTH, 64, 128).transpose(0, 2, 1))
    m['bmodg_rep'] = np.ascontiguousarray(np.broadcast_to(b_mod[:, None, 2 * D:], (DEPTH, 128, D)))
    m['w_in'] = w_in
    m['qk_g'] = np.ascontiguousarray(np.stack([q_norm, k_norm], axis=-1))
    ls_full = np.broadcast_to(log_step[..., None], lam_re.shape)
    lam3 = np.stack([lam_re, lam_im, ls_full], axis=1).reshape(DEPTH, 3, 2, 32, 128)
    m['lam_q'] = np.ascontiguousarray(lam3.transpose(0, 1, 4, 2, 3).reshape(DEPTH, 3, 128, 64))
    lam_flat = np.stack([lam_re, lam_im, ls_full], axis=1).reshape(DEPTH, 3, 1, 2, 4096)
    m['lam_rep'] = np.ascontiguousarray(np.broadcast_to(lam_flat, (DEPTH, 3, 128, 2, 4096)))
    bT = np.zeros((DEPTH, 2, 8, 128, 2, 512), f)
    cT = np.zeros((DEPTH, 2, 8, 128, 2, 4, 128), f)
    for ri, (bb, ccm) in enumerate(((b_re, c_re), (b_im, c_im))):
        for uc in range(8):
            for gl in range(8):
                g = uc * 8 + gl
                bT[:, ri, uc, gl * 16:(gl + 1) * 16, :, gl * 64:(gl + 1) * 64] = bb[:, :, g].transpose(0, 3, 1, 2)
                k, qo = gl // 2, (gl % 2) * 64
                cT[:, ri, uc, qo:qo + 64, :, k, gl * 16:(gl + 1) * 16] = ccm[:, :, g].transpose(0, 3, 1, 2)
    m['bT_pad'] = bT
    m['cT_pad'] = cT
    m['dskip_p'] = np.ascontiguousarray(d_skip.reshape(DEPTH, 8, 128).transpose(0, 2, 1))
    m['w_glu'] = w_glu
    m['w_fft'] = w_fft
    m['w_out'] = w_out
    m['fng_rep'] = np.ascontiguousarray(np.broadcast_to(final_norm_g[None, :], (128, D)))
    return m


_NC_CACHE = {}


def run(cfg, n_cores, inputs):
    g = {k: np.asarray(v) for k, v in inputs.items()}
    key = tuple(sorted(cfg.items()))
    if key not in _NC_CACHE:
        _NC_CACHE[key] = build(cfg)
    nc = _NC_CACHE[key]
    shared = _prep_shared(cfg, g['c_ctx'], g['norm_g'], g['w_mod'], g['b_mod'], g['w_in'], g['q_norm'], g['k_norm'],
                          g['lam_re'], g['lam_im'], g['log_step'], g['b_re'], g['b_im'], g['c_re'], g['c_im'],
                          g['d_skip'], g['w_glu'], g['w_fft'], g['w_out'], g['final_norm_g'])
    in_maps = [_prep_core(cfg, core, g['x_prompt'], g['x_sample'], g['cache_k'], g['cache_v'],
                          g['state_fwd_re'], g['state_fwd_im'], g['state_bwd_re'], g['state_bwd_im'], g['c'], shared)
               for core in range(n_cores)]
    res = run_bass_kernel_spmd(nc, in_maps, core_ids=list(range(n_cores)))
    R = res.results
    DEPTH, NPS, LP, LS = cfg['DEPTH'], cfg['NPS'], cfg['LP'], cfg['LS']
    NTP = NPS * LP
    cpb = cfg.get('CPB', 4)
    y_prompt = np.concatenate([R[c_]['y_out'][:NTP].reshape(NPS, LP, D) for c_ in range(n_cores)], axis=0)
    y_sample = np.stack([R[c_]['y_out'][NTP:] for c_ in range(0, n_cores, cpb)], axis=0)
    new_k = np.concatenate([R[c_]['newk_out'].reshape(NPS, DEPTH, LP, NKV, HD) for c_ in range(n_cores)], axis=0)
    new_v = np.concatenate([R[c_]['newv_out'].reshape(NPS, DEPTH, LP, NKV, HD) for c_ in range(n_cores)], axis=0)
    fins = []
    for c_ in range(n_cores):
        fo = R[c_]['fin_out'].reshape(DEPTH, 128, NPS, 2, 2, 32)
        fins.append(fo.transpose(2, 0, 3, 4, 5, 1).reshape(NPS, DEPTH, 2, 2, 64, 64))
    fo = np.concatenate(fins, axis=0)
    outs = (y_prompt, y_sample, new_k, new_v, fo[:, :, 0, 0], fo[:, :, 0, 1], fo[:, :, 1, 0], fo[:, :, 1, 1])
    return tuple(np.ascontiguousarray(o, dtype=np.float32) for o in outs)


def kernel(**inputs):
    return run(CFG_FULL, 8, inputs)
```
